# Optimizing a Trainium2 kernel written in Bass

```python
import math
import jax, jax.numpy as jnp
from jax import lax
import numpy as np

D_MODEL = 1024
BATCH = 2
SEQ = 8192
DEPTH = 2

CTX_LEN = 256
GRID_W = 64
N_DIR = 2
EPS = 1e-6
NEG_INF = -1e30

ATT_HEADS = 8
ATT_KV_HEADS = 2
ATT_HEAD_DIM = 64
ATT_WIDTH = ATT_HEADS * ATT_HEAD_DIM
ATT_KV_WIDTH = ATT_KV_HEADS * ATT_HEAD_DIM
WINDOW = 128
ATT_BLOCK = 128
ROPE_BASE = 10000.0

GDN_HEADS = 4
GDN_HEAD_DIM = 128
GDN_WIDTH = GDN_HEADS * GDN_HEAD_DIM
GDN_CHUNK = 64
GDN_CONV = 4

LRU_BLOCKS = 8
LRU_BLOCK_DIM = 64
LRU_WIDTH = LRU_BLOCKS * LRU_BLOCK_DIM
LRU_CONV = 4
LRU_C = 8.0

MIX_WIDTH = ATT_WIDTH + GDN_WIDTH + LRU_WIDTH

IN_SPLITS = (
    ("att_q", ATT_WIDTH), ("att_k", ATT_KV_WIDTH), ("att_v", ATT_KV_WIDTH), ("att_z", ATT_WIDTH),
    ("gdn_q", GDN_WIDTH), ("gdn_k", GDN_WIDTH), ("gdn_v", GDN_WIDTH),
    ("gdn_b", N_DIR * GDN_HEADS), ("gdn_a", N_DIR * GDN_HEADS), ("gdn_z", GDN_WIDTH),
    ("lru_x", LRU_WIDTH), ("lru_z", LRU_WIDTH),
)
IN_WIDTH = sum(w for _, w in IN_SPLITS)

kernel_name = "hybrid_parallel_group_dit_block"

F32 = jnp.float32


def split_cols(p):
    out = {}
    off = 0
    for name, w in IN_SPLITS:
        out[name] = p[..., off:off + w]
        off += w
    return out


def rms_norm(x, g):
    xf = x.astype(F32)
    y = xf * lax.rsqrt(jnp.mean(xf * xf, axis=-1, keepdims=True) + EPS)
    return (y * g.astype(F32)).astype(x.dtype)


def l2norm(t):
    return t * lax.rsqrt(jnp.sum(t * t, axis=-1, keepdims=True) + EPS)


def flip_if(t, d):
    return jnp.flip(t, axis=1) if d == 1 else t


def depthwise_conv(x, w, b=None):
    K, C = w.shape
    left = K // 2
    y = lax.conv_general_dilated(x, w[:, None, :].astype(x.dtype), (1,), [(left, K - 1 - left)],
                                 dimension_numbers=("NWC", "WIO", "NWC"), feature_group_count=C)
    if b is not None:
        y = y + b.astype(y.dtype)
    return y


def axial_rope(rows):
    t = jnp.arange(rows * GRID_W)
    row = (t // GRID_W).astype(F32)
    col = (t % GRID_W).astype(F32)
    n_freq = ATT_HEAD_DIM // 4
    inv = ROPE_BASE ** (-jnp.arange(n_freq, dtype=F32) / n_freq)
    ang = jnp.stack([row[:, None] * inv, col[:, None] * inv], axis=1)
    return jnp.cos(ang), jnp.sin(ang)


def apply_rope(x, cos, sin):
    B, T, H, dh = x.shape
    xr = x.astype(F32).reshape(B, T, H, 2, 2, dh // 4)
    a, b = xr[..., 0, :], xr[..., 1, :]
    c = cos[None, :, None]
    s = sin[None, :, None]
    out = jnp.stack([a * c - b * s, b * c + a * s], axis=-2)
    return out.reshape(B, T, H, dh).astype(x.dtype)


def band_attention(q, k, v, kc, vc, sink):
    B, T, Hq, dh = q.shape
    G = Hq // ATT_KV_HEADS
    nb = T // ATT_BLOCK
    qb = q.reshape(B, nb, ATT_BLOCK, ATT_KV_HEADS, G, dh)

    def band(t):
        tp = jnp.pad(t, ((0, 0), (ATT_BLOCK, ATT_BLOCK), (0, 0), (0, 0)))
        tp = tp.reshape(B, nb + 2, ATT_BLOCK, ATT_KV_HEADS, dh)
        return jnp.concatenate([tp[:, :-2], tp[:, 1:-1], tp[:, 2:]], axis=2)

    kb, vb = band(k), band(v)
    scale = dh ** -0.5
    s_loc = jnp.einsum("bnqhgd,bnkhd->bnhgqk", qb, kb, preferred_element_type=F32) * scale
    s_ctx = jnp.einsum("bnqhgd,bkhd->bnhgqk", qb, kc, preferred_element_type=F32) * scale
    blk = jnp.arange(nb)[:, None] * ATT_BLOCK
    qpos = blk + jnp.arange(ATT_BLOCK)
    kpos = blk - ATT_BLOCK + jnp.arange(3 * ATT_BLOCK)
    ok = ((jnp.abs(qpos[:, :, None] - kpos[:, None, :]) <= WINDOW)
          & (kpos >= 0)[:, None, :] & (kpos < T)[:, None, :])
    s_loc = jnp.where(ok[None, :, None, None], s_loc, NEG_INF)
    sk = sink.astype(F32).reshape(1, 1, ATT_KV_HEADS, G, 1)
    m = jnp.maximum(jnp.maximum(s_loc.max(-1), s_ctx.max(-1)), sk)
    e_loc = jnp.exp(s_loc - m[..., None])
    e_ctx = jnp.exp(s_ctx - m[..., None])
    denom = e_loc.sum(-1) + e_ctx.sum(-1) + jnp.exp(sk - m)
    o = (jnp.einsum("bnhgqk,bnkhd->bnhgqd", e_loc.astype(vb.dtype), vb, preferred_element_type=F32)
         + jnp.einsum("bnhgqk,bkhd->bnhgqd", e_ctx.astype(vc.dtype), vc, preferred_element_type=F32))
    o = o / denom[..., None]
    o = o.transpose(0, 1, 4, 2, 3, 5).reshape(B, T, Hq * dh)
    return o.astype(q.dtype)


def ctx_attention(qc, kc, vc, sink):
    B, L, Hq, dh = qc.shape
    G = Hq // ATT_KV_HEADS
    qg = qc.reshape(B, L, ATT_KV_HEADS, G, dh)
    s = jnp.einsum("bqhgd,bkhd->bhgqk", qg, kc, preferred_element_type=F32) * dh ** -0.5
    s_sink = jnp.broadcast_to(sink.astype(F32).reshape(1, ATT_KV_HEADS, G, 1, 1), (B, ATT_KV_HEADS, G, L, 1))
    p = jax.nn.softmax(jnp.concatenate([s, s_sink], axis=-1), axis=-1)[..., :L]
    o = jnp.einsum("bhgqk,bkhd->bqhgd", p.astype(vc.dtype), vc, preferred_element_type=F32)
    return o.reshape(B, L, Hq * dh).astype(qc.dtype)


def attention_mixer(pl, pc, sink, cos, sin, want_ctx):
    B, T, _ = pl["att_q"].shape
    L = pc["att_q"].shape[1]
    dh = ATT_HEAD_DIM
    q = apply_rope(pl["att_q"].reshape(B, T, ATT_HEADS, dh), cos, sin)
    k = apply_rope(pl["att_k"].reshape(B, T, ATT_KV_HEADS, dh), cos, sin)
    v = pl["att_v"].reshape(B, T, ATT_KV_HEADS, dh)
    kc = pc["att_k"].reshape(B, L, ATT_KV_HEADS, dh)
    vc = pc["att_v"].reshape(B, L, ATT_KV_HEADS, dh)
    o_lat = band_attention(q, k, v, kc, vc, sink) * jax.nn.silu(pl["att_z"])
    o_ctx = None
    if want_ctx:
        qc = pc["att_q"].reshape(B, L, ATT_HEADS, dh)
        o_ctx = ctx_attention(qc, kc, vc, sink) * jax.nn.silu(pc["att_z"])
    return o_lat, o_ctx


def gdn_chunked(q, k, v, beta, g, s0, want_out):
    B, T, H, dk = q.shape
    dv = v.shape[-1]
    C = GDN_CHUNK
    N = T // C

    def chunks(t):
        return jnp.moveaxis(t.reshape(B, N, C, H, *t.shape[3:]), 2, 3)

    q, k, v, beta, g = chunks(q), chunks(k), chunks(v), chunks(beta), chunks(g)
    G = jnp.cumsum(g, axis=-1)
    diff = G[..., :, None] - G[..., None, :]
    strict = jnp.tril(jnp.ones((C, C), bool), -1)
    incl = jnp.tril(jnp.ones((C, C), bool))
    kk = jnp.einsum("bnhid,bnhjd->bnhij", k, k)
    lmat = jnp.where(strict, beta[..., :, None] * jnp.exp(jnp.where(strict, diff, 0.0)) * kk, 0.0)
    a_mat = lmat + jnp.eye(C, dtype=F32)
    gam = jnp.exp(G)
    w = lax.linalg.triangular_solve(a_mat, (beta * gam)[..., None] * k,
                                    left_side=True, lower=True, unit_diagonal=True)
    u = lax.linalg.triangular_solve(a_mat, beta[..., None] * v,
                                    left_side=True, lower=True, unit_diagonal=True)
    kdec = jnp.exp(G[..., -1:] - G)[..., None] * k
    glast = jnp.exp(G[..., -1])
    xs = [w, u, kdec, glast]
    if want_out:
        qk = jnp.einsum("bnhid,bnhjd->bnhij", q, k)
        aqk = jnp.where(incl, jnp.exp(jnp.where(incl, diff, 0.0)) * qk, 0.0)
        xs = xs + [gam[..., None] * q, aqk]
    xs = [jnp.moveaxis(t, 1, 0) for t in xs]

    def step(s, inp):
        w_n, u_n, kdec_n, gl_n = inp[0], inp[1], inp[2], inp[3]
        u_n = u_n - jnp.einsum("bhcd,bhde->bhce", w_n, s)
        s_new = gl_n[..., None, None] * s + jnp.einsum("bhcd,bhce->bhde", kdec_n, u_n)
        if want_out:
            o = (jnp.einsum("bhcd,bhde->bhce", inp[4], s)
                 + jnp.einsum("bhij,bhje->bhie", inp[5], u_n))
            return s_new, o
        return s_new, None

    s_fin, o = lax.scan(step, s0, xs)
    if not want_out:
        return None, s_fin
    o = jnp.moveaxis(jnp.moveaxis(o, 0, 1), 3, 2).reshape(B, T, H, dv)
    return o, s_fin


def gdn_prepare(p, conv_w, a_log, dt_bias):
    qkv = jnp.concatenate([p["gdn_q"], p["gdn_k"], p["gdn_v"]], axis=-1)
    qkv = jax.nn.silu(depthwise_conv(qkv, conv_w)).astype(F32)
    B, T, _ = qkv.shape
    qkv = qkv.reshape(B, T, 3, GDN_HEADS, GDN_HEAD_DIM)
    q = l2norm(qkv[:, :, 0]) * GDN_HEAD_DIM ** -0.5
    k = l2norm(qkv[:, :, 1])
    v = qkv[:, :, 2]
    beta = jax.nn.sigmoid(p["gdn_b"].astype(F32).reshape(B, T, N_DIR, GDN_HEADS))
    g = -jnp.exp(a_log.astype(F32)) * jax.nn.softplus(
        p["gdn_a"].astype(F32).reshape(B, T, N_DIR, GDN_HEADS) + dt_bias.astype(F32))
    return q, k, v, beta, g


def gdn_mixer(pl, pc, conv_w, a_log, dt_bias, norm_w, want_ctx):
    lat = gdn_prepare(pl, conv_w, a_log, dt_bias)
    ctx = gdn_prepare(pc, conv_w, a_log, dt_bias)
    B = lat[0].shape[0]
    s0 = jnp.zeros((B, GDN_HEADS, GDN_HEAD_DIM, GDN_HEAD_DIM), F32)
    o_lat = 0.0
    o_ctx = 0.0
    for d in range(N_DIR):
        qc, kc, vc, bc, gc = ctx
        oc, sc = gdn_chunked(flip_if(qc, d), flip_if(kc, d), flip_if(vc, d),
                             flip_if(bc[:, :, d], d), flip_if(gc[:, :, d], d), s0, want_ctx)
        ql, kl, vl, bl, gl = lat
        ol, _ = gdn_chunked(flip_if(ql, d), flip_if(kl, d), flip_if(vl, d),
                            flip_if(bl[:, :, d], d), flip_if(gl[:, :, d], d), sc, True)
        o_lat = o_lat + flip_if(ol, d)
        if want_ctx:
            o_ctx = o_ctx + flip_if(oc, d)

    def finish(o, z):
        Bo, To = o.shape[:2]
        return rms_norm(o, norm_w).reshape(Bo, To, GDN_WIDTH) * jax.nn.silu(z.astype(F32))

    return finish(o_lat, pl["gdn_z"]), (finish(o_ctx, pc["gdn_z"]) if want_ctx else None)


def linear_scan(a, b, h0):
    b = b.at[:, 0].add(a[:, 0] * h0)

    def comb(l, r):
        return (l[0] * r[0], r[0] * l[1] + r[1])

    _, h = lax.associative_scan(comb, (a, b), axis=1)
    return h


def lru_gates(xc, w_r, b_r, w_i, b_i, lam):
    B, T, W = xc.shape
    xb = xc.reshape(B, T, LRU_BLOCKS, LRU_BLOCK_DIM)
    r = jax.nn.sigmoid(jnp.einsum("btnd,nde->btne", xb, w_r.astype(F32)).reshape(B, T, W) + b_r.astype(F32))
    i = jax.nn.sigmoid(jnp.einsum("btnd,nde->btne", xb, w_i.astype(F32)).reshape(B, T, W) + b_i.astype(F32))
    log_a = -LRU_C * r * jax.nn.softplus(-lam.astype(F32))
    a = jnp.exp(log_a)
    b = jnp.sqrt(-jnp.expm1(2.0 * log_a)) * (i * xc)
    return a, b


def lru_mixer(pl, pc, conv_w, conv_b, w_r, b_r, w_i, b_i, lam, want_ctx):
    xl = depthwise_conv(pl["lru_x"], conv_w, conv_b).astype(F32)
    xc = depthwise_conv(pc["lru_x"], conv_w, conv_b).astype(F32)
    B = xl.shape[0]
    h0 = jnp.zeros((B, LRU_WIDTH), F32)
    h_lat = 0.0
    h_ctx = 0.0
    for d in range(N_DIR):
        ac, bc = lru_gates(flip_if(xc, d), w_r[d], b_r[d], w_i[d], b_i[d], lam[d])
        hc = linear_scan(ac, bc, h0)
        al, bl = lru_gates(flip_if(xl, d), w_r[d], b_r[d], w_i[d], b_i[d], lam[d])
        hl = linear_scan(al, bl, hc[:, -1])
        h_lat = h_lat + flip_if(hl, d)
        if want_ctx:
            h_ctx = h_ctx + flip_if(hc, d)
    o_lat = h_lat * jax.nn.silu(pl["lru_z"].astype(F32))
    o_ctx = h_ctx * jax.nn.silu(pc["lru_z"].astype(F32)) if want_ctx else None
    return o_lat, o_ctx


def setup_inputs(seed: int = 0) -> dict:
    key = jax.random.key(seed)
    ks = jax.random.split(key, 24)
    D = D_MODEL

    def nrm(k, shape, s):
        return jax.random.normal(k, shape, F32) * s

    x = nrm(ks[0], (BATCH, SEQ, D), 1.0)
    c = nrm(ks[1], (BATCH, D), 1.0)
    ctx = nrm(ks[2], (BATCH, CTX_LEN, D), 1.0)
    c_ctx = nrm(ks[3], (D,), 1.0)
    norm_g = 1.0 + nrm(ks[4], (DEPTH, D), 0.1)
    w_mod = nrm(ks[5], (DEPTH, D, 3 * D), 0.5 * D ** -0.5)
    b_mod = nrm(ks[6], (DEPTH, 3 * D), 0.01)
    w_in = nrm(ks[7], (DEPTH, D, IN_WIDTH), D ** -0.5)
    att_sink = nrm(ks[8], (DEPTH, ATT_HEADS), 0.5)
    gdn_conv = nrm(ks[9], (DEPTH, GDN_CONV, 3 * GDN_WIDTH), GDN_CONV ** -0.5)
    gdn_a_log = jnp.log(jax.random.uniform(ks[10], (DEPTH, N_DIR, GDN_HEADS), F32, 1.0, 16.0))
    dt = jnp.exp(jax.random.uniform(ks[11], (DEPTH, N_DIR, GDN_HEADS), F32, math.log(1e-3), math.log(1e-1)))
    gdn_dt_bias = dt + jnp.log(-jnp.expm1(-dt))
    gdn_norm = 1.0 + nrm(ks[12], (DEPTH, GDN_HEAD_DIM), 0.1)
    lru_conv_w = nrm(ks[13], (DEPTH, LRU_CONV, LRU_WIDTH), LRU_CONV ** -0.5)
    lru_conv_b = nrm(ks[14], (DEPTH, LRU_WIDTH), 0.01)
    lru_w_r = nrm(ks[15], (DEPTH, N_DIR, LRU_BLOCKS, LRU_BLOCK_DIM, LRU_BLOCK_DIM), LRU_BLOCK_DIM ** -0.5)
    lru_b_r = nrm(ks[16], (DEPTH, N_DIR, LRU_WIDTH), 0.01)
    lru_w_i = nrm(ks[17], (DEPTH, N_DIR, LRU_BLOCKS, LRU_BLOCK_DIM, LRU_BLOCK_DIM), LRU_BLOCK_DIM ** -0.5)
    lru_b_i = nrm(ks[18], (DEPTH, N_DIR, LRU_WIDTH), 0.01)
    a0 = jax.random.uniform(ks[19], (DEPTH, N_DIR, LRU_WIDTH), F32, 0.9, 0.999) ** (1.0 / LRU_C)
    lru_lambda = jnp.log(a0) - jnp.log1p(-a0)
    w_out = nrm(ks[20], (DEPTH, MIX_WIDTH, D), MIX_WIDTH ** -0.5)
    final_g = 1.0 + nrm(ks[21], (D,), 0.1)
    return {"x": x, "c": c, "ctx": ctx, "c_ctx": c_ctx, "norm_g": norm_g, "w_mod": w_mod, "b_mod": b_mod,
            "w_in": w_in, "att_sink": att_sink, "gdn_conv": gdn_conv, "gdn_a_log": gdn_a_log,
            "gdn_dt_bias": gdn_dt_bias, "gdn_norm": gdn_norm, "lru_conv_w": lru_conv_w,
            "lru_conv_b": lru_conv_b, "lru_w_r": lru_w_r, "lru_b_r": lru_b_r, "lru_w_i": lru_w_i,
            "lru_b_i": lru_b_i, "lru_lambda": lru_lambda, "w_out": w_out, "final_g": final_g}


def reference(x, c, ctx, c_ctx, norm_g, w_mod, b_mod, w_in, att_sink, gdn_conv, gdn_a_log, gdn_dt_bias,
              gdn_norm, lru_conv_w, lru_conv_b, lru_w_r, lru_b_r, lru_w_i, lru_b_i, lru_lambda, w_out,
              final_g):
    B, T, D = x.shape
    ROWS = T // GRID_W
    cos, sin = axial_rope(ROWS)
    y = ctx
    for l in range(DEPTH):
        want_ctx = l < DEPTH - 1
        mod = jax.nn.silu(c) @ w_mod[l] + b_mod[l]
        mod_c = jax.nn.silu(c_ctx) @ w_mod[l] + b_mod[l]
        shift, scale, gate = jnp.split(mod[:, None, :], 3, axis=-1)
        shift_c, scale_c, gate_c = jnp.split(mod_c, 3)
        h = rms_norm(x, norm_g[l]) * (1.0 + scale) + shift
        hc = rms_norm(y, norm_g[l]) * (1.0 + scale_c) + shift_c
        pl = split_cols(h @ w_in[l])
        pc = split_cols(hc @ w_in[l])
        a_l, a_c = attention_mixer(pl, pc, att_sink[l], cos, sin, want_ctx)
        g_l, g_c = gdn_mixer(pl, pc, gdn_conv[l], gdn_a_log[l], gdn_dt_bias[l], gdn_norm[l], want_ctx)
        r_l, r_c = lru_mixer(pl, pc, lru_conv_w[l], lru_conv_b[l], lru_w_r[l], lru_b_r[l],
                             lru_w_i[l], lru_b_i[l], lru_lambda[l], want_ctx)
        o = jnp.concatenate([a_l.astype(x.dtype), g_l.astype(x.dtype), r_l.astype(x.dtype)], axis=-1) @ w_out[l]
        x = x + gate * o
        if want_ctx:
            oc = jnp.concatenate([a_c.astype(y.dtype), g_c.astype(y.dtype), r_c.astype(y.dtype)], axis=-1) @ w_out[l]
            y = y + gate_c * oc
    return rms_norm(x, final_g)
```

```python
import numpy as np
import concourse.bass as bass
import concourse.mybir as mybir
from concourse.bass_utils import run_bass_kernel_spmd

F32 = mybir.dt.float32
BF16 = mybir.dt.bfloat16
AF = mybir.ActivationFunctionType
ALU = mybir.AluOpType
AX = mybir.AxisListType

COMPUTE = ("pe", "act", "dve", "pool")
SEM_CAP = 16000
N_DMA_SEMS = 16


class Prog:
    def __init__(self, nc):
        self.nc = nc
        self.ops = []

    def op(self, eng, fn, reads=(), writes=(), dma=False):
        self.ops.append(dict(eng=eng, fn=fn, reads=tuple(reads), writes=tuple(writes), dma=dma))
        return len(self.ops) - 1

    def pe(self, fn, reads=(), writes=()):
        return self.op("pe", fn, reads, writes)

    def act(self, fn, reads=(), writes=()):
        return self.op("act", fn, reads, writes)

    def dve(self, fn, reads=(), writes=()):
        return self.op("dve", fn, reads, writes)

    def pool(self, fn, reads=(), writes=()):
        return self.op("pool", fn, reads, writes)

    def dma(self, fn, reads=(), writes=(), q="sp"):
        return self.op(q, fn, reads, writes, dma=True)

    def cc(self, fn, reads=(), writes=()):
        i = self.op("pool", fn, reads, writes, dma=True)
        self.ops[i]["cc"] = True
        return i

    def barrier(self):
        if not hasattr(self, "_bar_t"):
            raise RuntimeError("set P._bar_t (a [128,1] sbuf AP) first")
        t = self._bar_t
        i = self.op("dve", lambda e: e.memset(t, 0.0), (), ())
        self.ops[i]["barrier"] = True
        return i

    def build(self):
        nc = self.nc
        ops = self.ops
        n = len(ops)
        last_w = {}
        readers = {}
        deps = [None] * n
        last_eng = {}
        dmas_since = []
        cur_bar = None
        for i, o in enumerate(ops):
            d = {}
            if o.get("barrier"):
                for e_, j in last_eng.items():
                    if e_ != "dve":
                        d[j] = True
                for j in dmas_since:
                    d[j] = True
                dmas_since = []
                deps[i] = list(d.keys())
                cur_bar = i
                last_eng["dve"] = i
                continue
            for k in o["reads"]:
                j = last_w.get(k)
                if j is not None:
                    d[j] = True
            for k in o["writes"]:
                j = last_w.get(k)
                if j is not None and j not in d:
                    d[j] = d.get(j, False)
                for j in readers.get(k, ()):
                    if j not in d:
                        d[j] = False
            d.pop(i, None)
            keep = []
            for j, raw in d.items():
                oj = ops[j]
                if (not oj["dma"]) and (not o["dma"]) and oj["eng"] == o["eng"] and not raw:
                    continue
                if (not oj["dma"]) and (not o["dma"]) and oj["eng"] == o["eng"] == "pe":
                    continue
                keep.append(j)
            if cur_bar is not None and (o["dma"] or o["eng"] != "dve") and cur_bar not in keep:
                keep.append(cur_bar)
            deps[i] = keep
            if o["dma"]:
                if not o.get("cc"):
                    dmas_since.append(i)
            else:
                last_eng[o["eng"]] = i
            for k in o["reads"]:
                readers.setdefault(k, []).append(i)
            for k in o["writes"]:
                last_w[k] = i
                readers[k] = []
        dma_idx = [i for i, o in enumerate(ops) if o["dma"] and not o.get("cc")]
        for k, i in enumerate(dma_idx):
            if k >= N_DMA_SEMS:
                j = dma_idx[k - N_DMA_SEMS]
                if j not in deps[i]:
                    deps[i].append(j)
        sig = [False] * n
        for i in range(n):
            for j in deps[i]:
                sig[j] = True
        cnt = {e: 0 for e in COMPUTE}
        token = [None] * n
        ndma = 0
        ncc = 0
        final_dma = []
        for i, o in enumerate(ops):
            if o.get("cc"):
                ncc += 1
                token[i] = ("cc", 0, ncc)
                sig[i] = True
            elif o["dma"]:
                token[i] = ("dma", ndma % N_DMA_SEMS, 16 * (ndma // N_DMA_SEMS + 1))
                ndma += 1
                sig[i] = True
                final_dma.append(i)
            elif sig[i]:
                cnt[o["eng"]] += 1
                token[i] = ("cmp", o["eng"], cnt[o["eng"]])
        self.stats = dict(n_ops=n, ndma=ndma, cnt=dict(cnt))
        import contextlib
        stack = contextlib.ExitStack()
        dma_sems = [stack.enter_context(nc.semaphore(f"dq{i}")) for i in range(min(N_DMA_SEMS, max(ndma, 1)))]
        cc_sem = stack.enter_context(nc.semaphore("ccsem")) if ncc else None
        cmp_sems = {}
        for e in COMPUTE:
            ne = cnt[e] // SEM_CAP + 1
            cmp_sems[e] = [stack.enter_context(nc.semaphore(f"c_{e}{k}")) for k in range(ne)]

        def sem_of(tok):
            kind, a, v = tok
            if kind == "dma":
                return dma_sems[a], v
            if kind == "cc":
                return cc_sem, v
            ep = (v - 1) // SEM_CAP
            return cmp_sems[a][ep], v - ep * SEM_CAP

        by_eng = {}
        for i, o in enumerate(ops):
            by_eng.setdefault(o["eng"], []).append(i)
        last_dma_tok = {}
        for i in final_dma:
            last_dma_tok[token[i][1]] = token[i]

        block = stack.enter_context(nc.Block())

        def emit(engname):
            idxs = by_eng.get(engname, [])

            def body(eng):
                seen = {}
                for i in idxs:
                    o = ops[i]
                    for j in deps[i]:
                        tok = token[j]
                        key = (tok[0], tok[1])
                        if seen.get(key, 0) >= tok[2]:
                            continue
                        seen[key] = tok[2]
                        s, v = sem_of(tok)
                        if getattr(self, "dbg", None) and i >= self.dbg:
                            print("WAIT op", i, engname, "on op", j, ops[j]["eng"], tok, "sem", s, v)
                        eng.wait_ge(s, v)
                    ins = o["fn"](eng)
                    if sig[i]:
                        s, v = sem_of(token[i])
                        if o.get("cc"):
                            ins.then_inc(s)
                        else:
                            ins.then_inc(s, 16 if o["dma"] else 1)
                if engname == "sp":
                    for tok in last_dma_tok.values():
                        s, v = sem_of(tok)
                        eng.wait_ge(s, v)
                if engname == "pool" and ncc:
                    eng.wait_ge(cc_sem, ncc)
            return body

        block.tensor(emit("pe"))
        block.scalar(emit("act"))
        block.vector(emit("dve"))
        block.gpsimd(emit("pool"))
        block.sync(emit("sp"))
        stack.close()


import math
from contextlib import ExitStack
import numpy as np
import ml_dtypes
from concourse.ap import AP

T = 8192
L = 256
TT = T + L
D = 1024
EPS = 1e-6
TILES = [(512 * i, 512) for i in range(16)] + [(T, L)]
NATT = 384
NGDN = 516
NLRU = 256
GCH = 128


def revap(base):
    apl = [list(p) for p in base.ap]
    n = apl[-1][1]
    stp = apl[-1][0]
    apl[-1] = [-stp, n]
    return AP(base.tensor, base.offset + (n - 1) * stp, apl)


class Arena:
    def __init__(self, nc, st, name, words):
        self.t = st.enter_context(nc.sbuf_tensor(name, [128, words], F32))
        self.words = words
        self.off = 0
        self.name = name

    def mark(self):
        return self.off

    def reset(self, m=0):
        self.off = m

    def alloc(self, shape, dt=F32, parts=128):
        nel = int(np.prod(shape))
        sz = mybir.dt.size(dt)
        words = (nel * sz + 3) // 4
        words = (words + 7) // 8 * 8
        assert self.off + words <= self.words, (self.name, self.off, words, self.words)
        v = self.t[0:parts, self.off:self.off + words]
        self.off += words
        if dt != F32:
            v = v.bitcast(dt)
        v = v[:, 0:nel]
        if len(shape) == 2:
            v = v.rearrange("p (a b) -> p a b", a=shape[0], b=shape[1])
        elif len(shape) == 3:
            v = v.rearrange("p (a b c) -> p a b c", a=shape[0], b=shape[1], c=shape[2])
        return v


class Ctx:
    pass


class ChunkedDram:
    def __init__(self, nc, name, rows, dt, csize=1024):
        self.rows = rows
        self.csize = csize
        self.chunks = [(csize * c, csize) for c in range(T // csize)] + [(T, L)]
        self.nchunks = len(self.chunks)
        self.t = [nc.dram_tensor(f"{name}_c{c}", [rows, n], dt, kind="Internal").ap() for c, (t0, n) in enumerate(self.chunks)]

    def cidx(self, t0):
        return self.nchunks - 1 if t0 >= T else t0 // self.csize

    def last_tile_of_chunk(self, ti):
        if ti == 16:
            return True
        return (512 * (ti + 1)) % self.csize == 0

    def tiles_of_chunk(self, c):
        if c == self.nchunks - 1:
            return [16]
        per = self.csize // 512
        return list(range(per * c, per * c + per))

    def __getitem__(self, key):
        rs, ts = key
        r0 = 0 if rs.start is None else rs.start
        r1 = self.rows if rs.stop is None else rs.stop
        t0, t1 = ts.start, ts.stop
        c = self.cidx(t0)
        c0, n = self.chunks[c]
        assert c0 <= t0 and t1 <= c0 + n, (t0, t1)
        return self.t[c][r0:r1, t0 - c0:t1 - c0]

    def pk(self, t0, n):
        return self[:, t0:t0 + n].rearrange("(k p) t -> p k t", p=128)


def setup(nc, st, sbuf_words=49000):
    C = Ctx()
    C.nc = nc
    C.st = st
    C.P = Prog(nc)
    C.A = Arena(nc, st, "arena", sbuf_words)
    C.banks = [st.enter_context(nc.psum_tensor(f"bank{i}", [128, 512], F32)) for i in range(8)]
    return C


def dram_in(nc, name, shape, dt=F32):
    return nc.dram_tensor(name, list(shape), dt, kind="ExternalInput").ap()


def dram_out(nc, name, shape, dt=F32):
    return nc.dram_tensor(name, list(shape), dt, kind="ExternalOutput").ap()


def dram_tmp(nc, name, shape, dt=F32):
    return nc.dram_tensor(name, list(shape), dt, kind="Internal").ap()


def phase_h_gen(C, xT, cv, wmod, bmod, ng, H, xkey=None, xdt=F32):
    P, A, nc = C.P, C.A, C.nc
    m0 = A.mark()
    ones_bf = A.alloc([128], BF16)
    cvt = A.alloc([8, 2])
    scv = A.alloc([8, 2])
    bm = A.alloc([16])
    ngt = A.alloc([8])
    modT = A.alloc([16, 2])
    Avec = A.alloc([8, 2])
    C.ones_bf = ones_bf
    P.pool(lambda e: e.memset(ones_bf, 1.0), writes=["ones_bf"])
    P.dma(lambda e: e.dma_start(out=cvt, in_=cv), writes=["cvt"])
    P.dma(lambda e: e.dma_start(out=bm, in_=bmod), writes=["bm"])
    P.dma(lambda e: e.dma_start(out=ngt, in_=ng), writes=["ngt"])
    P.act(lambda e: e.activation(out=scv, in_=cvt, func=AF.Silu), reads=["cvt"], writes=["scv"])
    m1 = A.mark()
    wm = A.alloc([8, 1024])
    psm = C.banks[0][:, 0:32].rearrange("p (a b) -> p a b", a=16, b=2)
    for blk in range(2):
        P.dma(lambda e, blk=blk: e.dma_start(out=wm, in_=wmod[:, :, blk * 1024:(blk + 1) * 1024]), writes=["wm"])
        for cc in range(8):
            for k in range(8):
                P.pe(lambda e, blk=blk, cc=cc, k=k: e.matmul(psm[:, blk * 8 + cc, :], lhsT=wm[:, k, cc * 128:(cc + 1) * 128],
                                                            rhs=scv[:, k, :], start=(k == 0), stop=(k == 7)),
                     reads=["wm", "scv"], writes=[("bank", 0)])
    P.dve(lambda e: e.tensor_tensor(out=modT, in0=psm, in1=bm.unsqueeze(2).to_broadcast([128, 16, 2]), op=ALU.add),
          reads=[("bank", 0), "bm"], writes=["modT"])
    P.dve(lambda e: e.scalar_tensor_tensor(out=Avec, in0=modT[:, 8:16, :], scalar=1.0,
                                           in1=ngt.unsqueeze(2).to_broadcast([128, 8, 2]), op0=ALU.add, op1=ALU.mult),
          reads=["modT", "ngt"], writes=["Avec"])
    P.dve(lambda e: e.tensor_scalar(out=Avec, in0=Avec, scalar1=32.0, scalar2=None, op0=ALU.mult),
          reads=["Avec"], writes=["Avec"])
    xts = [A.alloc([8, 512]) for _ in range(2)]
    xss = xts if xdt == F32 else [A.alloc([8, 512], xdt) for _ in range(2)]
    sqs = [A.alloc([8, 512], BF16) for _ in range(2)]
    hbs = [A.alloc([8, 512], BF16) for _ in range(2)]
    rstd = [A.alloc([512]) for _ in range(2)]
    if isinstance(xT, ChunkedDram):
        xget = xT.pk
    else:
        xTv = xT.rearrange("(k p) t -> p k t", p=128)
        xget = lambda t0, n: xTv[:, :, t0:t0 + n]
    yield
    for ti, (t0, n) in enumerate(TILES):
        b = ti % 2
        s = 1 if t0 >= T else 0
        xt, sq, hb, rs = xts[b], sqs[b], hbs[b], rstd[b]
        xs_ = xss[b]
        ps = C.banks[5 + b]
        def _ld(tj):
            tt0, nn = TILES[tj]
            bb = tj % 2
            ckk = xT.cidx(tt0) if isinstance(xT, ChunkedDram) else 0
            xq = xss[bb]
            P.dma(lambda e: e.dma_start(out=xq[:, :, 0:nn], in_=xget(tt0, nn)), reads=([(xkey, ckk)] if xkey else []), writes=[("xs", bb)])
        if ti == 0:
            _ld(0)
        if ti + 1 < len(TILES):
            _ld(ti + 1)
        P.act(lambda e, xs_=xs_, sq=sq, n=n: e.activation(out=sq[:, :, 0:n], in_=xs_[:, :, 0:n], func=AF.Square),
              reads=[("xs", b)], writes=[("sq", b)])
        for k in range(8):
            P.pe(lambda e, ps=ps, sq=sq, k=k, n=n: e.matmul(ps[:, 0:n], lhsT=ones_bf, rhs=sq[:, k, 0:n], start=(k == 0), stop=(k == 7)),
                 reads=[("sq", b), "ones_bf"], writes=[("bank", 5 + b)])
        P.dve(lambda e, ps=ps, rs=rs, n=n: e.tensor_scalar(out=rs[:, 0:n], in0=ps[:, 0:n], scalar1=1024.0 * EPS, scalar2=None,
                                                          op0=ALU.add),
              reads=[("bank", 5 + b)], writes=[("rstd", b)])
        P.act(lambda e, rs=rs, n=n: e.activation(out=rs[:, 0:n], in_=rs[:, 0:n], func=AF.Ln), reads=[("rstd", b)], writes=[("rstd", b)])
        P.act(lambda e, rs=rs, n=n: e.activation(out=rs[:, 0:n], in_=rs[:, 0:n], func=AF.Exp, scale=-0.5), reads=[("rstd", b)], writes=[("rstd", b)])
        P.dve(lambda e, xt=xt, xs_=xs_, rs=rs, n=n: e.tensor_tensor(out=xt[:, :, 0:n], in0=xs_[:, :, 0:n],
                                                                   in1=rs[:, 0:n].unsqueeze(1).to_broadcast([128, 8, n]), op=ALU.mult),
              reads=[("xs", b), ("xt", b), ("rstd", b)], writes=[("xt", b)])
        for k in range(8):
            if k % 8 < 5:
                P.act(lambda e, xt=xt, hb=hb, k=k, n=n, s=s: e.activation(out=hb[:, k, 0:n], in_=xt[:, k, 0:n], func=AF.Identity,
                                                                          scale=Avec[:, k, s:s + 1], bias=modT[:, k, s:s + 1]),
                      reads=[("xt", b), "Avec", "modT"], writes=[("hb", b, k)])
            else:
                P.pool(lambda e, xt=xt, hb=hb, k=k, n=n, s=s: e.tensor_scalar(out=hb[:, k, 0:n], in0=xt[:, k, 0:n], scalar1=Avec[:, k, s:s + 1],
                                                                             scalar2=modT[:, k, s:s + 1], op0=ALU.mult, op1=ALU.add),
                       reads=[("xt", b), "Avec", "modT"], writes=[("hb", b, k)])
        P.dma(lambda e, hb=hb, t0=t0, n=n: e.dma_start(out=H[:, :, t0:t0 + n], in_=hb[:, :, 0:n]), reads=[("hb", b, k) for k in range(8)], writes=[("H", ti)])
        yield
    A.reset(m0)


def drain(g):
    for _ in g:
        pass


def phase_h(*a, **k):
    drain(phase_h_gen(*a, **k))


def load_weights(C, win, c0, ncols, tag):
    P, A = C.P, C.A
    wb = A.alloc([8, ncols], BF16)
    m = A.mark()
    wf = A.alloc([8, ncols])
    P.dma(lambda e: e.dma_start(out=wf, in_=win[:, :, c0:c0 + ncols]), writes=[("wf", tag)])
    for k in range(8):
        if k % 2 == 0:
            P.dve(lambda e, k=k: e.tensor_copy(out=wb[:, k, :], in_=wf[:, k, :]), reads=[("wf", tag)], writes=[("wb", tag, k)])
        else:
            P.act(lambda e, k=k: e.copy(out=wb[:, k, :], in_=wf[:, k, :]), reads=[("wf", tag)], writes=[("wb", tag, k)])
    C.wkeys = [("wb", tag, k) for k in range(8)]
    return wb, m


def project(C, H, wb, wtag, chunks, evac, banks, tiles=None, after_tile=None):
    P, A = C.P, C.A
    hts = [A.alloc([8, 512], BF16) for _ in range(2)]
    tl = TILES if tiles is None else tiles
    cnt = 0
    def _ld(tj):
        tt0, nn = tl[tj]
        hq = hts[tj % 2]
        P.dma(lambda e: e.dma_start(out=hq[:, :, 0:nn], in_=H[:, :, tt0:tt0 + nn]), reads=[("H", tt0 // 512)], writes=[("ht", wtag, tj % 2)])
    for ti, (t0, n) in enumerate(tl):
        b = ti % 2
        ht = hts[b]
        if ti == 0:
            _ld(0)
        if ti + 1 < len(tl):
            _ld(ti + 1)
        for ci, (c0, m) in enumerate(chunks):
            bank = banks[cnt % len(banks)]
            cnt += 1
            pskey = ("bank", bank)
            ps = C.banks[bank]
            for k in range(8):
                P.pe(lambda e, ps=ps, ht=ht, k=k, c0=c0, m=m, n=n: e.matmul(ps[0:m, 0:n], lhsT=wb[:, k, c0:c0 + m], rhs=ht[:, k, 0:n],
                                                                            start=(k == 0), stop=(k == 7)),
                     reads=[("ht", wtag, b), ("wb", wtag, k)], writes=[pskey])
            evac(ti, t0, n, ci, ps, pskey)
        if after_tile is not None:
            after_tile(ti)


def phase_lru(C, H, win, lru_cw, lru_cb, lru_w, lru_b, lru_lam, mixT, row0):
    P, A, nc = C.P, C.A, C.nc
    m0 = A.mark()
    wb, mw = load_weights(C, win, NATT + NGDN, NLRU, "lru")
    cw = A.alloc([4]); cb = A.alloc([1]); lw = A.alloc([2, 2, 128]); lb = A.alloc([2, 2]); lam = A.alloc([2])
    cst = A.alloc([2]); lbh = A.alloc([2, 2]); qtr = A.alloc([1])
    P.pool(lambda e: e.memset(qtr, 0.25), writes=["qtr"])
    P.dma(lambda e: e.dma_start(out=cw, in_=lru_cw), writes=["cw"])
    P.dma(lambda e: e.dma_start(out=cb, in_=lru_cb), writes=["cb"])
    P.dma(lambda e: e.dma_start(out=lw, in_=lru_w), writes=["lw"])
    P.dma(lambda e: e.dma_start(out=lb, in_=lru_b), writes=["lb"])
    P.dma(lambda e: e.dma_start(out=lam, in_=lru_lam), writes=["lam"])
    P.act(lambda e: e.activation(out=cst, in_=lam, func=AF.Exp, scale=-1.0), reads=["lam"], writes=["cst"])
    P.act(lambda e: e.activation(out=cst, in_=cst, func=AF.Ln, bias=1.0), reads=["cst"], writes=["cst"])
    P.dve(lambda e: e.tensor_scalar(out=cst, in0=cst, scalar1=-4.0, scalar2=None, op0=ALU.mult), reads=["cst"], writes=["cst"])
    P.dve(lambda e: e.tensor_scalar(out=lbh, in0=lb, scalar1=0.5, scalar2=None, op0=ALU.mult), reads=["lb"], writes=["lbh"])
    xc = A.alloc([TT])
    zs = A.alloc([TT], BF16)
    hsum = A.alloc([TT])
    mxr = A.mark()
    xraw = A.alloc([TT])
    mp = A.mark()

    def evac(ti, t0, n, ci, ps, pskey):
        if ci == 0:
            P.act(lambda e: e.copy(out=xraw[:, t0:t0 + n], in_=ps[:, 0:n]), reads=[pskey], writes=[("xraw", ti)])
        else:
            P.act(lambda e: e.activation(out=zs[:, t0:t0 + n], in_=ps[:, 0:n], func=AF.Silu), reads=[pskey], writes=[("zs", ti)])

    project(C, H, wb, "lru", [(0, 128), (128, 128)], evac, banks=[0, 1, 2, 3])
    A.reset(mp)
    allx = [("xraw", ti) for ti in range(17)]
    for (s, e_) in ((0, T), (T, TT)):
        P.dve(lambda e, s=s, e_=e_: e.tensor_scalar(out=xc[:, s:e_], in0=xraw[:, s:e_], scalar1=cw[:, 2:3], scalar2=cb[:, 0:1],
                                                    op0=ALU.mult, op1=ALU.add), reads=allx + ["cw", "cb"], writes=["xc"])
        P.dve(lambda e, s=s, e_=e_: e.scalar_tensor_tensor(out=xc[:, s + 2:e_], in0=xraw[:, s:e_ - 2], scalar=cw[:, 0:1],
                                                           in1=xc[:, s + 2:e_], op0=ALU.mult, op1=ALU.add), reads=allx + ["xc"], writes=["xc"])
        P.dve(lambda e, s=s, e_=e_: e.scalar_tensor_tensor(out=xc[:, s + 1:e_], in0=xraw[:, s:e_ - 1], scalar=cw[:, 1:2],
                                                           in1=xc[:, s + 1:e_], op0=ALU.mult, op1=ALU.add), reads=allx + ["xc"], writes=["xc"])
        P.dve(lambda e, s=s, e_=e_: e.scalar_tensor_tensor(out=xc[:, s:e_ - 1], in0=xraw[:, s + 1:e_], scalar=cw[:, 3:4],
                                                           in1=xc[:, s:e_ - 1], op0=ALU.mult, op1=ALU.add), reads=allx + ["xc"], writes=["xc"])
    P.barrier()
    A.reset(mxr)
    GS = 4
    NB = 2 * GS
    tr = [A.alloc([512]) for _ in range(NB)]
    tiu = [A.alloc([512]) for _ in range(NB)]
    sq = [A.alloc([512]) for _ in range(NB)]
    hb = [A.alloc([512]) for _ in range(NB)]
    cnt = 0
    for d in range(2):
        order = [16] + list(range(16)) if d == 0 else [16] + list(range(15, -1, -1))
        st_ = dict(prev=None, prevti=None, prevb=None)
        for g0 in range(0, len(order), GS):
            grp = []
            for ti in order[g0:g0 + GS]:
                grp.append((ti, cnt % NB))
                cnt += 1
            for (ti, b) in grp:
                t0, n = TILES[ti]
                bk = 2 * (b % 4)
                psr = C.banks[bk]
                psi = C.banks[bk + 1]
                kr, ki = ("bank", bk), ("bank", bk + 1)
                P.pe(lambda e, psr=psr, d=d, t0=t0, n=n: e.matmul(psr[:, 0:n], lhsT=lw[:, d, 0, :], rhs=xc[:, t0:t0 + n], start=True, stop=True),
                     reads=["lw", "xc"], writes=[kr])
                P.pe(lambda e, psi=psi, d=d, t0=t0, n=n: e.matmul(psi[:, 0:n], lhsT=lw[:, d, 1, :], rhs=xc[:, t0:t0 + n], start=True, stop=True),
                     reads=["lw", "xc"], writes=[ki])
                a_, u_, s_ = tr[b], tiu[b], sq[b]
                P.act(lambda e, a_=a_, psr=psr, d=d, n=n: e.activation(out=a_[:, 0:n], in_=psr[:, 0:n], func=AF.Tanh, scale=0.5, bias=lbh[:, d, 0:1]),
                      reads=[kr, "lbh"], writes=[("tr", b)])
                P.act(lambda e, u_=u_, psi=psi, d=d, n=n: e.activation(out=u_[:, 0:n], in_=psi[:, 0:n], func=AF.Tanh, scale=0.5, bias=lbh[:, d, 1:2]),
                      reads=[ki, "lbh"], writes=[("tiu", b)])
                P.act(lambda e, a_=a_, d=d, n=n: e.activation(out=a_[:, 0:n], in_=a_[:, 0:n], func=AF.Exp, scale=cst[:, d:d + 1], bias=cst[:, d:d + 1]),
                      reads=[("tr", b), "cst"], writes=[("tr", b)])
                P.dve(lambda e, u_=u_, t0=t0, n=n: e.scalar_tensor_tensor(out=u_[:, 0:n], in0=u_[:, 0:n], scalar=1.0, in1=xc[:, t0:t0 + n],
                                                                         op0=ALU.add, op1=ALU.mult), reads=[("tiu", b), "xc"], writes=[("tiu", b)])
                P.dve(lambda e, a_=a_, s_=s_, n=n: e.scalar_tensor_tensor(out=s_[:, 0:n], in0=a_[:, 0:n], scalar=-0.25, in1=a_[:, 0:n],
                                                                         op0=ALU.mult, op1=ALU.mult), reads=[("tr", b)], writes=[("sq", b)])
            for (ti, b) in grp:
                t0, n = TILES[ti]
                a_, u_, s_, h_ = tr[b], tiu[b], sq[b], hb[b]
                P.act(lambda e, s_=s_, n=n: e.activation(out=s_[:, 0:n], in_=s_[:, 0:n], func=AF.Sqrt, bias=qtr[:, 0:1]),
                      reads=[("sq", b), "qtr"], writes=[("sq", b)])
                P.pool(lambda e, u_=u_, s_=s_, n=n: e.tensor_tensor(out=u_[:, 0:n], in0=u_[:, 0:n], in1=s_[:, 0:n], op=ALU.mult),
                       reads=[("tiu", b), ("sq", b)], writes=[("tiu", b)])
                prev = st_["prev"]
                if d == 0:
                    outv = hsum[:, t0:t0 + n]
                    init = 0.0 if prev is None else hsum[:, prev:prev + 1]
                    rd = [("tr", b), ("tiu", b)] + ([] if prev is None else [("hsum", st_["prevti"])])
                    P.dve(lambda e, outv=outv, a_=a_, u_=u_, n=n, init=init: e.tensor_tensor_scan(out=outv, data0=a_[:, 0:n], data1=u_[:, 0:n],
                                                                                                 initial=init, op0=ALU.mult, op1=ALU.add),
                          reads=rd, writes=[("hsum", ti)])
                    st_["prev"] = t0 + n - 1
                    st_["prevti"] = ti
                else:
                    init = 0.0 if prev is None else prev
                    rd = [("tr", b), ("tiu", b)] + ([] if prev is None else [("hb", st_["prevb"])])
                    P.dve(lambda e, h_=h_, a_=a_, u_=u_, n=n, init=init: e.tensor_tensor_scan(out=revap(h_[:, 0:n]), data0=revap(a_[:, 0:n]),
                                                                                            data1=revap(u_[:, 0:n]), initial=init,
                                                                                            op0=ALU.mult, op1=ALU.add),
                          reads=rd, writes=[("hb", b)])
                    P.pool(lambda e, h_=h_, t0=t0, n=n: e.tensor_tensor(out=hsum[:, t0:t0 + n], in0=hsum[:, t0:t0 + n], in1=h_[:, 0:n], op=ALU.add),
                           reads=[("hb", b), ("hsum", ti)], writes=[("hsum", ti)])
                    st_["prev"] = h_[:, 0:1]
                    st_["prevb"] = b
    ob = [A.alloc([512], BF16) for _ in range(2)]
    for ti, (t0, n) in enumerate(TILES):
        b = ti % 2
        o_ = ob[b]
        P.dve(lambda e, o_=o_, t0=t0, n=n: e.tensor_tensor(out=o_[:, 0:n], in0=hsum[:, t0:t0 + n], in1=zs[:, t0:t0 + n], op=ALU.mult),
              reads=[("hsum", ti), ("zs", ti)], writes=[("ob", b)])
        P.dma(lambda e, o_=o_, t0=t0, n=n: e.dma_start(out=mixT[row0:row0 + 128, t0:t0 + n], in_=o_[:, 0:n]), reads=[("ob", b)], writes=[("mixT", "lru", ti)])
    A.reset(m0)


OFF = dict(att_q=0, att_k=512, att_v=640, att_z=768, gdn_q=1280, gdn_k=1792, gdn_v=2304, gdn_b=2816, gdn_a=2824,
           gdn_z=2832, lru_x=3344, lru_z=3856)


def core_cols(j):
    cols = []
    cols += list(range(OFF["att_q"] + 128 * j, OFF["att_q"] + 128 * j + 128))
    g = j // 2
    cols += list(range(OFF["att_k"] + 64 * g, OFF["att_k"] + 64 * g + 64))
    cols += list(range(OFF["att_z"] + 128 * j, OFF["att_z"] + 128 * j + 128))
    cols += list(range(OFF["att_v"] + 64 * g, OFF["att_v"] + 64 * g + 64))
    for nm in ("gdn_q", "gdn_k", "gdn_v", "gdn_z"):
        cols += list(range(OFF[nm] + 128 * j, OFF[nm] + 128 * j + 128))
    cols += [OFF["gdn_b"] + j, OFF["gdn_b"] + 4 + j, OFF["gdn_a"] + j, OFF["gdn_a"] + 4 + j]
    cols += list(range(OFF["lru_x"] + 128 * j, OFF["lru_x"] + 128 * j + 128))
    cols += list(range(OFF["lru_z"] + 128 * j, OFF["lru_z"] + 128 * j + 128))
    assert len(cols) == NATT + NGDN + NLRU
    return np.array(cols)


def pk(v):
    v = np.asarray(v)
    return np.ascontiguousarray(v.reshape(8, 128, -1).transpose(1, 0, 2))


def prep_A(inp, l, b, j, xT_full):
    f = np.float32
    d = {}
    if xT_full is not None:
        d["xT"] = xT_full
    d["cv"] = pk(np.stack([inp["c"][b], inp["c_ctx"]], axis=1).astype(f))
    d["wmod"] = pk(inp["w_mod"][l][:, 0:2048])
    d["bmod"] = np.ascontiguousarray(inp["b_mod"][l][0:2048].reshape(16, 128).T)
    d["ng"] = np.ascontiguousarray(inp["norm_g"][l].reshape(8, 128).T)
    d["win"] = pk(inp["w_in"][l][:, core_cols(j)])
    ch = slice(128 * j, 128 * j + 128)
    d["lru_cw"] = np.ascontiguousarray(inp["lru_conv_w"][l][:, ch].T)
    d["lru_cb"] = np.ascontiguousarray(inp["lru_conv_b"][l][ch].reshape(128, 1))
    lw = np.zeros((128, 2, 2, 128), f)
    for dd in range(2):
        for gi, nm in enumerate(("lru_w_r", "lru_w_i")):
            for bb in range(2):
                lw[64 * bb:64 * bb + 64, dd, gi, 64 * bb:64 * bb + 64] = inp[nm][l][dd][2 * j + bb]
    d["lru_w"] = lw
    lb = np.zeros((128, 2, 2), f)
    for dd in range(2):
        lb[:, dd, 0] = inp["lru_b_r"][l][dd][ch]
        lb[:, dd, 1] = inp["lru_b_i"][l][dd][ch]
    d["lru_b"] = lb
    d["lru_lam"] = np.ascontiguousarray(inp["lru_lambda"][l][:, ch].T)
    return d


def tiles_of(t0, t1):
    return list(range(t0 // 512, (t1 - 1) // 512 + 1))


def phase_att(C, H, win, ropeC, ropeS, rotT_d, e65_d, mask_d, ident_d, sink_d, mixT, want_ctx):
    P, A, nc = C.P, C.A, C.nc
    m0 = A.mark()
    SC = 0.125
    wb, _ = load_weights(C, win, 0, NATT, "att")
    qA = [A.alloc([TT], BF16) for _ in range(2)]
    kA = A.alloc([TT], BF16)
    zs = A.alloc([TT], BF16)
    Vt = A.alloc([66, 65], BF16)
    rotT = A.alloc([64]); e65 = A.alloc([65]); mask = A.alloc([384], BF16); ident = A.alloc([128])
    sk = A.alloc([2]); vsink = A.alloc([2, 65], BF16); ksink = A.alloc([1], BF16)
    kmx = A.alloc([17]); negK = A.alloc([1])
    P.dma(lambda e: e.dma_start(out=rotT[0:64, :], in_=rotT_d), writes=["rotT"])
    P.dma(lambda e: e.dma_start(out=e65[0:64, :], in_=e65_d), writes=["e65"])
    P.dma(lambda e: e.dma_start(out=mask, in_=mask_d), writes=["mask"])
    P.dma(lambda e: e.dma_start(out=ident, in_=ident_d), writes=["ident"])
    P.dma(lambda e: e.dma_start(out=sk[0:1, :], in_=sink_d), writes=["sk"])
    P.pool(lambda e: e.memset(vsink[0:1, :, :], 0.0), writes=["vsink"])
    P.act(lambda e: e.activation(out=vsink[0:1, :, 64], in_=sk[0:1, :], func=AF.Exp), reads=["sk", "vsink"], writes=["vsink"])
    P.pool(lambda e: e.memset(ksink[0:64, :], 0.0), writes=["ksink"])
    P.pool(lambda e: e.memset(ksink[64:65, :], 1.0), reads=["ksink"], writes=["ksink"])
    P.pool(lambda e: e.memset(kA[64:65, :], 1.0), writes=["kA1"])
    P.pool(lambda e: e.memset(Vt[:, :, 64:65], 1.0), writes=["Vt1"])
    NB = 2
    qf = [A.alloc([512]) for _ in range(NB)]
    qsq = [A.alloc([512]) for _ in range(NB)]
    t1 = [A.alloc([512]) for _ in range(NB)]
    t2 = [A.alloc([512]) for _ in range(NB)]
    ct = [A.alloc([512]) for _ in range(2)]
    stt = [A.alloc([512]) for _ in range(2)]
    ev = dict(n=0)

    vf = [A.alloc([512]) for _ in range(2)]

    def evac(ti, t0, n, ci, ps, pskey):
        lat = ti < 16
        if ci == 3:
            P.act(lambda e: e.activation(out=zs[:, t0:t0 + n], in_=ps[:, 0:n], func=AF.Silu), reads=[pskey], writes=[("zs", ti)])
            return
        if ci == 4:
            vb_ = ti % 2
            v_ = vf[vb_]
            nb = n // 128
            P.dve(lambda e: e.tensor_copy(out=v_[0:64, 0:n], in_=ps[0:64, 0:n]), reads=[pskey], writes=[("vf", vb_)])
            tb_ = 5 + vb_
            tps = C.banks[tb_]
            for bl in range(nb):
                P.pe(lambda e, bl=bl: e.transpose(tps[:, bl * 64:(bl + 1) * 64], v_[0:64, bl * 128:(bl + 1) * 128], ident[0:64, 0:64]),
                     reads=[("vf", vb_), "ident"], writes=[("bank", tb_)])
            P.act(lambda e: e.copy(out=Vt[:, 4 * ti:4 * ti + nb, 0:64], in_=tps[:, 0:nb * 64].rearrange("p (a b) -> p a b", a=nb, b=64)),
                  reads=[("bank", tb_)], writes=[("Vt", ti)])
            return
        if ci == 0 and lat:
            cb = ti % 2
            P.dma(lambda e: e.dma_start(out=ct[cb][0:64, 0:n], in_=ropeC[:, t0:t0 + n]), writes=[("ct", cb)])
            P.dma(lambda e: e.dma_start(out=stt[cb][0:64, 0:n], in_=ropeS[:, t0:t0 + n]), writes=[("st", cb)])
        b = ev["n"] % NB
        ev["n"] += 1
        dst = qA[ci] if ci < 2 else kA
        dkey = ("qA", ci, ti) if ci < 2 else ("kA", ti)
        f_, s_, a_, b_ = qf[b], qsq[b], t1[b], t2[b]
        rb = 6 + (b % 2)
        rps = C.banks[rb]
        rkey = ("bank", rb)
        P.act(lambda e: e.activation(out=s_[0:64, 0:n], in_=ps[0:64, 0:n], func=AF.Square), reads=[pskey], writes=[("qsq", b)])
        if lat:
            P.act(lambda e: e.copy(out=f_[0:64, 0:n], in_=ps[0:64, 0:n]), reads=[pskey], writes=[("qf", b)])
            P.pe(lambda e: e.matmul(rps[0:64, 0:n], lhsT=rotT[0:64, :], rhs=f_[0:64, 0:n], start=True, stop=True),
                 reads=["rotT", ("qf", b)], writes=[rkey])
            cb = ti % 2
            P.dve(lambda e: e.tensor_tensor(out=a_[0:64, 0:n], in0=f_[0:64, 0:n], in1=ct[cb][0:64, 0:n], op=ALU.mult),
                  reads=[("qf", b), ("ct", cb)], writes=[("t1", b)])
            P.dve(lambda e: e.tensor_tensor(out=b_[0:64, 0:n], in0=rps[0:64, 0:n], in1=stt[cb][0:64, 0:n], op=ALU.mult),
                  reads=[rkey, ("st", cb)], writes=[("t2", b)])
            P.pool(lambda e: e.tensor_tensor(out=dst[0:64, t0:t0 + n], in0=a_[0:64, 0:n], in1=b_[0:64, 0:n], op=ALU.add),
                   reads=[("t1", b), ("t2", b)], writes=[dkey])
        else:
            P.act(lambda e: e.copy(out=dst[0:64, t0:t0 + n], in_=ps[0:64, 0:n]), reads=[pskey], writes=[dkey])
        sb_ = 4
        sps = C.banks[sb_]
        skey = ("bank", sb_)
        P.pe(lambda e: e.matmul(sps[0:65, 0:n], lhsT=e65[0:64, :], rhs=s_[0:64, 0:n], start=True, stop=True),
             reads=["e65", ("qsq", b)], writes=[skey])
        if ci < 2:
            P.act(lambda e: e.activation(out=dst[64:65, t0:t0 + n], in_=sps[64:65, 0:n], func=AF.Sqrt), reads=[skey], writes=[("qAn", ci, ti)])
        else:
            P.dve(lambda e: e.reduce_max(out=kmx[64:65, ti:ti + 1], in_=sps[64:65, 0:n], axis=AX.X), reads=[skey], writes=[("kmx", ti)])

    project(C, H, wb, "att", [(0, 64), (64, 64), (128, 64), (192, 128), (320, 64)], evac, banks=[0, 1, 2, 3])
    P.dve(lambda e: e.reduce_max(out=negK[64:65, :], in_=kmx[64:65, :], axis=AX.X), reads=[("kmx", ti) for ti in range(17)], writes=["negK"])
    P.act(lambda e: e.activation(out=negK[64:65, :], in_=negK[64:65, :], func=AF.Sqrt), reads=["negK"], writes=["negK"])
    P.dve(lambda e: e.tensor_scalar(out=negK[64:65, :], in0=negK[64:65, :], scalar1=-(1.0 + 2.0 ** -7), scalar2=None, op0=ALU.mult),
          reads=["negK"], writes=["negK"])
    for h in range(2):
        P.dve(lambda e, h=h: e.tensor_scalar(out=qA[h][64:65, :], in0=qA[h][64:65, :], scalar1=negK[64:65, 0:1], scalar2=None, op0=ALU.mult),
              reads=["negK"] + [("qAn", h, ti) for ti in range(17)], writes=[("qAn", h, ti) for ti in range(17)])
    PT = [[A.alloc([384], BF16) for _ in range(4)] for h in range(2)]
    PTc = [[A.alloc([2, 512], BF16) for _ in range(2)] for h in range(2)]
    PTs = [[A.alloc([512], BF16) for _ in range(2)] for h in range(2)]
    rden = [A.alloc([2]) for _ in range(2)]
    on = [A.alloc([4, 128]) for _ in range(2)]
    ob = [A.alloc([512], BF16) for _ in range(2)]
    stc = dict(n=0, c=0)

    def qkeys(h, u0, u1):
        r = []
        for ti in tiles_of(u0, u1):
            r += [("qA", h, ti), ("qAn", h, ti)]
        return r

    def ctx_scores(qt, q0, nq):
        b = qt % 2
        for h in range(2):
            for cc in range(2):
                bank = 3 + (stc["c"] % 2)
                stc["c"] += 1
                ps = C.banks[bank]
                P.pe(lambda e, ps=ps, h=h, cc=cc: e.matmul(ps[:, 0:nq], lhsT=kA[0:65, T + 128 * cc:T + 128 * cc + 128], rhs=qA[h][0:65, q0:q0 + nq],
                                                           start=True, stop=True),
                     reads=[("kA", 16), "kA1"] + qkeys(h, q0, q0 + nq), writes=[("bank", bank)])
                P.act(lambda e, ps=ps, h=h, cc=cc: e.activation(out=PTc[h][b][:, cc, 0:nq], in_=ps[:, 0:nq], func=AF.Exp, scale=SC),
                      reads=[("bank", bank)], writes=[("PTc", h, b, cc)])
            bank = 3 + (stc["c"] % 2)
            stc["c"] += 1
            ps = C.banks[bank]
            P.pe(lambda e, ps=ps, h=h: e.matmul(ps[0:1, 0:nq], lhsT=ksink[0:65, 0:1], rhs=qA[h][0:65, q0:q0 + nq], start=True, stop=True),
                 reads=["ksink"] + qkeys(h, q0, q0 + nq), writes=[("bank", bank)])
            P.act(lambda e, ps=ps, h=h: e.activation(out=PTs[h][b][0:1, 0:nq], in_=ps[0:1, 0:nq], func=AF.Exp, scale=SC),
                  reads=[("bank", bank)], writes=[("PTs", h, b)])

    def pv_block(blk, qt, qoff, locals_):
        b = qt % 2
        g = blk // 4
        gb = g % 2
        obank = 5 + (blk % 2)
        ops_ = C.banks[obank]
        for h in range(2):
            mm = []
            for (c, co) in locals_:
                mm.append((PT[h][c % 4][:, co:co + 128], Vt[:, c, :], [("PT", h, c % 4), ("Vt", c // 4), "Vt1"]))
            for cc in range(2):
                mm.append((PTc[h][b][:, cc, qoff:qoff + 128], Vt[:, 64 + cc, :], [("PTc", h, b, cc), ("Vt", 16), "Vt1"]))
            mm.append((PTs[h][b][0:1, qoff:qoff + 128], vsink[0:1, h, :], [("PTs", h, b), "vsink"]))
            for i, (l_, r_, rk) in enumerate(mm):
                P.pe(lambda e, l_=l_, r_=r_, i=i, h=h, last=(i == len(mm) - 1): e.matmul(ops_[:, h * 65:(h + 1) * 65], lhsT=l_, rhs=r_,
                                                                                         start=(i == 0), stop=last),
                     reads=rk, writes=[("bank", obank)])
        o3 = ops_[:, 0:130].rearrange("p (a b) -> p a b", a=2, b=65)
        rd = rden[blk % 2]
        P.dve(lambda e: e.reciprocal(out=rd, in_=o3[:, :, 64]), reads=[("bank", obank)], writes=[("rden", blk % 2)])
        for h in range(2):
            P.dve(lambda e, h=h: e.tensor_scalar(out=on[gb][:, blk % 4, h * 64:(h + 1) * 64], in0=o3[:, h, 0:64], scalar1=rd[:, h:h + 1],
                                                 scalar2=None, op0=ALU.mult),
                  reads=[("bank", obank), ("rden", blk % 2)], writes=[("on", gb, blk % 4)])

    def flush_group(g, nblk):
        gb = g % 2
        tps = C.banks[7]
        for i in range(nblk):
            P.pe(lambda e, i=i: e.transpose(tps[:, i * 128:(i + 1) * 128], on[gb][:, i, :], ident), reads=[("on", gb, i), "ident"], writes=[("bank", 7)])
        t0 = 512 * g
        n = 128 * nblk
        o_ = ob[gb]
        P.dve(lambda e: e.tensor_tensor(out=o_[:, 0:n], in0=tps[:, 0:n], in1=zs[:, t0:t0 + n], op=ALU.mult),
              reads=[("bank", 7), ("zs", g)], writes=[("ob", gb)])
        P.dma(lambda e: e.dma_start(out=mixT[0:128, t0:t0 + n], in_=o_[:, 0:n]), reads=[("ob", gb)], writes=[("mixT", "att", g)])

    def do_pv(n):
        locs = [(c, 128 * (n - c + 1)) for c in (n - 1, n, n + 1) if 0 <= c < 64]
        pv_block(n, n // 4, 128 * (n % 4), locs)
        if n % 4 == 3:
            flush_group(n // 4, 4)

    for c in range(64):
        if c % 4 == 0:
            ctx_scores(c // 4, 512 * (c // 4), 512)
        u0 = max(0, 128 * (c - 1))
        u1 = min(T, 128 * (c + 2))
        n = u1 - u0
        mo = u0 - 128 * (c - 1)
        for h in range(2):
            bank = stc["n"] % 3
            stc["n"] += 1
            ps = C.banks[bank]
            pt = PT[h][c % 4]
            P.pe(lambda e, ps=ps, h=h, c=c, u0=u0, u1=u1, n=n: e.matmul(ps[:, 0:n], lhsT=kA[0:65, 128 * c:128 * c + 128], rhs=qA[h][0:65, u0:u1],
                                                                        start=True, stop=True),
                 reads=[("kA", c // 4), "kA1"] + qkeys(h, u0, u1), writes=[("bank", bank)])
            P.act(lambda e, ps=ps, pt=pt, n=n, mo=mo: e.activation(out=pt[:, mo:mo + n], in_=ps[:, 0:n], func=AF.Exp, scale=SC),
                  reads=[("bank", bank)], writes=[("PT", h, c % 4)])
            P.pool(lambda e, pt=pt, n=n, mo=mo: e.tensor_tensor(out=pt[:, mo:mo + n], in0=pt[:, mo:mo + n], in1=mask[:, mo:mo + n], op=ALU.mult),
                   reads=[("PT", h, c % 4), "mask"], writes=[("PT", h, c % 4)])
        if c >= 2:
            do_pv(c - 2)
    do_pv(62)
    do_pv(63)
    if want_ctx:
        ctx_scores(16, T, 256)
        pv_block(64, 16, 0, [])
        pv_block(65, 16, 128, [])
        flush_group(16, 2)
    A.reset(m0)


def att_consts():
    f = np.float32
    t = np.arange(T)
    row = (t // 64).astype(np.float64)
    col = (t % 64).astype(np.float64)
    inv = (10000.0 ** (-np.arange(16, dtype=np.float32) / 16)).astype(np.float32).astype(np.float64)
    Cc = np.zeros((64, T), f); Ss = np.zeros((64, T), f)
    for axis, pos in enumerate((row, col)):
        ang = (pos[None, :].astype(np.float32) * inv[:, None].astype(np.float32)).astype(np.float32)
        for half in range(2):
            sl = slice(axis * 32 + half * 16, axis * 32 + half * 16 + 16)
            Cc[sl] = np.cos(ang); Ss[sl] = np.sin(ang)
    rotT = np.zeros((64, 64), f)
    for axis in range(2):
        for fq in range(16):
            a = axis * 32 + fq; b_ = axis * 32 + 16 + fq
            rotT[b_, a] = -1.0
            rotT[a, b_] = 1.0
    e65 = np.zeros((64, 65), f); e65[:, 64] = 1.0
    r = np.arange(128)[:, None]; u = np.arange(384)[None, :]
    mask = (np.abs(u - 128 - r) <= 128).astype(ml_dtypes.bfloat16)
    ident = np.eye(128, dtype=f)
    return dict(ropeC=Cc, ropeS=Ss, rotT=rotT, e65=e65, mask=mask, ident=ident)


def prep_att(inp, l, j):
    d = att_consts()
    d["sink"] = np.ascontiguousarray(inp["att_sink"][l][2 * j:2 * j + 2].reshape(1, 2).astype(np.float32))
    return d


def merge_gens(gens):
    gens = [g for g in gens if g is not None]
    while gens:
        nxt = []
        for g in gens:
            try:
                next(g)
                nxt.append(g)
            except StopIteration:
                pass
        gens = nxt


def gdn_consts():
    f = np.float32
    p = np.arange(GCH)[:, None]; q = np.arange(GCH)[None, :]
    bd = (p // 64) == (q // 64)
    m = np.stack([(p < q), (p <= q), (p > q), (p >= q), bd, ~bd], axis=1).astype(f)
    return dict(gmask=np.ascontiguousarray(m), ident=np.eye(128, dtype=f))


def prep_gdn(inp, l, j):
    f = np.float32
    d = gdn_consts()
    cw = np.zeros((128, 3, 4), f)
    for sec in range(3):
        cw[:, sec, :] = inp["gdn_conv"][l][:, sec * 512 + 128 * j: sec * 512 + 128 * j + 128].T
    d["gdn_cw"] = cw
    d["gdn_par"] = np.array([[inp["gdn_a_log"][l][0][j], inp["gdn_a_log"][l][1][j],
                              inp["gdn_dt_bias"][l][0][j], inp["gdn_dt_bias"][l][1][j]]], f)
    d["gdn_nw"] = np.ascontiguousarray(inp["gdn_norm"][l].reshape(128, 1).astype(f))
    return d


def phase_gdn(C, H, win, gdn_cw, gdn_par, gdn_nw, gmask_d, ident_d, mixT, scr, stop=99, grow0=128, after_tile=None):
    P, A, nc = C.P, C.A, C.nc
    m0 = A.mark()
    raw, prc, zsD, baD, rowsD, glD, OD = scr["raw"], scr["prc"], scr["zsD"], scr["ba"], scr["rows"], scr["gl"], scr["OD"]
    prcb = scr["prcb"]
    ident = A.alloc([128]); gmask = A.alloc([6, GCH]); cw = A.alloc([3, 4]); par = A.alloc([4]); nw = A.alloc([1])
    onesf = A.alloc([128]); negA = A.alloc([2]); epsb = A.alloc([1])
    P.dma(lambda e: e.dma_start(out=ident, in_=ident_d), writes=["ident"])
    P.dma(lambda e: e.dma_start(out=gmask[0:GCH], in_=gmask_d), writes=["gmask"])
    P.dma(lambda e: e.dma_start(out=cw, in_=gdn_cw), writes=["gcw"])
    P.dma(lambda e: e.dma_start(out=par, in_=gdn_par[0, :].partition_broadcast(128)), writes=["par"])
    P.dma(lambda e: e.dma_start(out=nw, in_=gdn_nw), writes=["nw"])
    P.pool(lambda e: e.memset(onesf, 1.0), writes=["onesf"])
    P.pool(lambda e: e.memset(epsb, EPS), writes=["epsb"])
    P.act(lambda e: e.activation(out=negA, in_=par[:, 0:2], func=AF.Exp), reads=["par"], writes=["negA"])
    P.dve(lambda e: e.tensor_scalar(out=negA, in0=negA, scalar1=-1.0, scalar2=None, op0=ALU.mult), reads=["negA"], writes=["negA"])
    mk2 = A.mark()
    wb, _ = load_weights(C, win, NATT, NGDN, "gdn")
    stz = [A.alloc([512], BF16) for _ in range(2)]
    stb = [A.alloc([512]) for _ in range(2)]
    RW = TT + 8
    rawS = A.alloc([3, RW])
    def col(t):
        return t + 2 if t < T else t + 5
    for (c0_, c1_) in ((0, 2), (T + 2, T + 5), (TT + 5, TT + 8)):
        P.pool(lambda e, c0_=c0_, c1_=c1_: e.memset(rawS[:, :, c0_:c1_], 0.0), writes=[("rawpad", c0_)])
    cv_ = [A.alloc([3, 512]) for _ in range(2)]
    sq_ = [A.alloc([2, 512]) for _ in range(2)]
    rs_ = [A.alloc([2, 512]) for _ in range(2)]
    cvb_ = [A.alloc([2, 512], BF16) for _ in range(2)]
    ev = dict(n=0, z=0, b=0)

    def evac(ti, t0, n, ci, ps, pskey):
        if ci < 3:
            cc_ = col(t0)
            if ci != 1:
                P.act(lambda e: e.copy(out=rawS[:, ci, cc_:cc_ + n], in_=ps[:, 0:n]), reads=[pskey], writes=[("raws", ti, ci)])
            else:
                P.dve(lambda e: e.tensor_copy(out=rawS[:, ci, cc_:cc_ + n], in_=ps[:, 0:n]), reads=[pskey], writes=[("raws", ti, ci)])
        elif ci == 3:
            b = ev["z"] % 2
            ev["z"] += 1
            s_ = stz[b]
            P.act(lambda e: e.activation(out=s_[:, 0:n], in_=ps[:, 0:n], func=AF.Silu), reads=[pskey], writes=[("stz", b)])
            P.dma(lambda e: e.dma_start(out=zsD[:, t0:t0 + n], in_=s_[:, 0:n]), reads=[("stz", b)], writes=[("zsD", ti)])
        else:
            b = ev["b"] % 2
            ev["b"] += 1
            s_ = stb[b]
            P.dve(lambda e: e.tensor_copy(out=s_[0:4, 0:n], in_=ps[0:4, 0:n]), reads=[pskey], writes=[("stb", b)])
            P.dma(lambda e: e.dma_start(out=baD[:, t0:t0 + n], in_=s_[0:4, 0:n]), reads=[("stb", b)], writes=[("baD", ti)])

    def stage2(ti):
        t0, n = TILES[ti]
        b = ti % 2
        c_, s_, r_ = cv_[b], sq_[b], rs_[b]
        cb = col(t0) - 2
        nb_ = [tj for tj in (ti - 1, ti, ti + 1) if 0 <= tj <= 16 and (tj < 16) == (ti < 16)]
        for sec in range(3):
            rk = [("raws", tj, sec) for tj in nb_] + [("rawpad", 0), ("rawpad", T + 2), ("rawpad", TT + 5), "gcw"]
            P.dve(lambda e, c_=c_, sec=sec: e.tensor_scalar(out=c_[:, sec, 0:n], in0=rawS[:, sec, cb + 2:cb + 2 + n], scalar1=cw[:, sec, 2:3], scalar2=None,
                                                           op0=ALU.mult), reads=rk, writes=[("gcv", b, sec)])
            for k in (0, 1, 3):
                P.dve(lambda e, c_=c_, sec=sec, k=k: e.scalar_tensor_tensor(out=c_[:, sec, 0:n], in0=rawS[:, sec, cb + k:cb + k + n], scalar=cw[:, sec, k:k + 1],
                                                                         in1=c_[:, sec, 0:n], op0=ALU.mult, op1=ALU.add),
                      reads=rk + [("gcv", b, sec)], writes=[("gcv", b, sec)])
        P.act(lambda e: e.activation(out=c_[:, :, 0:n], in_=c_[:, :, 0:n], func=AF.Silu),
              reads=[("gcv", b, s) for s in range(3)], writes=[("gcv", b, s) for s in range(3)])
        P.act(lambda e: e.activation(out=s_[:, :, 0:n], in_=c_[:, 0:2, 0:n], func=AF.Square),
              reads=[("gcv", b, 0), ("gcv", b, 1)], writes=[("gsq", b)])
        for sec in range(2):
            bank = 6 + sec
            ps = C.banks[bank]
            P.pe(lambda e, ps=ps, sec=sec: e.matmul(ps[:, 0:n], lhsT=onesf, rhs=s_[:, sec, 0:n], start=True, stop=True),
                 reads=[("gsq", b), "onesf"], writes=[("bank", bank)])
            P.act(lambda e, ps=ps, sec=sec: e.activation(out=r_[:, sec, 0:n], in_=ps[:, 0:n], func=AF.Ln, bias=epsb[:, 0:1]),
                  reads=[("bank", bank), "epsb"], writes=[("grs", b, sec)])
            P.act(lambda e, sec=sec: e.activation(out=r_[:, sec, 0:n], in_=r_[:, sec, 0:n], func=AF.Exp, scale=-0.5),
                  reads=[("grs", b, sec)], writes=[("grs", b, sec)])
            sc_ = (128.0 ** -0.5) if sec == 0 else 1.0
            P.dve(lambda e, sec=sec, sc_=sc_: e.scalar_tensor_tensor(out=c_[:, sec, 0:n], in0=c_[:, sec, 0:n], scalar=sc_, in1=r_[:, sec, 0:n],
                                                                  op0=ALU.mult, op1=ALU.mult),
                  reads=[("gcv", b, sec), ("grs", b, sec)], writes=[("gcv", b, sec)])
        P.dma(lambda e: e.dma_start(out=prc[:, :, t0:t0 + n].rearrange("s p t -> p s t"), in_=c_[:, :, 0:n]),
              reads=[("gcv", b, s) for s in range(3)], writes=[("prc", ti)])
        cb2 = cvb_[b]
        P.pool(lambda e: e.tensor_copy(out=cb2[:, :, 0:n], in_=c_[:, 0:2, 0:n]), reads=[("gcv", b, 0), ("gcv", b, 1)], writes=[("gcvb", b)])
        P.dma(lambda e: e.dma_start(out=prcb[:, :, t0:t0 + n].rearrange("s p t -> p s t"), in_=cb2[:, :, 0:n]),
              reads=[("gcvb", b)], writes=[("prcb", ti)])

    def hook(ti):
        if 1 <= ti <= 15:
            stage2(ti - 1)
        if ti == 15:
            stage2(15)
        if ti == 16:
            stage2(16)

    project(C, H, wb, "gdn", [(0, 128), (128, 128), (256, 128), (384, 128), (512, 4)], evac, banks=[0, 1, 2, 3, 4, 5], after_tile=hook)
    A.reset(mk2)
    P.barrier()
    if stop <= 2:
        return
    CH = GCH
    NG = 512 // CH
    NLAT = T // CH
    NCTX = L // CH
    NCH = NLAT + NCTX
    NLEV = 5
    colT = [[A.alloc([NCH]) for _ in range(4)] for d in range(2)]
    glb = [A.alloc([NCH]) for d in range(2)]
    mk3 = A.mark()
    segs = [(0, NLAT, 0), (NLAT, NCTX, 1)]
    for (ch0, npp, si) in segs:
        X = A.alloc([4, CH])
        P.dma(lambda e, X=X, ch0=ch0, npp=npp: e.dma_start(out=X[0:npp], in_=baD[:, CH * ch0:CH * (ch0 + npp)].rearrange("r (c i) -> c r i", i=CH)),
              reads=[("baD", x) for x in range(17)], writes=[("X", si)])
        for d in range(2):
            bet = A.alloc([CH]); xx = A.alloc([CH]); tt = A.alloc([CH]); G = A.alloc([CH]); q2 = A.alloc([CH]); q3 = A.alloc([CH]); gl = A.alloc([1])
            kx = ("sc", si, d)
            P.act(lambda e, bet=bet, X=X, d=d, npp=npp: e.activation(out=bet[0:npp], in_=X[0:npp, d, :], func=AF.Sigmoid), reads=[("X", si)], writes=[kx + ("bet",)])
            P.act(lambda e, tt=tt, X=X, d=d, npp=npp: e.activation(out=tt[0:npp], in_=X[0:npp, 2 + d, :], func=AF.Abs, bias=par[0:npp, 2 + d:3 + d]),
                  reads=[("X", si), "par"], writes=[kx + ("tt",)])
            P.act(lambda e, tt=tt, npp=npp: e.activation(out=tt[0:npp], in_=tt[0:npp], func=AF.Exp, scale=-1.0), reads=[kx + ("tt",)], writes=[kx + ("tt",)])
            P.act(lambda e, tt=tt, npp=npp: e.activation(out=tt[0:npp], in_=tt[0:npp], func=AF.Ln, bias=1.0), reads=[kx + ("tt",)], writes=[kx + ("tt",)])
            P.dve(lambda e, xx=xx, X=X, d=d, npp=npp: e.tensor_scalar(out=xx[0:npp], in0=X[0:npp, 2 + d, :], scalar1=par[0:npp, 2 + d:3 + d], scalar2=0.0,
                                                                     op0=ALU.add, op1=ALU.max), reads=[("X", si), "par"], writes=[kx + ("xx",)])
            P.dve(lambda e, xx=xx, tt=tt, npp=npp: e.tensor_tensor(out=xx[0:npp], in0=xx[0:npp], in1=tt[0:npp], op=ALU.add),
                  reads=[kx + ("xx",), kx + ("tt",)], writes=[kx + ("xx",)])
            P.dve(lambda e, xx=xx, d=d, npp=npp: e.tensor_scalar(out=xx[0:npp], in0=xx[0:npp], scalar1=negA[0:npp, d:d + 1], scalar2=None, op0=ALU.mult),
                  reads=[kx + ("xx",), "negA"], writes=[kx + ("xx",)])
            if d == 0:
                P.dve(lambda e, G=G, xx=xx, npp=npp: e.tensor_tensor_scan(out=G[0:npp], data0=onesf[0:npp, 0:CH], data1=xx[0:npp], initial=0.0, op0=ALU.mult, op1=ALU.add),
                      reads=[kx + ("xx",), "onesf"], writes=[kx + ("G",)])
                gle = G[0:npp, CH - 1:CH]
            else:
                P.dve(lambda e, G=G, xx=xx, npp=npp: e.tensor_tensor_scan(out=revap(G[0:npp]), data0=onesf[0:npp, 0:CH], data1=revap(xx[0:npp]), initial=0.0,
                                                                          op0=ALU.mult, op1=ALU.add),
                      reads=[kx + ("xx",), "onesf"], writes=[kx + ("G",)])
                gle = G[0:npp, 0:1]
            P.act(lambda e, q2=q2, G=G, npp=npp: e.activation(out=q2[0:npp], in_=G[0:npp], func=AF.Exp), reads=[kx + ("G",)], writes=[kx + ("q2",)])
            P.dve(lambda e, q2=q2, bet=bet, npp=npp: e.tensor_tensor(out=q2[0:npp], in0=q2[0:npp], in1=bet[0:npp], op=ALU.mult),
                  reads=[kx + ("q2",), kx + ("bet",)], writes=[kx + ("q2",)])
            P.act(lambda e, q3=q3, G=G, gle=gle, npp=npp: e.activation(out=q3[0:npp], in_=G[0:npp], func=AF.Exp, scale=-1.0, bias=gle),
                  reads=[kx + ("G",)], writes=[kx + ("q3",)])
            P.act(lambda e, gl=gl, gle=gle, npp=npp: e.activation(out=gl[0:npp], in_=gle, func=AF.Exp), reads=[kx + ("G",)], writes=[kx + ("gl",)])
            P.dma(lambda e, G=G, d=d, ch0=ch0, npp=npp: e.dma_start(out=rowsD[d, 0, CH * ch0:CH * (ch0 + npp)].rearrange("(c i) -> c i", i=CH), in_=G[0:npp]),
                  reads=[kx + ("G",)], writes=[("rowsD", d, si, 0)])
            P.dma(lambda e, bet=bet, d=d, ch0=ch0, npp=npp: e.dma_start(out=rowsD[d, 1, CH * ch0:CH * (ch0 + npp)].rearrange("(c i) -> c i", i=CH), in_=bet[0:npp]),
                  reads=[kx + ("bet",)], writes=[("rowsD", d, si, 1)])
            P.dma(lambda e, gl=gl, d=d, ch0=ch0, npp=npp: e.dma_start(out=glD[d, ch0:ch0 + npp].rearrange("(c i) -> c i", i=1), in_=gl[0:npp]),
                  reads=[kx + ("gl",)], writes=[("glD", d, si)])
            for qi, src in enumerate((G, bet, q2, q3)):
                bank = qi
                ps = C.banks[bank]
                nm = ("G", "bet", "q2", "q3")[qi]
                P.pe(lambda e, ps=ps, src=src, npp=npp: e.transpose(ps[0:CH, 0:npp], src[0:npp, :], ident[0:npp, 0:npp]), reads=[kx + (nm,), "ident"], writes=[("bank", bank)])
                P.act(lambda e, ps=ps, d=d, qi=qi, ch0=ch0, npp=npp: e.copy(out=colT[d][qi][0:CH, ch0:ch0 + npp], in_=ps[0:CH, 0:npp]),
                      reads=[("bank", bank)], writes=[("colT", d, qi, si)])
    for d in range(2):
        P.dma(lambda e, d=d: e.dma_start(out=glb[d], in_=glD[d, 0:NCH].partition_broadcast(128)), reads=[("glD", d, 0), ("glD", d, 1)], writes=[("glb", d)])
    P.barrier()
    A.reset(mk3)
    if stop <= 3:
        return
    NBUF = 2
    def gb(shape):
        return [[A.alloc(shape) for _ in range(NBUF)] for d in range(2)]
    WTs = gb([NG, CH]); Us = gb([NG, 128]); KDs = gb([NG, 128]); QGs = gb([NG, CH]); AQs = gb([NG, CH]); OGs = None
    S = [[A.alloc([128]) for _ in range(2)] for d in range(2)]
    Up = [[A.alloc([128]) for _ in range(2)] for d in range(2)]
    qkv_t = [A.alloc([3, 512]) for _ in range(2)]
    Grow = [A.alloc([512]) for _ in range(2)]
    Brow = [A.alloc([512]) for _ in range(2)]
    E1_ = [A.alloc([NG, CH]) for _ in range(2)]; DA_ = [A.alloc([NG, CH]) for _ in range(2)]
    DN_ = [A.alloc([NG, CH]) for _ in range(2)]; DL_ = [A.alloc([NG, CH]) for _ in range(2)]
    Nm_ = [[A.alloc([NG, CH], BF16) for _ in range(2)] for _ in range(2)]; Lm_ = [[A.alloc([NG, CH], BF16) for _ in range(2)] for _ in range(2)]
    Xm_ = [[A.alloc([NG, CH], BF16) for _ in range(2)] for _ in range(2)]
    Kbg_ = [A.alloc([NG, 128], BF16) for _ in range(2)]; Vb_ = [A.alloc([NG, 128], BF16) for _ in range(2)]
    qkb_t = [A.alloc([2, 512], BF16) for _ in range(2)]
    OTs = A.alloc([TT])
    otw = set()
    identb = A.alloc([CH], BF16)
    P.pool(lambda e: e.tensor_copy(out=identb[0:CH, :], in_=ident[0:CH, 0:CH]), reads=["ident"], writes=["identb"])
    nmask = A.alloc([4, CH])
    P.dve(lambda e: e.tensor_scalar(out=nmask[0:CH], in0=gmask[0:CH, 0:4, :], scalar1=-1.0, scalar2=30000.0, op0=ALU.add, op1=ALU.mult), reads=["gmask"], writes=["nmask"])
    bdm = A.alloc([2, CH], BF16)
    P.pool(lambda e: e.tensor_copy(out=bdm[0:CH], in_=gmask[0:CH, 4:6, :]), reads=["gmask"], writes=["bdm"])
    Noff_ = [A.alloc([NG, CH], BF16) for _ in range(2)]; Wd_ = [A.alloc([NG, 128], BF16) for _ in range(2)]; Ud_ = [A.alloc([NG, 128], BF16) for _ in range(2)]
    pc = dict(n=0)

    def chunk_list(d, grp):
        if grp == 16:
            cs = list(range(NLAT, NLAT + NCTX))
        else:
            cs = list(range(NG * grp, NG * grp + NG))
        return cs if d == 0 else cs[::-1]

    def pre(d, grp, buf):
        cs = sorted(chunk_list(d, grp))
        c0 = cs[0]; ncn = len(cs); n = CH * ncn; tok0 = CH * c0
        pb = d
        qt = qkv_t[pb]; gr = Grow[pb]; br = Brow[pb]; qb = qkb_t[pb]
        E1 = E1_[d]; DA = DA_[d]; DN = DN_[d]; DL = DL_[d]; Nm = Nm_[d]; Lm = Lm_[d]; Xm = Xm_[d]; Kbg = Kbg_[d]; Vb = Vb_[d]
        Noff = Noff_[d]; Wd = Wd_[d]; Ud = Ud_[d]
        BA, BB = 2 * d, 2 * d + 1
        bankA, bankB = C.banks[BA], C.banks[BB]
        K = lambda *a: ("pre", pb) + a
        P.dma(lambda e: e.dma_start(out=qt[:, :, 0:n], in_=prc[:, :, tok0:tok0 + n].rearrange("s p t -> p s t")),
              reads=[("prc", x) for x in range(17)], writes=[K("qkv")])
        P.dma(lambda e: e.dma_start(out=qb[:, :, 0:n], in_=prcb[:, :, tok0:tok0 + n].rearrange("s p t -> p s t")),
              reads=[("prcb", x) for x in range(17)], writes=[K("qkb")])
        P.dma(lambda e: e.dma_start(out=gr[:, 0:n], in_=rowsD[d, 0, tok0:tok0 + n].partition_broadcast(128)),
              reads=[("rowsD", d, s_, 0) for s_ in range(2)], writes=[K("gr")])
        P.dma(lambda e: e.dma_start(out=br[0:CH, 0:n], in_=rowsD[d, 1, tok0:tok0 + n].partition_broadcast(CH)),
              reads=[("rowsD", d, s_, 1) for s_ in range(2)], writes=[K("br")])
        yield
        v3 = lambda t: t[0:CH, 0:ncn, :]
        gr3 = gr[0:CH, 0:n].rearrange("p (c i) -> p c i", i=CH)
        br3 = br[0:CH, 0:n].rearrange("p (c i) -> p c i", i=CH)
        def colb(qi, w):
            return colT[d][qi][0:CH, c0:c0 + ncn].unsqueeze(2).to_broadcast([CH, ncn, w])
        ck = [("colT", d, qi, s_) for qi in range(4) for s_ in range(2)]
        mstr = 0 if d == 0 else 2
        mstrT = 2 if d == 0 else 0
        def nmb(i):
            return nmask[0:CH, i, :].unsqueeze(1).to_broadcast([CH, ncn, CH])
        P.pool(lambda e: e.tensor_tensor(out=v3(E1), in0=gr3, in1=colb(0, CH), op=ALU.subtract), reads=[K("gr")] + ck, writes=[("E1", d)])
        P.dve(lambda e: e.scalar_tensor_tensor(out=v3(DA), in0=v3(E1), scalar=0.0, in1=nmb(mstr + 1), op0=ALU.min, op1=ALU.add), reads=[("E1", d), "nmask"], writes=[("DA", d)])
        P.dve(lambda e: e.scalar_tensor_tensor(out=v3(DN), in0=v3(E1), scalar=0.0, in1=nmb(mstr), op0=ALU.min, op1=ALU.add), reads=[("E1", d), "nmask"], writes=[("DN", d)])
        P.dve(lambda e: e.scalar_tensor_tensor(out=v3(DL), in0=v3(E1), scalar=0.0, in1=nmb(mstrT), op0=ALU.max, op1=ALU.subtract), reads=[("E1", d), "nmask"], writes=[("DL", d)])
        P.act(lambda e: e.activation(out=v3(DA), in_=v3(DA), func=AF.Exp), reads=[("DA", d)], writes=[("DA", d)])
        P.act(lambda e: e.activation(out=v3(DN), in_=v3(DN), func=AF.Exp), reads=[("DN", d)], writes=[("DN", d)])
        P.act(lambda e: e.activation(out=v3(DL), in_=v3(DL), func=AF.Exp, scale=-1.0), reads=[("DL", d)], writes=[("DL", d)])
        P.act(lambda e: e.activation(out=gr[:, 0:n], in_=gr[:, 0:n], func=AF.Exp), reads=[K("gr"), ("E1", d)], writes=[K("gr")])
        P.pool(lambda e: e.tensor_tensor(out=QGs[d][buf][:, 0:ncn, :], in0=qt[:, 0, 0:n].rearrange("p (c i) -> p c i", i=CH),
                                         in1=gr[:, 0:n].rearrange("p (c i) -> p c i", i=CH), op=ALU.mult),
               reads=[K("qkv"), K("gr")], writes=[("QG", d, buf)])
        P.pool(lambda e: e.tensor_tensor(out=v3(DN), in0=v3(DN), in1=br3, op=ALU.mult), reads=[("DN", d), K("br")], writes=[("DN", d)])
        P.pool(lambda e: e.tensor_tensor(out=v3(DL), in0=v3(DL), in1=colb(1, CH), op=ALU.mult), reads=[("DL", d)] + ck, writes=[("DL", d)])
        yield
        for i in range(ncn):
            ks = qb[:, 1, CH * i:CH * i + CH]
            P.pe(lambda e, i=i, ks=ks: e.matmul(bankA[0:CH, CH * i:CH * i + CH], lhsT=ks, rhs=ks, start=True, stop=True), reads=[K("qkb")], writes=[("bank", BA)])
        for i in range(ncn):
            ks = qb[:, 1, CH * i:CH * i + CH]; qs = qb[:, 0, CH * i:CH * i + CH]
            P.pe(lambda e, i=i, ks=ks, qs=qs: e.matmul(bankB[0:CH, CH * i:CH * i + CH], lhsT=ks, rhs=qs, start=True, stop=True), reads=[K("qkb")], writes=[("bank", BB)])
        b03 = bankA[0:CH, 0:n].rearrange("p (c i) -> p c i", i=CH)
        b13 = bankB[0:CH, 0:n].rearrange("p (c i) -> p c i", i=CH)
        P.dve(lambda e: e.tensor_tensor(out=v3(Nm[0]), in0=b03, in1=v3(DN), op=ALU.mult), reads=[("bank", BA), ("DN", d)], writes=[("Nm", d, 0)])
        P.dve(lambda e: e.tensor_tensor(out=v3(Lm[0]), in0=b03, in1=v3(DL), op=ALU.mult), reads=[("bank", BA), ("DL", d)], writes=[("Lm", d, 0)])
        P.dve(lambda e: e.tensor_tensor(out=AQs[d][buf][0:CH, 0:ncn, :], in0=b13, in1=v3(DA), op=ALU.mult), reads=[("bank", BB), ("DA", d)], writes=[("AQ", d, buf)])
        def bdb(i):
            return bdm[0:CH, i, :].unsqueeze(1).to_broadcast([CH, ncn, CH])
        P.pool(lambda e: e.tensor_tensor(out=v3(Noff), in0=v3(Nm[0]), in1=bdb(1), op=ALU.mult), reads=[("Nm", d, 0), "bdm"], writes=[("Noff", d)])
        P.pool(lambda e: e.tensor_tensor(out=v3(Nm[0]), in0=v3(Nm[0]), in1=bdb(0), op=ALU.mult), reads=[("Nm", d, 0), "bdm", ("Noff", d)], writes=[("Nm", d, 0)])
        P.pool(lambda e: e.tensor_tensor(out=v3(Lm[0]), in0=v3(Lm[0]), in1=bdb(0), op=ALU.mult), reads=[("Lm", d, 0), "bdm"], writes=[("Lm", d, 0)])
        P.pool(lambda e: e.tensor_tensor(out=v3(Xm[0]), in0=identb[0:CH, 0:CH].unsqueeze(1).to_broadcast([CH, ncn, CH]), in1=v3(Nm[0]), op=ALU.subtract),
               reads=[("Nm", d, 0), "identb"], writes=[("Xm", d, 0)])
        yield
        for (src_sec, bank, dst, qi, nm) in ((1, 2, Kbg, 2, ("Kbg", d)), (2, 3, Vb, 1, ("Vb", d))):
            ps = C.banks[bank]
            for i in range(ncn):
                P.pe(lambda e, ps=ps, i=i, src_sec=src_sec: e.transpose(ps[0:CH, i * 128:(i + 1) * 128], qt[:, src_sec, CH * i:CH * i + CH], ident),
                     reads=[K("qkv"), "ident"], writes=[("bank", bank)])
            ps3 = ps[0:CH, 0:128 * ncn].rearrange("p (c i) -> p c i", i=128)
            cb_ = colT[d][qi][0:CH, c0:c0 + ncn].unsqueeze(2).to_broadcast([CH, ncn, 128])
            P.dve(lambda e, ps3=ps3, dst=dst, cb_=cb_: e.tensor_tensor(out=dst[0:CH, 0:ncn, :], in0=ps3, in1=cb_, op=ALU.mult),
                  reads=[("bank", bank)] + ck, writes=[(nm, d)])
            if src_sec == 1:
                cb2 = colT[d][3][0:CH, c0:c0 + ncn].unsqueeze(2).to_broadcast([CH, ncn, 128])
                P.dve(lambda e, ps3=ps3, cb2=cb2: e.tensor_tensor(out=KDs[d][buf][0:CH, 0:ncn, :], in0=ps3, in1=cb2, op=ALU.mult),
                      reads=[("bank", bank)] + ck, writes=[("KD", d, buf)])
        yield
        cur = 0
        for lev in range(1, NLEV + 1):
            nxt = 1 - cur
            for i in range(ncn):
                P.pe(lambda e, i=i, cur=cur: e.matmul(bankA[0:CH, CH * i:CH * i + CH], lhsT=Nm[cur][0:CH, i, :], rhs=Lm[cur][0:CH, i, :], start=True, stop=True),
                     reads=[("Nm", d, cur), ("Lm", d, cur)], writes=[("bank", BA)])
            if lev < NLEV:
                for i in range(ncn):
                    P.pe(lambda e, i=i, cur=cur: e.matmul(bankB[0:CH, CH * i:CH * i + CH], lhsT=Lm[cur][0:CH, i, :], rhs=Nm[cur][0:CH, i, :], start=True, stop=True),
                         reads=[("Nm", d, cur), ("Lm", d, cur)], writes=[("bank", BB)])
            b43 = bankA[0:CH, 0:n].rearrange("p (c i) -> p c i", i=CH)
            b53 = bankB[0:CH, 0:n].rearrange("p (c i) -> p c i", i=CH)
            P.act(lambda e, nxt=nxt, b43=b43: e.copy(out=v3(Lm[nxt]), in_=b43), reads=[("bank", BA)], writes=[("Lm", d, nxt)])
            if lev < NLEV:
                P.act(lambda e, nxt=nxt, b53=b53: e.copy(out=v3(Nm[nxt]), in_=b53), reads=[("bank", BB)], writes=[("Nm", d, nxt)])
            yield
            for i in range(ncn):
                P.pe(lambda e, i=i, nxt=nxt, cur=cur: e.matmul(bankA[0:CH, CH * i:CH * i + CH], lhsT=Lm[nxt][0:CH, i, :], rhs=Xm[cur][0:CH, i, :], start=True, stop=True),
                     reads=[("Lm", d, nxt), ("Xm", d, cur)], writes=[("bank", BA)])
            P.dve(lambda e, nxt=nxt, cur=cur: e.tensor_tensor(out=v3(Xm[nxt]), in0=b03, in1=v3(Xm[cur]), op=ALU.add), reads=[("bank", BA), ("Xm", d, cur)], writes=[("Xm", d, nxt)])
            cur = nxt
            yield
        Xf = Xm[cur]
        for (src, mid, nm, bnk) in ((Kbg, Wd, ("Kbg", d), 0), (Vb, Ud, ("Vb", d), 3)):
            psx = C.banks[bnk]
            for i in range(ncn):
                P.pe(lambda e, psx=psx, i=i, src=src: e.matmul(psx[0:CH, i * 128:(i + 1) * 128], lhsT=Xf[0:CH, i, :], rhs=src[0:CH, i, :], start=True, stop=True),
                     reads=[(nm, d), ("Xm", d, cur)], writes=[("bank", bnk)])
            psx3 = psx[0:CH, 0:128 * ncn].rearrange("p (c i) -> p c i", i=128)
            P.act(lambda e, psx3=psx3, mid=mid: e.copy(out=mid[0:CH, 0:ncn, :], in_=psx3), reads=[("bank", bnk)], writes=[(nm, d, "mid")])
            for i in range(ncn):
                P.pe(lambda e, psx=psx, i=i, mid=mid: e.matmul(psx[0:CH, i * 128:(i + 1) * 128], lhsT=Noff[0:CH, i, :], rhs=mid[0:CH, i, :], start=True, stop=True),
                     reads=[(nm, d, "mid"), ("Noff", d)], writes=[("bank", bnk)])
            P.dve(lambda e, psx3=psx3, src=src: e.tensor_tensor(out=src[0:CH, 0:ncn, :], in0=src[0:CH, 0:ncn, :], in1=psx3, op=ALU.subtract),
                  reads=[("bank", bnk), (nm, d)], writes=[(nm, d)])
        yield
        for i in range(ncn):
            P.pe(lambda e, i=i: e.matmul(bankB[:, CH * i:CH * i + CH], lhsT=Kbg[0:CH, i, :], rhs=Xf[0:CH, i, :], start=True, stop=True),
                 reads=[("Kbg", d), ("Xm", d, cur)], writes=[("bank", BB)])
        P.act(lambda e: e.copy(out=WTs[d][buf][:, 0:ncn, :], in_=bankB[:, 0:n].rearrange("p (c i) -> p c i", i=CH)), reads=[("bank", BB)], writes=[("WT", d, buf)])
        ps = bankA
        for i in range(ncn):
            P.pe(lambda e, ps=ps, i=i: e.matmul(ps[0:CH, i * 128:(i + 1) * 128], lhsT=Xf[0:CH, i, :], rhs=Vb[0:CH, i, :], start=True, stop=True),
                 reads=[("Vb", d), ("Xm", d, cur)], writes=[("bank", BA)])
        P.act(lambda e, ps=ps: e.copy(out=Us[d][buf][0:CH, 0:ncn, :], in_=ps[0:CH, 0:128 * ncn].rearrange("p (c i) -> p c i", i=128)),
              reads=[("bank", BA)], writes=[("U", d, buf)])
        yield

    sstate = [dict(cur=0, n=0) for d in range(2)]

    def scan(d, grp, buf):
        cs = chunk_list(d, grp)
        c0 = min(cs); ncn = len(cs)
        st = sstate[d]
        psA = C.banks[4 + 2 * d]
        psB = C.banks[5 + 2 * d]
        kA_, kB_ = ("bank", 4 + 2 * d), ("bank", 5 + 2 * d)
        for c in cs:
            i = c - c0
            cur = st["cur"]; nxt = 1 - cur
            ob_ = st["n"] % 2
            st["n"] += 1
            ub = ob_
            ku, ks, ko = kA_, kB_, kA_
            P.pe(lambda e, i=i, cur=cur: e.matmul(psA[0:CH, 0:128], lhsT=WTs[d][buf][:, i, :], rhs=S[d][cur], start=True, stop=True),
                 reads=[("WT", d, buf), ("S", d, cur)], writes=[ku])
            P.dve(lambda e, i=i, ub=ub: e.tensor_tensor(out=Up[d][ub][0:CH, :], in0=Us[d][buf][0:CH, i, :], in1=psA[0:CH, 0:128], op=ALU.subtract),
                  reads=[ku, ("U", d, buf)], writes=[("Up", d, ub)])
            P.pe(lambda e, i=i, ub=ub: e.matmul(psB[:, 0:128], lhsT=KDs[d][buf][0:CH, i, :], rhs=Up[d][ub][0:CH, :], start=True, stop=True),
                 reads=[("KD", d, buf), ("Up", d, ub)], writes=[ks])
            oc = 128 + CH * ob_
            P.pe(lambda e, i=i, cur=cur, oc=oc: e.matmul(psA[:, oc:oc + CH], lhsT=S[d][cur], rhs=QGs[d][buf][:, i, :], start=True, stop=False),
                 reads=[("S", d, cur), ("QG", d, buf)], writes=[ko])
            P.pe(lambda e, i=i, ub=ub, oc=oc: e.matmul(psA[:, oc:oc + CH], lhsT=Up[d][ub][0:CH, :], rhs=AQs[d][buf][0:CH, i, :], start=False, stop=True),
                 reads=[("Up", d, ub), ("AQ", d, buf)], writes=[ko])
            P.act(lambda e, c=c, cur=cur, nxt=nxt: e.activation(out=S[d][nxt], in_=S[d][cur], func=AF.Copy, scale=glb[d][:, c:c + 1]),
                  reads=[("S", d, cur), ("glb", d)], writes=[("S", d, nxt)])
            P.dve(lambda e, nxt=nxt: e.tensor_tensor(out=S[d][nxt], in0=S[d][nxt], in1=psB[:, 0:128], op=ALU.add),
                  reads=[ks, ("S", d, nxt)], writes=[("S", d, nxt)])
            if c not in otw:
                otw.add(c)
                P.act(lambda e, c=c, oc=oc: e.copy(out=OTs[:, CH * c:CH * c + CH], in_=psA[:, oc:oc + CH]), reads=[ko], writes=[("OT", c)])
            else:
                P.dve(lambda e, c=c, oc=oc: e.tensor_tensor(out=OTs[:, CH * c:CH * c + CH], in0=OTs[:, CH * c:CH * c + CH], in1=psA[:, oc:oc + CH], op=ALU.add),
                      reads=[ko, ("OT", c)], writes=[("OT", c)])
            st["cur"] = nxt
            yield

    for d in range(2):
        P.pool(lambda e, d=d: e.memset(S[d][0], 0.0), writes=[("S", d, 0)])
    order = [[16] + list(range(16)), [16] + list(range(15, -1, -1))]
    nsteps = 17

    merge_gens([pre(0, order[0][0], 0), pre(1, order[1][0], 0)])
    for s_ in range(nsteps):
        gens = [scan(0, order[0][s_], s_ % 2), scan(1, order[1][s_], s_ % 2)]
        if s_ + 1 < nsteps:
            gens.append(pre(0, order[0][s_ + 1], (s_ + 1) % 2))
            gens.append(pre(1, order[1][s_ + 1], (s_ + 1) % 2))
        merge_gens(gens)
    P.barrier()
    mk5 = A.mark()
    if stop <= 4:
        return
    o0 = [A.alloc([512]) for _ in range(2)]; o1 = [A.alloc([512]) for _ in range(2)]; sq2 = [A.alloc([512]) for _ in range(2)]
    rr = [A.alloc([512]) for _ in range(2)]; zt = [A.alloc([512], BF16) for _ in range(2)]; ob = [A.alloc([512], BF16) for _ in range(2)]
    eps128 = A.alloc([1])
    P.pool(lambda e: e.memset(eps128, 128.0 * EPS), writes=["eps128"])
    nws = A.alloc([1])
    P.dve(lambda e: e.tensor_scalar(out=nws, in0=nw, scalar1=math.sqrt(128.0), scalar2=None, op0=ALU.mult), reads=["nw"], writes=["nws"])
    for ti, (t0, n) in enumerate(TILES):
        b = ti % 2
        P.dma(lambda e, t0=t0, n=n, b=b: e.dma_start(out=zt[b][:, 0:n], in_=zsD[:, t0:t0 + n]), reads=[("zsD", ti)], writes=[("zt", b)])
        otk = [("OT", c_) for c_ in range(t0 // GCH, (t0 + n) // GCH)]
        P.act(lambda e, t0=t0, n=n, b=b: e.activation(out=sq2[b][:, 0:n], in_=OTs[:, t0:t0 + n], func=AF.Square), reads=otk, writes=[("sq2", b)])
        bank = b
        ps = C.banks[bank]
        P.pe(lambda e, ps=ps, n=n, b=b: e.matmul(ps[:, 0:n], lhsT=onesf, rhs=sq2[b][:, 0:n], start=True, stop=True), reads=[("sq2", b), "onesf"], writes=[("bank", bank)])
        P.act(lambda e, ps=ps, n=n, b=b: e.activation(out=rr[b][:, 0:n], in_=ps[:, 0:n], func=AF.Ln, bias=eps128[:, 0:1]), reads=[("bank", bank), "eps128"], writes=[("rr", b)])
        P.act(lambda e, n=n, b=b: e.activation(out=rr[b][:, 0:n], in_=rr[b][:, 0:n], func=AF.Exp, scale=-0.5), reads=[("rr", b)], writes=[("rr", b)])
        P.dve(lambda e, t0=t0, n=n, b=b: e.scalar_tensor_tensor(out=o0[b][:, 0:n], in0=OTs[:, t0:t0 + n], scalar=nws[:, 0:1], in1=rr[b][:, 0:n], op0=ALU.mult, op1=ALU.mult),
              reads=otk + [("rr", b), "nws"], writes=[("o0", b)])
        P.dve(lambda e, n=n, b=b: e.tensor_tensor(out=ob[b][:, 0:n], in0=o0[b][:, 0:n], in1=zt[b][:, 0:n], op=ALU.mult), reads=[("o0", b), ("zt", b)], writes=[("gob", b)])
        P.dma(lambda e, t0=t0, n=n, b=b: e.dma_start(out=mixT[grow0:grow0 + 128, t0:t0 + n], in_=ob[b][:, 0:n]), reads=[("gob", b)], writes=[("mixT", "gdn", ti)])
        if after_tile is not None:
            after_tile(ti)
    A.reset(m0)


A_INPUTS = ["xT", "cv", "wmod", "bmod", "ng", "win", "lru_cw", "lru_cb", "lru_w", "lru_b", "lru_lam",
            "ropeC", "ropeS", "rotT", "e65", "mask", "ident", "sink", "gdn_cw", "gdn_par", "gdn_nw", "gmask"]


def build_A():
    nc = bass.Bass("TRN2", target_bir_lowering=False)
    xT = dram_in(nc, "xT", [D, TT]); cv = dram_in(nc, "cv", [128, 8, 2]); wmod = dram_in(nc, "wmod", [128, 8, 2048])
    bmod = dram_in(nc, "bmod", [128, 16]); ng = dram_in(nc, "ng", [128, 8]); win = dram_in(nc, "win", [128, 8, NATT + NGDN + NLRU])
    lru_cw = dram_in(nc, "lru_cw", [128, 4]); lru_cb = dram_in(nc, "lru_cb", [128, 1]); lru_w = dram_in(nc, "lru_w", [128, 2, 2, 128])
    lru_b = dram_in(nc, "lru_b", [128, 2, 2]); lru_lam = dram_in(nc, "lru_lam", [128, 2])
    ropeC = dram_in(nc, "ropeC", [64, T]); ropeS = dram_in(nc, "ropeS", [64, T]); rotT = dram_in(nc, "rotT", [64, 64]); e65 = dram_in(nc, "e65", [64, 65])
    mask = dram_in(nc, "mask", [128, 384], BF16); ident = dram_in(nc, "ident", [128, 128]); sink = dram_in(nc, "sink", [1, 2])
    gdn_cw = dram_in(nc, "gdn_cw", [128, 3, 4]); gdn_par = dram_in(nc, "gdn_par", [1, 4]); gdn_nw = dram_in(nc, "gdn_nw", [128, 1])
    gmask = dram_in(nc, "gmask", [GCH, 6, GCH])
    H = dram_tmp(nc, "H", [128, 8, TT], BF16)
    scr = dict(raw=dram_tmp(nc, "raw", [3, 128, TT]), prc=dram_tmp(nc, "prc", [3, 128, TT]), zsD=dram_tmp(nc, "zsD", [128, TT], BF16),
               ba=dram_tmp(nc, "ba", [4, TT]), rows=dram_tmp(nc, "rows", [2, 2, TT]), gl=dram_tmp(nc, "gl", [2, 132]), OD=dram_tmp(nc, "OD", [2, 128, TT]),
               prcb=dram_tmp(nc, "prcb", [2, 128, TT], BF16))
    mixT = dram_out(nc, "mixT", [384, TT], BF16)
    with ExitStack() as st:
        C = setup(nc, st)
        C.P._bar_t = C.A.alloc([1])
        phase_h(C, xT, cv, wmod, bmod, ng, H)
        C.P.barrier()
        phase_att(C, H, win, ropeC, ropeS, rotT, e65, mask, ident, sink, mixT, True)
        C.P.barrier()
        phase_lru(C, H, win, lru_cw, lru_cb, lru_w, lru_b, lru_lam, mixT, 256)
        C.P.barrier()
        phase_gdn(C, H, win, gdn_cw, gdn_par, gdn_nw, gmask, ident, mixT, scr)
        C.P.build()
    return nc


NTB = 2112
BTILES = [(512 * i, 512) for i in range(4)] + [(2048, 64)]
B_INPUTS = ["xs", "ms", "wout", "cv", "wmodg", "bmodg", "fg"]


def build_B():
    nc = bass.Bass("TRN2", target_bir_lowering=False)
    xs = dram_in(nc, "xs", [D, NTB]); ms = dram_in(nc, "ms", [1536, NTB], BF16); wout = dram_in(nc, "wout", [128, 12, D])
    cv = dram_in(nc, "cv", [128, 8, 2]); wmodg = dram_in(nc, "wmodg", [128, 8, 1024]); bmodg = dram_in(nc, "bmodg", [128, 8]); fg = dram_in(nc, "fg", [128, 8])
    xo = dram_out(nc, "xo", [D, NTB]); xn = dram_out(nc, "xn", [D, NTB])
    with ExitStack() as st:
        C = setup(nc, st)
        P, A = C.P, C.A
        P._bar_t = A.alloc([1])
        ones_bf = A.alloc([128], BF16)
        cvt = A.alloc([8, 2]); scv = A.alloc([8, 2]); bm = A.alloc([8]); gate = A.alloc([8, 2]); fgt = A.alloc([8]); epsb = A.alloc([1])
        P.pool(lambda e: e.memset(ones_bf, 1.0), writes=["ones_bf"])
        P.pool(lambda e: e.memset(epsb, 1024.0 * EPS), writes=["epsb"])
        P.dma(lambda e: e.dma_start(out=cvt, in_=cv), writes=["cvt"])
        P.dma(lambda e: e.dma_start(out=bm, in_=bmodg), writes=["bm"])
        P.dma(lambda e: e.dma_start(out=fgt, in_=fg), writes=["fgt"])
        P.act(lambda e: e.activation(out=scv, in_=cvt, func=AF.Silu), reads=["cvt"], writes=["scv"])
        wm = A.alloc([8, 1024])
        P.dma(lambda e: e.dma_start(out=wm, in_=wmodg), writes=["wm"])
        psm = C.banks[0][:, 0:16].rearrange("p (a b) -> p a b", a=8, b=2)
        for cc in range(8):
            for k in range(8):
                P.pe(lambda e, cc=cc, k=k: e.matmul(psm[:, cc, :], lhsT=wm[:, k, cc * 128:(cc + 1) * 128], rhs=scv[:, k, :], start=(k == 0), stop=(k == 7)),
                     reads=["wm", "scv"], writes=[("bank", 0)])
        P.dve(lambda e: e.tensor_tensor(out=gate, in0=psm, in1=bm.unsqueeze(2).to_broadcast([128, 8, 2]), op=ALU.add), reads=[("bank", 0), "bm"], writes=["gate"])
        P.dve(lambda e: e.tensor_scalar(out=fgt, in0=fgt, scalar1=32.0, scalar2=None, op0=ALU.mult), reads=["fgt"], writes=["fgt"])
        wob = A.alloc([12, D], BF16)
        wof = A.alloc([12, D])
        P.dma(lambda e: e.dma_start(out=wof, in_=wout), writes=["wof"])
        for k in range(12):
            eng = P.dve if k % 2 == 0 else P.pool
            eng(lambda e, k=k: e.tensor_copy(out=wob[:, k, :], in_=wof[:, k, :]), reads=["wof"], writes=[("wob", k)])
        mt = [A.alloc([12, 512], BF16) for _ in range(2)]
        xt = [A.alloc([8, 512]) for _ in range(2)]
        tt_ = [A.alloc([512]) for _ in range(2)]
        sq = [A.alloc([8, 512], BF16) for _ in range(2)]
        rs = [A.alloc([512]) for _ in range(2)]
        msv = ms.rearrange("(k p) t -> p k t", p=128)
        xsv = xs.rearrange("(k p) t -> p k t", p=128)
        xov = xo.rearrange("(k p) t -> p k t", p=128)
        xnv = xn.rearrange("(k p) t -> p k t", p=128)
        cnt = 0
        for ti, (t0, n) in enumerate(BTILES):
            b = ti % 2
            s = 1 if ti == 4 else 0
            P.dma(lambda e, b=b, t0=t0, n=n: e.dma_start(out=mt[b][:, :, 0:n], in_=msv[:, :, t0:t0 + n]), writes=[("mt", b)])
            P.dma(lambda e, b=b, t0=t0, n=n: e.dma_start(out=xt[b][:, :, 0:n], in_=xsv[:, :, t0:t0 + n]), writes=[("xt", b)])
            for dc in range(8):
                bank = 1 + (cnt % 4)
                cnt += 1
                ps = C.banks[bank]
                for k in range(12):
                    P.pe(lambda e, ps=ps, b=b, k=k, dc=dc, n=n: e.matmul(ps[:, 0:n], lhsT=wob[:, k, dc * 128:(dc + 1) * 128], rhs=mt[b][:, k, 0:n],
                                                                       start=(k == 0), stop=(k == 11)),
                         reads=[("mt", b), ("wob", k)], writes=[("bank", bank)])
                tb = cnt % 2
                P.act(lambda e, ps=ps, tb=tb, dc=dc, n=n, s=s: e.activation(out=tt_[tb][:, 0:n], in_=ps[:, 0:n], func=AF.Identity, scale=gate[:, dc, s:s + 1]),
                      reads=[("bank", bank), "gate"], writes=[("tt", tb)])
                P.dve(lambda e, b=b, tb=tb, dc=dc, n=n: e.tensor_tensor(out=xt[b][:, dc, 0:n], in0=xt[b][:, dc, 0:n], in1=tt_[tb][:, 0:n], op=ALU.add),
                      reads=[("tt", tb), ("xt", b)], writes=[("xt", b)])
            P.dma(lambda e, b=b, t0=t0, n=n: e.dma_start(out=xov[:, :, t0:t0 + n], in_=xt[b][:, :, 0:n]), reads=[("xt", b)], writes=[("xo", ti)])
            P.act(lambda e, b=b, n=n: e.activation(out=sq[b][:, :, 0:n], in_=xt[b][:, :, 0:n], func=AF.Square), reads=[("xt", b)], writes=[("sq", b)])
            bank = 5 + b
            ps = C.banks[bank]
            for k in range(8):
                P.pe(lambda e, ps=ps, b=b, k=k, n=n: e.matmul(ps[:, 0:n], lhsT=ones_bf, rhs=sq[b][:, k, 0:n], start=(k == 0), stop=(k == 7)),
                     reads=[("sq", b), "ones_bf"], writes=[("bank", bank)])
            P.act(lambda e, ps=ps, b=b, n=n: e.activation(out=rs[b][:, 0:n], in_=ps[:, 0:n], func=AF.Ln, bias=epsb[:, 0:1]), reads=[("bank", bank), "epsb"], writes=[("rs", b)])
            P.act(lambda e, b=b, n=n: e.activation(out=rs[b][:, 0:n], in_=rs[b][:, 0:n], func=AF.Exp, scale=-0.5), reads=[("rs", b)], writes=[("rs", b)])
            P.dve(lambda e, b=b, n=n: e.tensor_tensor(out=xt[b][:, :, 0:n], in0=xt[b][:, :, 0:n], in1=rs[b][:, 0:n].unsqueeze(1).to_broadcast([128, 8, n]), op=ALU.mult),
                  reads=[("xt", b), ("rs", b), ("xo", ti)], writes=[("xt", b)])
            P.dve(lambda e, b=b, n=n: e.tensor_tensor(out=xt[b][:, :, 0:n], in0=xt[b][:, :, 0:n], in1=fgt.unsqueeze(2).to_broadcast([128, 8, n]), op=ALU.mult),
                  reads=[("xt", b), "fgt"], writes=[("xt", b)])
            P.dma(lambda e, b=b, t0=t0, n=n: e.dma_start(out=xnv[:, :, t0:t0 + n], in_=xt[b][:, :, 0:n]), reads=[("xt", b)], writes=[("xn", ti)])
        P.build()
    return nc


def prep_B(inp, l, b, j, xT_full, mix_full):
    f = np.float32
    lat = slice(2048 * j, 2048 * (j + 1)); cx = slice(T + 64 * j, T + 64 * (j + 1))
    d = {}
    d["xs"] = np.ascontiguousarray(np.concatenate([xT_full[:, lat], xT_full[:, cx]], axis=1))
    d["ms"] = np.ascontiguousarray(np.concatenate([mix_full[:, lat], mix_full[:, cx]], axis=1))
    d["wout"] = np.ascontiguousarray(inp["w_out"][l].reshape(12, 128, D).transpose(1, 0, 2))
    d["cv"] = pk(np.stack([inp["c"][b], inp["c_ctx"]], axis=1).astype(f))
    d["wmodg"] = pk(inp["w_mod"][l][:, 2048:3072])
    d["bmodg"] = np.ascontiguousarray(inp["b_mod"][l][2048:3072].reshape(8, 128).T)
    d["fg"] = np.ascontiguousarray(inp["final_g"].reshape(8, 128).T)
    return d


def prep_A_all(inp, l, b, j, xT_full):
    d = prep_A(inp, l, b, j, xT_full)
    d.update(prep_att(inp, l, j))
    d.update(prep_gdn(inp, l, j))
    return d


def run_forward(inp):
    inp = {k: np.asarray(v) for k, v in inp.items()}
    ncA = build_A()
    ncB = build_B()
    xT = [np.ascontiguousarray(np.concatenate([inp["x"][b], inp["ctx"][b]], axis=0).T.astype(np.float32)) for b in range(2)]
    out = None
    for l in range(2):
        maps = []
        for core in range(8):
            b, j = core // 4, core % 4
            d = prep_A_all(inp, l, b, j, xT[b])
            maps.append({k: d[k] for k in A_INPUTS})
        res = run_bass_kernel_spmd(ncA, maps, core_ids=list(range(8)))
        mix = []
        for b in range(2):
            m = np.zeros((1536, TT), ml_dtypes.bfloat16)
            for j in range(4):
                r = np.asarray(res.results[4 * b + j]["mixT"])
                m[128 * j:128 * j + 128] = r[0:128]
                m[512 + 128 * j:512 + 128 * j + 128] = r[128:256]
                m[1024 + 128 * j:1024 + 128 * j + 128] = r[256:384]
            mix.append(m)
        maps = []
        for core in range(8):
            b, j = core // 4, core % 4
            d = prep_B(inp, l, b, j, xT[b], mix[b])
            maps.append({k: d[k] for k in B_INPUTS})
        res = run_bass_kernel_spmd(ncB, maps, core_ids=list(range(8)))
        if l == 0:
            for b in range(2):
                nx = np.empty((D, TT), np.float32)
                for j in range(4):
                    r = np.asarray(res.results[4 * b + j]["xo"])
                    nx[:, 2048 * j:2048 * (j + 1)] = r[:, 0:2048]
                    nx[:, T + 64 * j:T + 64 * (j + 1)] = r[:, 2048:2112]
                xT[b] = nx
        else:
            out = np.empty((2, T, D), np.float32)
            for b in range(2):
                for j in range(4):
                    r = np.asarray(res.results[4 * b + j]["xn"])
                    out[b, 2048 * j:2048 * (j + 1), :] = r[:, 0:2048].T
    return out


A_CONST = ["ropeC", "ropeS", "rotT", "e65", "mask", "ident", "gmask"]
A_LAYER = ["cv", "wmod", "bmod", "ng", "win", "lru_cw", "lru_cb", "lru_w", "lru_b", "lru_lam", "sink", "gdn_cw", "gdn_par", "gdn_nw"]
A_SHAPES = dict(cv=[128, 8, 2], wmod=[128, 8, 2048], bmod=[128, 16], ng=[128, 8], win=[128, 8, NATT + NGDN + NLRU], lru_cw=[128, 4], lru_cb=[128, 1],
                lru_w=[128, 2, 2, 128], lru_b=[128, 2, 2], lru_lam=[128, 2], sink=[1, 2], gdn_cw=[128, 3, 4], gdn_par=[1, 4], gdn_nw=[128, 1],
                ropeC=[64, T], ropeS=[64, T], rotT=[64, 64], e65=[64, 65], mask=[128, 384], ident=[128, 128], gmask=[GCH, 6, GCH])
B_LAYER = ["woutj", "wmodgj", "bmodgj"]
B_SHAPES = dict(woutj=[128, 12, 256], wmodgj=[128, 8, 256], bmodgj=[128, 2])
GROUPS = [[0, 1, 2, 3], [4, 5, 6, 7]]


def phase_bf_gen(C, l, mixA_all, mixG_all, xsrc, xdst, xdst_b, x_all, cv, wmodgj, bmodgj, woutj):
    P, A = C.P, C.A
    m0 = A.mark()
    cvt = A.alloc([8, 2]); scv = A.alloc([8, 2]); bm = A.alloc([2]); gate = A.alloc([2, 2])
    P.dma(lambda e: e.dma_start(out=cvt, in_=cv), writes=["cvt_B"])
    P.dma(lambda e: e.dma_start(out=bm, in_=bmodgj), writes=["bm_B"])
    P.act(lambda e: e.activation(out=scv, in_=cvt, func=AF.Silu), reads=["cvt_B"], writes=["scv_B"])
    wm = A.alloc([8, 256])
    P.dma(lambda e: e.dma_start(out=wm, in_=wmodgj), writes=["wm_B"])
    psm = C.banks[0][:, 0:4].rearrange("p (a b) -> p a b", a=2, b=2)
    for cc in range(2):
        for k in range(8):
            P.pe(lambda e, cc=cc, k=k: e.matmul(psm[:, cc, :], lhsT=wm[:, k, cc * 128:(cc + 1) * 128], rhs=scv[:, k, :], start=(k == 0), stop=(k == 7)),
                 reads=["wm_B", "scv_B"], writes=[("bank", 0)])
    P.dve(lambda e: e.tensor_tensor(out=gate, in0=psm, in1=bm.unsqueeze(2).to_broadcast([128, 2, 2]), op=ALU.add), reads=[("bank", 0), "bm_B"], writes=["gate_B"])
    wob = A.alloc([12, 256], BF16)
    wof = A.alloc([12, 256])
    P.dma(lambda e: e.dma_start(out=wof, in_=woutj), writes=["wof_B"])
    for k in range(12):
        if k % 2 == 0:
            P.dve(lambda e, k=k: e.tensor_copy(out=wob[:, k, :], in_=wof[:, k, :]), reads=["wof_B"], writes=[("wob", k)])
        else:
            P.act(lambda e, k=k: e.copy(out=wob[:, k, :], in_=wof[:, k, :]), reads=["wof_B"], writes=[("wob", k)])
    mt = [A.alloc([12, 512], BF16) for _ in range(2)]
    xt = [A.alloc([2, 512]) for _ in range(2)]
    xb = [A.alloc([2, 512], BF16) for _ in range(2)]
    tt_ = [A.alloc([512]) for _ in range(2)]
    cnt = 0
    yield
    for ti, (t0, n) in enumerate(TILES):
        b = ti % 2
        s = 1 if ti == 16 else 0
        def _ldb(tj):
            tt0, nn = TILES[tj]
            bb = tj % 2
            P.dma(lambda e: e.dma_start(out=mt[bb][:, 0:8, 0:nn], in_=mixA_all.pk(tt0, nn)), reads=[("mixA_all", mixA_all.cidx(tt0))], writes=[("mt", bb)])
            P.dma(lambda e: e.dma_start(out=mt[bb][:, 8:12, 0:nn], in_=mixG_all.pk(tt0, nn)), reads=[("mixG_all", mixG_all.cidx(tt0))], writes=[("mtg", bb)])
            P.dma(lambda e: e.dma_start(out=xt[bb][:, :, 0:nn], in_=xsrc.pk(tt0, nn) if isinstance(xsrc, ChunkedDram) else xsrc.rearrange("(k p) t -> p k t", p=128)[:, :, tt0:tt0 + nn]),
                  reads=[("xrows", l - 1, tj)], writes=[("xtB", bb)])
        if ti == 0:
            _ldb(0)
        if ti + 1 < len(TILES):
            _ldb(ti + 1)
        for dc in range(2):
            bank = 1 + (cnt % 4)
            cnt += 1
            ps = C.banks[bank]
            for k in range(12):
                P.pe(lambda e, ps=ps, b=b, k=k, dc=dc, n=n: e.matmul(ps[:, 0:n], lhsT=wob[:, k, dc * 128:(dc + 1) * 128], rhs=mt[b][:, k, 0:n],
                                                                   start=(k == 0), stop=(k == 11)),
                     reads=[("mt", b), ("mtg", b), ("wob", k)], writes=[("bank", bank)])
            tb = cnt % 2
            P.act(lambda e, ps=ps, tb=tb, dc=dc, n=n, s=s: e.activation(out=tt_[tb][:, 0:n], in_=ps[:, 0:n], func=AF.Identity, scale=gate[:, dc, s:s + 1]),
                  reads=[("bank", bank), "gate_B"], writes=[("ttB", tb)])
            P.dve(lambda e, b=b, tb=tb, dc=dc, n=n: e.tensor_tensor(out=xt[b][:, dc, 0:n], in0=xt[b][:, dc, 0:n], in1=tt_[tb][:, 0:n], op=ALU.add),
                  reads=[("ttB", tb), ("xtB", b)], writes=[("xtB", b)])
        P.act(lambda e, b=b, n=n: e.copy(out=xb[b][:, :, 0:n], in_=xt[b][:, :, 0:n]), reads=[("xtB", b)], writes=[("xbB", b)])
        P.dma(lambda e, b=b, t0=t0, n=n: e.dma_start(out=xdst.pk(t0, n), in_=xt[b][:, :, 0:n]), reads=[("xtB", b)], writes=[("xrows", l, ti)])
        P.dma(lambda e, b=b, t0=t0, n=n: e.dma_start(out=xdst_b.pk(t0, n), in_=xb[b][:, :, 0:n]), reads=[("xbB", b)], writes=[("xrowsb", l, ti)])
        if xdst_b.last_tile_of_chunk(ti):
            c = xdst_b.cidx(t0)
            tis = xdst_b.tiles_of_chunk(c)
            P.cc(lambda e, c=c: e.collective_compute("AllGather", ALU.bypass, replica_groups=GROUPS, ins=[xdst_b.t[c].opt()], outs=[x_all.t[c].opt()]),
                 reads=[("xrowsb", l, t_) for t_ in tis], writes=[("x_all", c)])
        yield
    A.reset(m0)


def phase_final_gen(C, x_all, xo, fgj, xn):
    P, A = C.P, C.A
    m0 = A.mark()
    ones_bf = A.alloc([128], BF16); fgt = A.alloc([2]); epsb = A.alloc([1])
    P.pool(lambda e: e.memset(ones_bf, 1.0), writes=["ones_bf"])
    P.pool(lambda e: e.memset(epsb, 1024.0 * EPS), writes=["epsb"])
    P.dma(lambda e: e.dma_start(out=fgt, in_=fgj), writes=["fgt"])
    P.dve(lambda e: e.tensor_scalar(out=fgt, in0=fgt, scalar1=32.0, scalar2=None, op0=ALU.mult), reads=["fgt"], writes=["fgt"])
    xa = [A.alloc([8, 512], BF16) for _ in range(2)]; sq = [A.alloc([8, 512], BF16) for _ in range(2)]
    rs = [A.alloc([512]) for _ in range(2)]; xt = [A.alloc([2, 512]) for _ in range(2)]
    xnv = xn.rearrange("(k p) t -> p k t", p=128)
    yield
    for ti, (t0, n) in enumerate(TILES[:16]):
        b = ti % 2
        P.dma(lambda e, b=b, t0=t0, n=n: e.dma_start(out=xa[b][:, :, 0:n], in_=x_all.pk(t0, n)), reads=[("x_all", x_all.cidx(t0))], writes=[("xa", b)])
        P.dma(lambda e, b=b, t0=t0, n=n: e.dma_start(out=xt[b][:, :, 0:n], in_=xo.pk(t0, n)), reads=[("xrows", 1, ti)], writes=[("xt", b)])
        P.act(lambda e, b=b, n=n: e.activation(out=sq[b][:, :, 0:n], in_=xa[b][:, :, 0:n], func=AF.Square), reads=[("xa", b)], writes=[("sq", b)])
        bank = 5 + b
        ps = C.banks[bank]
        for k in range(8):
            P.pe(lambda e, ps=ps, b=b, k=k, n=n: e.matmul(ps[:, 0:n], lhsT=ones_bf, rhs=sq[b][:, k, 0:n], start=(k == 0), stop=(k == 7)),
                 reads=[("sq", b), "ones_bf"], writes=[("bank", bank)])
        P.act(lambda e, ps=ps, b=b, n=n: e.activation(out=rs[b][:, 0:n], in_=ps[:, 0:n], func=AF.Ln, bias=epsb[:, 0:1]), reads=[("bank", bank), "epsb"], writes=[("rs", b)])
        P.act(lambda e, b=b, n=n: e.activation(out=rs[b][:, 0:n], in_=rs[b][:, 0:n], func=AF.Exp, scale=-0.5), reads=[("rs", b)], writes=[("rs", b)])
        P.dve(lambda e, b=b, n=n: e.tensor_tensor(out=xt[b][:, :, 0:n], in0=xt[b][:, :, 0:n], in1=rs[b][:, 0:n].unsqueeze(1).to_broadcast([128, 2, n]), op=ALU.mult),
              reads=[("xt", b), ("rs", b)], writes=[("xt", b)])
        P.dve(lambda e, b=b, n=n: e.tensor_tensor(out=xt[b][:, :, 0:n], in0=xt[b][:, :, 0:n], in1=fgt.unsqueeze(2).to_broadcast([128, 2, n]), op=ALU.mult),
              reads=[("xt", b), "fgt"], writes=[("xt", b)])
        P.dma(lambda e, b=b, t0=t0, n=n: e.dma_start(out=xnv[:, :, t0:t0 + n], in_=xt[b][:, :, 0:n]), reads=[("xt", b)], writes=[("xn", ti)])
        yield
    A.reset(m0)


def build_fused():
    nc = bass.Bass("TRN2", target_bir_lowering=False)
    xT0 = dram_in(nc, "xT0", [D, TT]); xr0 = dram_in(nc, "xr0", [256, TT]); fgj = dram_in(nc, "fgj", [128, 2])
    cst = {k: dram_in(nc, k, A_SHAPES[k], BF16 if k == "mask" else F32) for k in A_CONST}
    lay = [{k: dram_in(nc, f"{k}_{l}", A_SHAPES[k]) for k in A_LAYER} for l in range(2)]
    layb = [{k: dram_in(nc, f"{k}_{l}", B_SHAPES[k]) for k in B_LAYER} for l in range(2)]
    H = dram_tmp(nc, "H", [128, 8, TT], BF16)
    scr = dict(raw=dram_tmp(nc, "raw", [3, 128, TT]), prc=dram_tmp(nc, "prc", [3, 128, TT]), zsD=dram_tmp(nc, "zsD", [128, TT], BF16),
               ba=dram_tmp(nc, "ba", [4, TT]), rows=dram_tmp(nc, "rows", [2, 2, TT]), gl=dram_tmp(nc, "gl", [2, 132]), OD=dram_tmp(nc, "OD", [2, 128, TT]),
               prcb=dram_tmp(nc, "prcb", [2, 128, TT], BF16))
    mixA_mine = ChunkedDram(nc, "mixA_mine", 256, BF16, 512); mixA_all = ChunkedDram(nc, "mixA_all", 1024, BF16, 512)
    mixG_mine = ChunkedDram(nc, "mixG_mine", 128, BF16, 512); mixG_all = ChunkedDram(nc, "mixG_all", 512, BF16, 512)
    xo = [ChunkedDram(nc, f"xo{l}", 256, F32, 512) for l in range(2)]
    xob = [ChunkedDram(nc, f"xob{l}", 256, BF16, 512) for l in range(2)]
    x_all = ChunkedDram(nc, "x_all", D, BF16, 512)
    xn = dram_out(nc, "xn", [256, T])
    with ExitStack() as st:
        C = setup(nc, st)
        P = C.P
        P._bar_t = C.A.alloc([1])
        base_mark = C.A.mark()

        def lagged(gb_, gn_, lag=3):
            next(gb_)
            C.A.reset(base_mark)
            next(gn_)
            nb_done = 0
            alive_b = alive_n = True
            while alive_b or alive_n:
                if alive_b:
                    try:
                        next(gb_); nb_done += 1
                    except StopIteration:
                        alive_b = False
                if alive_n and (nb_done >= lag or not alive_b):
                    try:
                        next(gn_)
                    except StopIteration:
                        alive_n = False

        hgen = None
        for l in range(2):
            p = lay[l]
            if l == 0:
                C.A.reset(base_mark)
                phase_h(C, xT0, p["cv"], p["wmod"], p["bmod"], p["ng"], H)
            P.barrier()
            C.A.reset(base_mark)
            phase_att(C, H, p["win"], cst["ropeC"], cst["ropeS"], cst["rotT"], cst["e65"], cst["mask"], cst["ident"], p["sink"], mixA_mine, True)
            P.barrier()
            phase_lru(C, H, p["win"], p["lru_cw"], p["lru_cb"], p["lru_w"], p["lru_b"], p["lru_lam"], mixA_mine, 128)
            P.barrier()
            akeys = [("mixT", nm, ti) for nm in ("att", "lru") for ti in range(17)]
            for c in range(mixA_mine.nchunks):
                P.cc(lambda e, c=c: e.collective_compute("AllGather", ALU.bypass, replica_groups=GROUPS, ins=[mixA_mine.t[c].opt()], outs=[mixA_all.t[c].opt()]),
                     reads=akeys, writes=[("mixA_all", c)])
            def _ag(ti):
                if mixG_mine.last_tile_of_chunk(ti):
                    c = mixG_mine.cidx(TILES[ti][0])
                    tis = mixG_mine.tiles_of_chunk(c)
                    P.cc(lambda e, c=c: e.collective_compute("AllGather", ALU.bypass, replica_groups=GROUPS, ins=[mixG_mine.t[c].opt()], outs=[mixG_all.t[c].opt()]),
                         reads=[("mixT", "gdn", t_) for t_ in tis], writes=[("mixG_all", c)])
            phase_gdn(C, H, p["win"], p["gdn_cw"], p["gdn_par"], p["gdn_nw"], cst["gmask"], cst["ident"], mixG_mine, scr, grow0=0, after_tile=_ag)
            P.barrier()
            pb = layb[l]
            C.A.reset(31200)
            gb_ = phase_bf_gen(C, l, mixA_all, mixG_all, xr0 if l == 0 else xo[0], xo[l], xob[l], x_all, p["cv"], pb["wmodgj"], pb["bmodgj"], pb["woutj"])
            if l == 0:
                p1 = lay[1]
                gn_ = phase_h_gen(C, x_all, p1["cv"], p1["wmod"], p1["bmod"], p1["ng"], H, xkey="x_all", xdt=BF16)
            else:
                gn_ = phase_final_gen(C, x_all, xo[1], fgj, xn)
            lagged(gb_, gn_)
        C.A.reset(base_mark)
        P.build()
        C.stats = P.stats
    return nc


def wout_reordered(w_out_l):
    rows = []
    for r in range(4):
        rows += list(range(128 * r, 128 * r + 128)) + list(range(1024 + 128 * r, 1024 + 128 * r + 128))
    for r in range(4):
        rows += list(range(512 + 128 * r, 512 + 128 * r + 128))
    return w_out_l[np.array(rows)]


def prep_fused(inp, b, j, xT_full):
    f = np.float32
    d = {}
    d["xT0"] = xT_full
    d["xr0"] = np.ascontiguousarray(xT_full[256 * j:256 * j + 256])
    d["fgj"] = np.ascontiguousarray(inp["final_g"][256 * j:256 * j + 256].reshape(2, 128).T.astype(f))
    c = att_consts(); c.update(gdn_consts())
    for k in A_CONST:
        d[k] = c[k]
    for l in range(2):
        a = prep_A_all(inp, l, b, j, None)
        for k in A_LAYER:
            d[f"{k}_{l}"] = a[k]
        wo = wout_reordered(inp["w_out"][l])[:, 256 * j:256 * j + 256]
        d[f"woutj_{l}"] = np.ascontiguousarray(wo.reshape(12, 128, 256).transpose(1, 0, 2))
        d[f"wmodgj_{l}"] = pk(inp["w_mod"][l][:, 2048 + 256 * j:2048 + 256 * j + 256])
        d[f"bmodgj_{l}"] = np.ascontiguousarray(inp["b_mod"][l][2048 + 256 * j:2048 + 256 * j + 256].reshape(2, 128).T)
    return d


_NC_CACHE = {}


def run_fused(inp):
    inp = {k: np.asarray(v) for k, v in inp.items()}
    if "nc" not in _NC_CACHE:
        _NC_CACHE["nc"] = build_fused()
    nc = _NC_CACHE["nc"]
    xT = [np.ascontiguousarray(np.concatenate([inp["x"][b], inp["ctx"][b]], axis=0).T.astype(np.float32)) for b in range(2)]
    maps = []
    for core in range(8):
        b, j = core // 4, core % 4
        maps.append(prep_fused(inp, b, j, xT[b]))
    res = run_bass_kernel_spmd(nc, maps, core_ids=list(range(8)))
    out = np.empty((2, T, D), np.float32)
    for core in range(8):
        b, j = core // 4, core % 4
        out[b, :, 256 * j:256 * j + 256] = np.asarray(res.results[core]["xn"]).T
    return out


def kernel(**inputs):
    return run_fused(inputs)
```

```python
import numpy as np
import concourse.bass as bass
import concourse.mybir as mybir
from concourse.bass_utils import run_bass_kernel_spmd

F32 = mybir.dt.float32
BF16 = mybir.dt.bfloat16
AF = mybir.ActivationFunctionType
ALU = mybir.AluOpType
AX = mybir.AxisListType

COMPUTE = ("pe", "act", "dve", "pool")
SEM_CAP = 16000
N_DMA_SEMS = 16


class Prog:
    def __init__(self, nc):
        self.nc = nc
        self.ops = []

    def op(self, eng, fn, reads=(), writes=(), dma=False):
        self.ops.append(dict(eng=eng, fn=fn, reads=tuple(reads), writes=tuple(writes), dma=dma))
        return len(self.ops) - 1

    def pe(self, fn, reads=(), writes=()):
        return self.op("pe", fn, reads, writes)

    def act(self, fn, reads=(), writes=()):
        return self.op("act", fn, reads, writes)

    def dve(self, fn, reads=(), writes=()):
        return self.op("dve", fn, reads, writes)

    def pool(self, fn, reads=(), writes=()):
        return self.op("pool", fn, reads, writes)

    def dma(self, fn, reads=(), writes=(), q="sp"):
        return self.op(q, fn, reads, writes, dma=True)

    def cc(self, fn, reads=(), writes=()):
        i = self.op("pool", fn, reads, writes, dma=True)
        self.ops[i]["cc"] = True
        return i

    def barrier(self):
        if not hasattr(self, "_bar_t"):
            raise RuntimeError("set P._bar_t (a [128,1] sbuf AP) first")
        t = self._bar_t
        i = self.op("dve", lambda e: e.memset(t, 0.0), (), ())
        self.ops[i]["barrier"] = True
        return i

    def build(self):
        nc = self.nc
        ops = self.ops
        n = len(ops)
        last_w = {}
        readers = {}
        deps = [None] * n
        last_eng = {}
        dmas_since = []
        cur_bar = None
        for i, o in enumerate(ops):
            d = {}
            if o.get("barrier"):
                for e_, j in last_eng.items():
                    if e_ != "dve":
                        d[j] = True
                for j in dmas_since:
                    d[j] = True
                dmas_since = []
                deps[i] = list(d.keys())
                cur_bar = i
                last_eng["dve"] = i
                continue
            for k in o["reads"]:
                j = last_w.get(k)
                if j is not None:
                    d[j] = True
            for k in o["writes"]:
                j = last_w.get(k)
                if j is not None and j not in d:
                    d[j] = d.get(j, False)
                for j in readers.get(k, ()):
                    if j not in d:
                        d[j] = False
            d.pop(i, None)
            keep = []
            for j, raw in d.items():
                oj = ops[j]
                if (not oj["dma"]) and (not o["dma"]) and oj["eng"] == o["eng"] and not raw:
                    continue
                if (not oj["dma"]) and (not o["dma"]) and oj["eng"] == o["eng"] == "pe":
                    continue
                keep.append(j)
            if cur_bar is not None and (o["dma"] or o["eng"] != "dve") and cur_bar not in keep:
                keep.append(cur_bar)
            deps[i] = keep
            if o["dma"]:
                if not o.get("cc"):
                    dmas_since.append(i)
            else:
                last_eng[o["eng"]] = i
            for k in o["reads"]:
                readers.setdefault(k, []).append(i)
            for k in o["writes"]:
                last_w[k] = i
                readers[k] = []
        dma_idx = [i for i, o in enumerate(ops) if o["dma"] and not o.get("cc")]
        for k, i in enumerate(dma_idx):
            if k >= N_DMA_SEMS:
                j = dma_idx[k - N_DMA_SEMS]
                if j not in deps[i]:
                    deps[i].append(j)
        sig = [False] * n
        for i in range(n):
            for j in deps[i]:
                sig[j] = True
        cnt = {e: 0 for e in COMPUTE}
        token = [None] * n
        ndma = 0
        ncc = 0
        final_dma = []
        for i, o in enumerate(ops):
            if o.get("cc"):
                ncc += 1
                token[i] = ("cc", 0, ncc)
                sig[i] = True
            elif o["dma"]:
                token[i] = ("dma", ndma % N_DMA_SEMS, 16 * (ndma // N_DMA_SEMS + 1))
                ndma += 1
                sig[i] = True
                final_dma.append(i)
            elif sig[i]:
                cnt[o["eng"]] += 1
                token[i] = ("cmp", o["eng"], cnt[o["eng"]])
        self.stats = dict(n_ops=n, ndma=ndma, cnt=dict(cnt))
        import contextlib
        stack = contextlib.ExitStack()
        dma_sems = [stack.enter_context(nc.semaphore(f"dq{i}")) for i in range(min(N_DMA_SEMS, max(ndma, 1)))]
        cc_sem = stack.enter_context(nc.semaphore("ccsem")) if ncc else None
        cmp_sems = {}
        for e in COMPUTE:
            ne = cnt[e] // SEM_CAP + 1
            cmp_sems[e] = [stack.enter_context(nc.semaphore(f"c_{e}{k}")) for k in range(ne)]

        def sem_of(tok):
            kind, a, v = tok
            if kind == "dma":
                return dma_sems[a], v
            if kind == "cc":
                return cc_sem, v
            ep = (v - 1) // SEM_CAP
            return cmp_sems[a][ep], v - ep * SEM_CAP

        by_eng = {}
        for i, o in enumerate(ops):
            by_eng.setdefault(o["eng"], []).append(i)
        last_dma_tok = {}
        for i in final_dma:
            last_dma_tok[token[i][1]] = token[i]

        block = stack.enter_context(nc.Block())

        def emit(engname):
            idxs = by_eng.get(engname, [])

            def body(eng):
                seen = {}
                for i in idxs:
                    o = ops[i]
                    for j in deps[i]:
                        tok = token[j]
                        key = (tok[0], tok[1])
                        if seen.get(key, 0) >= tok[2]:
                            continue
                        seen[key] = tok[2]
                        s, v = sem_of(tok)
                        if getattr(self, "dbg", None) and i >= self.dbg:
                            print("WAIT op", i, engname, "on op", j, ops[j]["eng"], tok, "sem", s, v)
                        eng.wait_ge(s, v)
                    ins = o["fn"](eng)
                    if sig[i]:
                        s, v = sem_of(token[i])
                        if o.get("cc"):
                            ins.then_inc(s)
                        else:
                            ins.then_inc(s, 16 if o["dma"] else 1)
                if engname == "sp":
                    for tok in last_dma_tok.values():
                        s, v = sem_of(tok)
                        eng.wait_ge(s, v)
                if engname == "pool" and ncc:
                    eng.wait_ge(cc_sem, ncc)
            return body

        block.tensor(emit("pe"))
        block.scalar(emit("act"))
        block.vector(emit("dve"))
        block.gpsimd(emit("pool"))
        block.sync(emit("sp"))
        stack.close()


import math
from contextlib import ExitStack
import numpy as np
import ml_dtypes
from concourse.ap import AP

T = 8192
L = 256
TT = T + L
D = 1024
EPS = 1e-6
TILES = [(512 * i, 512) for i in range(16)] + [(T, L)]
NATT = 384
NGDN = 516
NLRU = 256
GCH = 128


def revap(base):
    apl = [list(p) for p in base.ap]
    n = apl[-1][1]
    stp = apl[-1][0]
    apl[-1] = [-stp, n]
    return AP(base.tensor, base.offset + (n - 1) * stp, apl)


class Arena:
    def __init__(self, nc, st, name, words):
        self.t = st.enter_context(nc.sbuf_tensor(name, [128, words], F32))
        self.words = words
        self.off = 0
        self.name = name

    def mark(self):
        return self.off

    def reset(self, m=0):
        self.off = m

    def alloc(self, shape, dt=F32, parts=128):
        nel = int(np.prod(shape))
        sz = mybir.dt.size(dt)
        words = (nel * sz + 3) // 4
        words = (words + 7) // 8 * 8
        assert self.off + words <= self.words, (self.name, self.off, words, self.words)
        v = self.t[0:parts, self.off:self.off + words]
        self.off += words
        if dt != F32:
            v = v.bitcast(dt)
        v = v[:, 0:nel]
        if len(shape) == 2:
            v = v.rearrange("p (a b) -> p a b", a=shape[0], b=shape[1])
        elif len(shape) == 3:
            v = v.rearrange("p (a b c) -> p a b c", a=shape[0], b=shape[1], c=shape[2])
        return v


class Ctx:
    pass


class ChunkedDram:
    def __init__(self, nc, name, rows, dt, csize=1024):
        self.rows = rows
        self.csize = csize
        self.chunks = [(csize * c, csize) for c in range(T // csize)] + [(T, L)]
        self.nchunks = len(self.chunks)
        self.t = [nc.dram_tensor(f"{name}_c{c}", [rows, n], dt, kind="Internal").ap() for c, (t0, n) in enumerate(self.chunks)]

    def cidx(self, t0):
        return self.nchunks - 1 if t0 >= T else t0 // self.csize

    def last_tile_of_chunk(self, ti):
        if ti == 16:
            return True
        return (512 * (ti + 1)) % self.csize == 0

    def tiles_of_chunk(self, c):
        if c == self.nchunks - 1:
            return [16]
        per = self.csize // 512
        return list(range(per * c, per * c + per))

    def __getitem__(self, key):
        rs, ts = key
        r0 = 0 if rs.start is None else rs.start
        r1 = self.rows if rs.stop is None else rs.stop
        t0, t1 = ts.start, ts.stop
        c = self.cidx(t0)
        c0, n = self.chunks[c]
        assert c0 <= t0 and t1 <= c0 + n, (t0, t1)
        return self.t[c][r0:r1, t0 - c0:t1 - c0]

    def pk(self, t0, n):
        return self[:, t0:t0 + n].rearrange("(k p) t -> p k t", p=128)


def setup(nc, st, sbuf_words=49000):
    C = Ctx()
    C.nc = nc
    C.st = st
    C.P = Prog(nc)
    C.A = Arena(nc, st, "arena", sbuf_words)
    C.banks = [st.enter_context(nc.psum_tensor(f"bank{i}", [128, 512], F32)) for i in range(8)]
    return C


def dram_in(nc, name, shape, dt=F32):
    return nc.dram_tensor(name, list(shape), dt, kind="ExternalInput").ap()


def dram_out(nc, name, shape, dt=F32):
    return nc.dram_tensor(name, list(shape), dt, kind="ExternalOutput").ap()


def dram_tmp(nc, name, shape, dt=F32):
    return nc.dram_tensor(name, list(shape), dt, kind="Internal").ap()


def phase_h_gen(C, xT, cv, wmod, bmod, ng, H, xkey=None, xdt=F32):
    P, A, nc = C.P, C.A, C.nc
    m0 = A.mark()
    ones_bf = A.alloc([128], BF16)
    cvt = A.alloc([8, 2])
    scv = A.alloc([8, 2])
    bm = A.alloc([16])
    ngt = A.alloc([8])
    modT = A.alloc([16, 2])
    Avec = A.alloc([8, 2])
    C.ones_bf = ones_bf
    P.pool(lambda e: e.memset(ones_bf, 1.0), writes=["ones_bf"])
    P.dma(lambda e: e.dma_start(out=cvt, in_=cv), writes=["cvt"])
    P.dma(lambda e: e.dma_start(out=bm, in_=bmod), writes=["bm"])
    P.dma(lambda e: e.dma_start(out=ngt, in_=ng), writes=["ngt"])
    P.act(lambda e: e.activation(out=scv, in_=cvt, func=AF.Silu), reads=["cvt"], writes=["scv"])
    m1 = A.mark()
    wm = A.alloc([8, 1024])
    psm = C.banks[0][:, 0:32].rearrange("p (a b) -> p a b", a=16, b=2)
    for blk in range(2):
        P.dma(lambda e, blk=blk: e.dma_start(out=wm, in_=wmod[:, :, blk * 1024:(blk + 1) * 1024]), writes=["wm"])
        for cc in range(8):
            for k in range(8):
                P.pe(lambda e, blk=blk, cc=cc, k=k: e.matmul(psm[:, blk * 8 + cc, :], lhsT=wm[:, k, cc * 128:(cc + 1) * 128],
                                                            rhs=scv[:, k, :], start=(k == 0), stop=(k == 7)),
                     reads=["wm", "scv"], writes=[("bank", 0)])
    P.dve(lambda e: e.tensor_tensor(out=modT, in0=psm, in1=bm.unsqueeze(2).to_broadcast([128, 16, 2]), op=ALU.add),
          reads=[("bank", 0), "bm"], writes=["modT"])
    P.dve(lambda e: e.scalar_tensor_tensor(out=Avec, in0=modT[:, 8:16, :], scalar=1.0,
                                           in1=ngt.unsqueeze(2).to_broadcast([128, 8, 2]), op0=ALU.add, op1=ALU.mult),
          reads=["modT", "ngt"], writes=["Avec"])
    P.dve(lambda e: e.tensor_scalar(out=Avec, in0=Avec, scalar1=32.0, scalar2=None, op0=ALU.mult),
          reads=["Avec"], writes=["Avec"])
    xts = [A.alloc([8, 512]) for _ in range(2)]
    xss = xts if xdt == F32 else [A.alloc([8, 512], xdt) for _ in range(2)]
    sqs = [A.alloc([8, 512], BF16) for _ in range(2)]
    hbs = [A.alloc([8, 512], BF16) for _ in range(2)]
    rstd = [A.alloc([512]) for _ in range(2)]
    if isinstance(xT, ChunkedDram):
        xget = xT.pk
    else:
        xTv = xT.rearrange("(k p) t -> p k t", p=128)
        xget = lambda t0, n: xTv[:, :, t0:t0 + n]
    yield
    for ti, (t0, n) in enumerate(TILES):
        b = ti % 2
        s = 1 if t0 >= T else 0
        xt, sq, hb, rs = xts[b], sqs[b], hbs[b], rstd[b]
        xs_ = xss[b]
        ps = C.banks[5 + b]
        def _ld(tj):
            tt0, nn = TILES[tj]
            bb = tj % 2
            ckk = xT.cidx(tt0) if isinstance(xT, ChunkedDram) else 0
            xq = xss[bb]
            P.dma(lambda e: e.dma_start(out=xq[:, :, 0:nn], in_=xget(tt0, nn)), reads=([(xkey, ckk)] if xkey else []), writes=[("xs", bb)])
        if ti == 0:
            _ld(0)
        if ti + 1 < len(TILES):
            _ld(ti + 1)
        P.act(lambda e, xs_=xs_, sq=sq, n=n: e.activation(out=sq[:, :, 0:n], in_=xs_[:, :, 0:n], func=AF.Square),
              reads=[("xs", b)], writes=[("sq", b)])
        for k in range(8):
            P.pe(lambda e, ps=ps, sq=sq, k=k, n=n: e.matmul(ps[:, 0:n], lhsT=ones_bf, rhs=sq[:, k, 0:n], start=(k == 0), stop=(k == 7)),
                 reads=[("sq", b), "ones_bf"], writes=[("bank", 5 + b)])
        P.dve(lambda e, ps=ps, rs=rs, n=n: e.tensor_scalar(out=rs[:, 0:n], in0=ps[:, 0:n], scalar1=1024.0 * EPS, scalar2=None,
                                                          op0=ALU.add),
              reads=[("bank", 5 + b)], writes=[("rstd", b)])
        P.act(lambda e, rs=rs, n=n: e.activation(out=rs[:, 0:n], in_=rs[:, 0:n], func=AF.Ln), reads=[("rstd", b)], writes=[("rstd", b)])
        P.act(lambda e, rs=rs, n=n: e.activation(out=rs[:, 0:n], in_=rs[:, 0:n], func=AF.Exp, scale=-0.5), reads=[("rstd", b)], writes=[("rstd", b)])
        P.dve(lambda e, xt=xt, xs_=xs_, rs=rs, n=n: e.tensor_tensor(out=xt[:, :, 0:n], in0=xs_[:, :, 0:n],
                                                                   in1=rs[:, 0:n].unsqueeze(1).to_broadcast([128, 8, n]), op=ALU.mult),
              reads=[("xs", b), ("xt", b), ("rstd", b)], writes=[("xt", b)])
        for k in range(8):
            if k % 8 < 5:
                P.act(lambda e, xt=xt, hb=hb, k=k, n=n, s=s: e.activation(out=hb[:, k, 0:n], in_=xt[:, k, 0:n], func=AF.Identity,
                                                                          scale=Avec[:, k, s:s + 1], bias=modT[:, k, s:s + 1]),
                      reads=[("xt", b), "Avec", "modT"], writes=[("hb", b, k)])
            else:
                P.pool(lambda e, xt=xt, hb=hb, k=k, n=n, s=s: e.tensor_scalar(out=hb[:, k, 0:n], in0=xt[:, k, 0:n], scalar1=Avec[:, k, s:s + 1],
                                                                             scalar2=modT[:, k, s:s + 1], op0=ALU.mult, op1=ALU.add),
                       reads=[("xt", b), "Avec", "modT"], writes=[("hb", b, k)])
        P.dma(lambda e, hb=hb, t0=t0, n=n: e.dma_start(out=H[:, :, t0:t0 + n], in_=hb[:, :, 0:n]), reads=[("hb", b, k) for k in range(8)], writes=[("H", ti)])
        yield
    A.reset(m0)


def drain(g):
    for _ in g:
        pass


def phase_h(*a, **k):
    drain(phase_h_gen(*a, **k))


def load_weights(C, win, c0, ncols, tag):
    P, A = C.P, C.A
    wb = A.alloc([8, ncols], BF16)
    m = A.mark()
    wf = A.alloc([8, ncols])
    P.dma(lambda e: e.dma_start(out=wf, in_=win[:, :, c0:c0 + ncols]), writes=[("wf", tag)])
    for k in range(8):
        if k % 2 == 0:
            P.dve(lambda e, k=k: e.tensor_copy(out=wb[:, k, :], in_=wf[:, k, :]), reads=[("wf", tag)], writes=[("wb", tag, k)])
        else:
            P.act(lambda e, k=k: e.copy(out=wb[:, k, :], in_=wf[:, k, :]), reads=[("wf", tag)], writes=[("wb", tag, k)])
    C.wkeys = [("wb", tag, k) for k in range(8)]
    return wb, m


def project(C, H, wb, wtag, chunks, evac, banks, tiles=None, after_tile=None):
    P, A = C.P, C.A
    hts = [A.alloc([8, 512], BF16) for _ in range(2)]
    tl = TILES if tiles is None else tiles
    cnt = 0
    def _ld(tj):
        tt0, nn = tl[tj]
        hq = hts[tj % 2]
        P.dma(lambda e: e.dma_start(out=hq[:, :, 0:nn], in_=H[:, :, tt0:tt0 + nn]), reads=[("H", tt0 // 512)], writes=[("ht", wtag, tj % 2)])
    for ti, (t0, n) in enumerate(tl):
        b = ti % 2
        ht = hts[b]
        if ti == 0:
            _ld(0)
        if ti + 1 < len(tl):
            _ld(ti + 1)
        for ci, (c0, m) in enumerate(chunks):
            bank = banks[cnt % len(banks)]
            cnt += 1
            pskey = ("bank", bank)
            ps = C.banks[bank]
            for k in range(8):
                P.pe(lambda e, ps=ps, ht=ht, k=k, c0=c0, m=m, n=n: e.matmul(ps[0:m, 0:n], lhsT=wb[:, k, c0:c0 + m], rhs=ht[:, k, 0:n],
                                                                            start=(k == 0), stop=(k == 7)),
                     reads=[("ht", wtag, b), ("wb", wtag, k)], writes=[pskey])
            evac(ti, t0, n, ci, ps, pskey)
        if after_tile is not None:
            after_tile(ti)


def phase_lru(C, H, win, lru_cw, lru_cb, lru_w, lru_b, lru_lam, mixT, row0):
    P, A, nc = C.P, C.A, C.nc
    m0 = A.mark()
    wb, mw = load_weights(C, win, NATT + NGDN, NLRU, "lru")
    cw = A.alloc([4]); cb = A.alloc([1]); lw = A.alloc([2, 2, 128]); lb = A.alloc([2, 2]); lam = A.alloc([2])
    cst = A.alloc([2]); lbh = A.alloc([2, 2]); qtr = A.alloc([1])
    P.pool(lambda e: e.memset(qtr, 0.25), writes=["qtr"])
    P.dma(lambda e: e.dma_start(out=cw, in_=lru_cw), writes=["cw"])
    P.dma(lambda e: e.dma_start(out=cb, in_=lru_cb), writes=["cb"])
    P.dma(lambda e: e.dma_start(out=lw, in_=lru_w), writes=["lw"])
    P.dma(lambda e: e.dma_start(out=lb, in_=lru_b), writes=["lb"])
    P.dma(lambda e: e.dma_start(out=lam, in_=lru_lam), writes=["lam"])
    P.act(lambda e: e.activation(out=cst, in_=lam, func=AF.Exp, scale=-1.0), reads=["lam"], writes=["cst"])
    P.act(lambda e: e.activation(out=cst, in_=cst, func=AF.Ln, bias=1.0), reads=["cst"], writes=["cst"])
    P.dve(lambda e: e.tensor_scalar(out=cst, in0=cst, scalar1=-4.0, scalar2=None, op0=ALU.mult), reads=["cst"], writes=["cst"])
    P.dve(lambda e: e.tensor_scalar(out=lbh, in0=lb, scalar1=0.5, scalar2=None, op0=ALU.mult), reads=["lb"], writes=["lbh"])
    xc = A.alloc([TT])
    zs = A.alloc([TT], BF16)
    hsum = A.alloc([TT])
    mxr = A.mark()
    xraw = A.alloc([TT])
    mp = A.mark()

    def evac(ti, t0, n, ci, ps, pskey):
        if ci == 0:
            P.act(lambda e: e.copy(out=xraw[:, t0:t0 + n], in_=ps[:, 0:n]), reads=[pskey], writes=[("xraw", ti)])
        else:
            P.act(lambda e: e.activation(out=zs[:, t0:t0 + n], in_=ps[:, 0:n], func=AF.Silu), reads=[pskey], writes=[("zs", ti)])

    project(C, H, wb, "lru", [(0, 128), (128, 128)], evac, banks=[0, 1, 2, 3])
    A.reset(mp)
    allx = [("xraw", ti) for ti in range(17)]
    for (s, e_) in ((0, T), (T, TT)):
        P.dve(lambda e, s=s, e_=e_: e.tensor_scalar(out=xc[:, s:e_], in0=xraw[:, s:e_], scalar1=cw[:, 2:3], scalar2=cb[:, 0:1],
                                                    op0=ALU.mult, op1=ALU.add), reads=allx + ["cw", "cb"], writes=["xc"])
        P.dve(lambda e, s=s, e_=e_: e.scalar_tensor_tensor(out=xc[:, s + 2:e_], in0=xraw[:, s:e_ - 2], scalar=cw[:, 0:1],
                                                           in1=xc[:, s + 2:e_], op0=ALU.mult, op1=ALU.add), reads=allx + ["xc"], writes=["xc"])
        P.dve(lambda e, s=s, e_=e_: e.scalar_tensor_tensor(out=xc[:, s + 1:e_], in0=xraw[:, s:e_ - 1], scalar=cw[:, 1:2],
                                                           in1=xc[:, s + 1:e_], op0=ALU.mult, op1=ALU.add), reads=allx + ["xc"], writes=["xc"])
        P.dve(lambda e, s=s, e_=e_: e.scalar_tensor_tensor(out=xc[:, s:e_ - 1], in0=xraw[:, s + 1:e_], scalar=cw[:, 3:4],
                                                           in1=xc[:, s:e_ - 1], op0=ALU.mult, op1=ALU.add), reads=allx + ["xc"], writes=["xc"])
    P.barrier()
    A.reset(mxr)
    GS = 4
    NB = 2 * GS
    tr = [A.alloc([512]) for _ in range(NB)]
    tiu = [A.alloc([512]) for _ in range(NB)]
    sq = [A.alloc([512]) for _ in range(NB)]
    hb = [A.alloc([512]) for _ in range(NB)]
    cnt = 0
    for d in range(2):
        order = [16] + list(range(16)) if d == 0 else [16] + list(range(15, -1, -1))
        st_ = dict(prev=None, prevti=None, prevb=None)
        for g0 in range(0, len(order), GS):
            grp = []
            for ti in order[g0:g0 + GS]:
                grp.append((ti, cnt % NB))
                cnt += 1
            for (ti, b) in grp:
                t0, n = TILES[ti]
                bk = 2 * (b % 4)
                psr = C.banks[bk]
                psi = C.banks[bk + 1]
                kr, ki = ("bank", bk), ("bank", bk + 1)
                P.pe(lambda e, psr=psr, d=d, t0=t0, n=n: e.matmul(psr[:, 0:n], lhsT=lw[:, d, 0, :], rhs=xc[:, t0:t0 + n], start=True, stop=True),
                     reads=["lw", "xc"], writes=[kr])
                P.pe(lambda e, psi=psi, d=d, t0=t0, n=n: e.matmul(psi[:, 0:n], lhsT=lw[:, d, 1, :], rhs=xc[:, t0:t0 + n], start=True, stop=True),
                     reads=["lw", "xc"], writes=[ki])
                a_, u_, s_ = tr[b], tiu[b], sq[b]
                P.act(lambda e, a_=a_, psr=psr, d=d, n=n: e.activation(out=a_[:, 0:n], in_=psr[:, 0:n], func=AF.Tanh, scale=0.5, bias=lbh[:, d, 0:1]),
                      reads=[kr, "lbh"], writes=[("tr", b)])
                P.act(lambda e, u_=u_, psi=psi, d=d, n=n: e.activation(out=u_[:, 0:n], in_=psi[:, 0:n], func=AF.Tanh, scale=0.5, bias=lbh[:, d, 1:2]),
                      reads=[ki, "lbh"], writes=[("tiu", b)])
                P.act(lambda e, a_=a_, d=d, n=n: e.activation(out=a_[:, 0:n], in_=a_[:, 0:n], func=AF.Exp, scale=cst[:, d:d + 1], bias=cst[:, d:d + 1]),
                      reads=[("tr", b), "cst"], writes=[("tr", b)])
                P.dve(lambda e, u_=u_, t0=t0, n=n: e.scalar_tensor_tensor(out=u_[:, 0:n], in0=u_[:, 0:n], scalar=1.0, in1=xc[:, t0:t0 + n],
                                                                         op0=ALU.add, op1=ALU.mult), reads=[("tiu", b), "xc"], writes=[("tiu", b)])
                P.dve(lambda e, a_=a_, s_=s_, n=n: e.scalar_tensor_tensor(out=s_[:, 0:n], in0=a_[:, 0:n], scalar=-0.25, in1=a_[:, 0:n],
                                                                         op0=ALU.mult, op1=ALU.mult), reads=[("tr", b)], writes=[("sq", b)])
            for (ti, b) in grp:
                t0, n = TILES[ti]
                a_, u_, s_, h_ = tr[b], tiu[b], sq[b], hb[b]
                P.act(lambda e, s_=s_, n=n: e.activation(out=s_[:, 0:n], in_=s_[:, 0:n], func=AF.Sqrt, bias=qtr[:, 0:1]),
                      reads=[("sq", b), "qtr"], writes=[("sq", b)])
                P.pool(lambda e, u_=u_, s_=s_, n=n: e.tensor_tensor(out=u_[:, 0:n], in0=u_[:, 0:n], in1=s_[:, 0:n], op=ALU.mult),
                       reads=[("tiu", b), ("sq", b)], writes=[("tiu", b)])
                prev = st_["prev"]
                if d == 0:
                    outv = hsum[:, t0:t0 + n]
                    init = 0.0 if prev is None else hsum[:, prev:prev + 1]
                    rd = [("tr", b), ("tiu", b)] + ([] if prev is None else [("hsum", st_["prevti"])])
                    P.dve(lambda e, outv=outv, a_=a_, u_=u_, n=n, init=init: e.tensor_tensor_scan(out=outv, data0=a_[:, 0:n], data1=u_[:, 0:n],
                                                                                                 initial=init, op0=ALU.mult, op1=ALU.add),
                          reads=rd, writes=[("hsum", ti)])
                    st_["prev"] = t0 + n - 1
                    st_["prevti"] = ti
                else:
                    init = 0.0 if prev is None else prev
                    rd = [("tr", b), ("tiu", b)] + ([] if prev is None else [("hb", st_["prevb"])])
                    P.dve(lambda e, h_=h_, a_=a_, u_=u_, n=n, init=init: e.tensor_tensor_scan(out=revap(h_[:, 0:n]), data0=revap(a_[:, 0:n]),
                                                                                            data1=revap(u_[:, 0:n]), initial=init,
                                                                                            op0=ALU.mult, op1=ALU.add),
                          reads=rd, writes=[("hb", b)])
                    P.pool(lambda e, h_=h_, t0=t0, n=n: e.tensor_tensor(out=hsum[:, t0:t0 + n], in0=hsum[:, t0:t0 + n], in1=h_[:, 0:n], op=ALU.add),
                           reads=[("hb", b), ("hsum", ti)], writes=[("hsum", ti)])
                    st_["prev"] = h_[:, 0:1]
                    st_["prevb"] = b
    ob = [A.alloc([512], BF16) for _ in range(2)]
    for ti, (t0, n) in enumerate(TILES):
        b = ti % 2
        o_ = ob[b]
        P.dve(lambda e, o_=o_, t0=t0, n=n: e.tensor_tensor(out=o_[:, 0:n], in0=hsum[:, t0:t0 + n], in1=zs[:, t0:t0 + n], op=ALU.mult),
              reads=[("hsum", ti), ("zs", ti)], writes=[("ob", b)])
        P.dma(lambda e, o_=o_, t0=t0, n=n: e.dma_start(out=mixT[row0:row0 + 128, t0:t0 + n], in_=o_[:, 0:n]), reads=[("ob", b)], writes=[("mixT", "lru", ti)])
    A.reset(m0)


OFF = dict(att_q=0, att_k=512, att_v=640, att_z=768, gdn_q=1280, gdn_k=1792, gdn_v=2304, gdn_b=2816, gdn_a=2824,
           gdn_z=2832, lru_x=3344, lru_z=3856)


def core_cols(j):
    cols = []
    cols += list(range(OFF["att_q"] + 128 * j, OFF["att_q"] + 128 * j + 128))
    g = j // 2
    cols += list(range(OFF["att_k"] + 64 * g, OFF["att_k"] + 64 * g + 64))
    cols += list(range(OFF["att_z"] + 128 * j, OFF["att_z"] + 128 * j + 128))
    cols += list(range(OFF["att_v"] + 64 * g, OFF["att_v"] + 64 * g + 64))
    for nm in ("gdn_q", "gdn_k", "gdn_v", "gdn_z"):
        cols += list(range(OFF[nm] + 128 * j, OFF[nm] + 128 * j + 128))
    cols += [OFF["gdn_b"] + j, OFF["gdn_b"] + 4 + j, OFF["gdn_a"] + j, OFF["gdn_a"] + 4 + j]
    cols += list(range(OFF["lru_x"] + 128 * j, OFF["lru_x"] + 128 * j + 128))
    cols += list(range(OFF["lru_z"] + 128 * j, OFF["lru_z"] + 128 * j + 128))
    assert len(cols) == NATT + NGDN + NLRU
    return np.array(cols)


def pk(v):
    v = np.asarray(v)
    return np.ascontiguousarray(v.reshape(8, 128, -1).transpose(1, 0, 2))


def prep_A(inp, l, b, j, xT_full):
    f = np.float32
    d = {}
    if xT_full is not None:
        d["xT"] = xT_full
    d["cv"] = pk(np.stack([inp["c"][b], inp["c_ctx"]], axis=1).astype(f))
    d["wmod"] = pk(inp["w_mod"][l][:, 0:2048])
    d["bmod"] = np.ascontiguousarray(inp["b_mod"][l][0:2048].reshape(16, 128).T)
    d["ng"] = np.ascontiguousarray(inp["norm_g"][l].reshape(8, 128).T)
    d["win"] = pk(inp["w_in"][l][:, core_cols(j)])
    ch = slice(128 * j, 128 * j + 128)
    d["lru_cw"] = np.ascontiguousarray(inp["lru_conv_w"][l][:, ch].T)
    d["lru_cb"] = np.ascontiguousarray(inp["lru_conv_b"][l][ch].reshape(128, 1))
    lw = np.zeros((128, 2, 2, 128), f)
    for dd in range(2):
        for gi, nm in enumerate(("lru_w_r", "lru_w_i")):
            for bb in range(2):
                lw[64 * bb:64 * bb + 64, dd, gi, 64 * bb:64 * bb + 64] = inp[nm][l][dd][2 * j + bb]
    d["lru_w"] = lw
    lb = np.zeros((128, 2, 2), f)
    for dd in range(2):
        lb[:, dd, 0] = inp["lru_b_r"][l][dd][ch]
        lb[:, dd, 1] = inp["lru_b_i"][l][dd][ch]
    d["lru_b"] = lb
    d["lru_lam"] = np.ascontiguousarray(inp["lru_lambda"][l][:, ch].T)
    return d


def tiles_of(t0, t1):
    return list(range(t0 // 512, (t1 - 1) // 512 + 1))


def phase_att(C, H, win, ropeC, ropeS, rotT_d, e65_d, mask_d, ident_d, sink_d, mixT, want_ctx):
    P, A, nc = C.P, C.A, C.nc
    m0 = A.mark()
    SC = 0.125
    wb, _ = load_weights(C, win, 0, NATT, "att")
    qA = [A.alloc([TT], BF16) for _ in range(2)]
    kA = A.alloc([TT], BF16)
    zs = A.alloc([TT], BF16)
    Vt = A.alloc([66, 65], BF16)
    rotT = A.alloc([64]); e65 = A.alloc([65]); mask = A.alloc([384], BF16); ident = A.alloc([128])
    sk = A.alloc([2]); vsink = A.alloc([2, 65], BF16); ksink = A.alloc([1], BF16)
    kmx = A.alloc([17]); negK = A.alloc([1])
    P.dma(lambda e: e.dma_start(out=rotT[0:64, :], in_=rotT_d), writes=["rotT"])
    P.dma(lambda e: e.dma_start(out=e65[0:64, :], in_=e65_d), writes=["e65"])
    P.dma(lambda e: e.dma_start(out=mask, in_=mask_d), writes=["mask"])
    P.dma(lambda e: e.dma_start(out=ident, in_=ident_d), writes=["ident"])
    P.dma(lambda e: e.dma_start(out=sk[0:1, :], in_=sink_d), writes=["sk"])
    P.pool(lambda e: e.memset(vsink[0:1, :, :], 0.0), writes=["vsink"])
    P.act(lambda e: e.activation(out=vsink[0:1, :, 64], in_=sk[0:1, :], func=AF.Exp), reads=["sk", "vsink"], writes=["vsink"])
    P.pool(lambda e: e.memset(ksink[0:64, :], 0.0), writes=["ksink"])
    P.pool(lambda e: e.memset(ksink[64:65, :], 1.0), reads=["ksink"], writes=["ksink"])
    P.pool(lambda e: e.memset(kA[64:65, :], 1.0), writes=["kA1"])
    P.pool(lambda e: e.memset(Vt[:, :, 64:65], 1.0), writes=["Vt1"])
    NB = 2
    qf = [A.alloc([512]) for _ in range(NB)]
    qsq = [A.alloc([512]) for _ in range(NB)]
    t1 = [A.alloc([512]) for _ in range(NB)]
    t2 = [A.alloc([512]) for _ in range(NB)]
    ct = [A.alloc([512]) for _ in range(2)]
    stt = [A.alloc([512]) for _ in range(2)]
    ev = dict(n=0)

    vf = [A.alloc([512]) for _ in range(2)]

    def evac(ti, t0, n, ci, ps, pskey):
        lat = ti < 16
        if ci == 3:
            P.act(lambda e: e.activation(out=zs[:, t0:t0 + n], in_=ps[:, 0:n], func=AF.Silu), reads=[pskey], writes=[("zs", ti)])
            return
        if ci == 4:
            vb_ = ti % 2
            v_ = vf[vb_]
            nb = n // 128
            P.dve(lambda e: e.tensor_copy(out=v_[0:64, 0:n], in_=ps[0:64, 0:n]), reads=[pskey], writes=[("vf", vb_)])
            tb_ = 5 + vb_
            tps = C.banks[tb_]
            for bl in range(nb):
                P.pe(lambda e, bl=bl: e.transpose(tps[:, bl * 64:(bl + 1) * 64], v_[0:64, bl * 128:(bl + 1) * 128], ident[0:64, 0:64]),
                     reads=[("vf", vb_), "ident"], writes=[("bank", tb_)])
            P.act(lambda e: e.copy(out=Vt[:, 4 * ti:4 * ti + nb, 0:64], in_=tps[:, 0:nb * 64].rearrange("p (a b) -> p a b", a=nb, b=64)),
                  reads=[("bank", tb_)], writes=[("Vt", ti)])
            return
        if ci == 0 and lat:
            cb = ti % 2
            P.dma(lambda e: e.dma_start(out=ct[cb][0:64, 0:n], in_=ropeC[:, t0:t0 + n]), writes=[("ct", cb)])
            P.dma(lambda e: e.dma_start(out=stt[cb][0:64, 0:n], in_=ropeS[:, t0:t0 + n]), writes=[("st", cb)])
        b = ev["n"] % NB
        ev["n"] += 1
        dst = qA[ci] if ci < 2 else kA
        dkey = ("qA", ci, ti) if ci < 2 else ("kA", ti)
        f_, s_, a_, b_ = qf[b], qsq[b], t1[b], t2[b]
        rb = 6 + (b % 2)
        rps = C.banks[rb]
        rkey = ("bank", rb)
        P.act(lambda e: e.activation(out=s_[0:64, 0:n], in_=ps[0:64, 0:n], func=AF.Square), reads=[pskey], writes=[("qsq", b)])
        if lat:
            P.act(lambda e: e.copy(out=f_[0:64, 0:n], in_=ps[0:64, 0:n]), reads=[pskey], writes=[("qf", b)])
            P.pe(lambda e: e.matmul(rps[0:64, 0:n], lhsT=rotT[0:64, :], rhs=f_[0:64, 0:n], start=True, stop=True),
                 reads=["rotT", ("qf", b)], writes=[rkey])
            cb = ti % 2
            P.dve(lambda e: e.tensor_tensor(out=a_[0:64, 0:n], in0=f_[0:64, 0:n], in1=ct[cb][0:64, 0:n], op=ALU.mult),
                  reads=[("qf", b), ("ct", cb)], writes=[("t1", b)])
            P.dve(lambda e: e.tensor_tensor(out=b_[0:64, 0:n], in0=rps[0:64, 0:n], in1=stt[cb][0:64, 0:n], op=ALU.mult),
                  reads=[rkey, ("st", cb)], writes=[("t2", b)])
            P.pool(lambda e: e.tensor_tensor(out=dst[0:64, t0:t0 + n], in0=a_[0:64, 0:n], in1=b_[0:64, 0:n], op=ALU.add),
                   reads=[("t1", b), ("t2", b)], writes=[dkey])
        else:
            P.act(lambda e: e.copy(out=dst[0:64, t0:t0 + n], in_=ps[0:64, 0:n]), reads=[pskey], writes=[dkey])
        sb_ = 4
        sps = C.banks[sb_]
        skey = ("bank", sb_)
        P.pe(lambda e: e.matmul(sps[0:65, 0:n], lhsT=e65[0:64, :], rhs=s_[0:64, 0:n], start=True, stop=True),
             reads=["e65", ("qsq", b)], writes=[skey])
        if ci < 2:
            P.act(lambda e: e.activation(out=dst[64:65, t0:t0 + n], in_=sps[64:65, 0:n], func=AF.Sqrt), reads=[skey], writes=[("qAn", ci, ti)])
        else:
            P.dve(lambda e: e.reduce_max(out=kmx[64:65, ti:ti + 1], in_=sps[64:65, 0:n], axis=AX.X), reads=[skey], writes=[("kmx", ti)])

    project(C, H, wb, "att", [(0, 64), (64, 64), (128, 64), (192, 128), (320, 64)], evac, banks=[0, 1, 2, 3])
    P.dve(lambda e: e.reduce_max(out=negK[64:65, :], in_=kmx[64:65, :], axis=AX.X), reads=[("kmx", ti) for ti in range(17)], writes=["negK"])
    P.act(lambda e: e.activation(out=negK[64:65, :], in_=negK[64:65, :], func=AF.Sqrt), reads=["negK"], writes=["negK"])
    P.dve(lambda e: e.tensor_scalar(out=negK[64:65, :], in0=negK[64:65, :], scalar1=-(1.0 + 2.0 ** -7), scalar2=None, op0=ALU.mult),
          reads=["negK"], writes=["negK"])
    for h in range(2):
        P.dve(lambda e, h=h: e.tensor_scalar(out=qA[h][64:65, :], in0=qA[h][64:65, :], scalar1=negK[64:65, 0:1], scalar2=None, op0=ALU.mult),
              reads=["negK"] + [("qAn", h, ti) for ti in range(17)], writes=[("qAn", h, ti) for ti in range(17)])
    PT = [[A.alloc([384], BF16) for _ in range(4)] for h in range(2)]
    PTc = [[A.alloc([2, 512], BF16) for _ in range(2)] for h in range(2)]
    PTs = [[A.alloc([512], BF16) for _ in range(2)] for h in range(2)]
    rden = [A.alloc([2]) for _ in range(2)]
    on = [A.alloc([4, 128]) for _ in range(2)]
    ob = [A.alloc([512], BF16) for _ in range(2)]
    stc = dict(n=0, c=0)

    def qkeys(h, u0, u1):
        r = []
        for ti in tiles_of(u0, u1):
            r += [("qA", h, ti), ("qAn", h, ti)]
        return r

    def ctx_scores(qt, q0, nq):
        b = qt % 2
        for h in range(2):
            for cc in range(2):
                bank = 3 + (stc["c"] % 2)
                stc["c"] += 1
                ps = C.banks[bank]
                P.pe(lambda e, ps=ps, h=h, cc=cc: e.matmul(ps[:, 0:nq], lhsT=kA[0:65, T + 128 * cc:T + 128 * cc + 128], rhs=qA[h][0:65, q0:q0 + nq],
                                                           start=True, stop=True),
                     reads=[("kA", 16), "kA1"] + qkeys(h, q0, q0 + nq), writes=[("bank", bank)])
                P.act(lambda e, ps=ps, h=h, cc=cc: e.activation(out=PTc[h][b][:, cc, 0:nq], in_=ps[:, 0:nq], func=AF.Exp, scale=SC),
                      reads=[("bank", bank)], writes=[("PTc", h, b, cc)])
            bank = 3 + (stc["c"] % 2)
            stc["c"] += 1
            ps = C.banks[bank]
            P.pe(lambda e, ps=ps, h=h: e.matmul(ps[0:1, 0:nq], lhsT=ksink[0:65, 0:1], rhs=qA[h][0:65, q0:q0 + nq], start=True, stop=True),
                 reads=["ksink"] + qkeys(h, q0, q0 + nq), writes=[("bank", bank)])
            P.act(lambda e, ps=ps, h=h: e.activation(out=PTs[h][b][0:1, 0:nq], in_=ps[0:1, 0:nq], func=AF.Exp, scale=SC),
                  reads=[("bank", bank)], writes=[("PTs", h, b)])

    def pv_block(blk, qt, qoff, locals_):
        b = qt % 2
        g = blk // 4
        gb = g % 2
        obank = 5 + (blk % 2)
        ops_ = C.banks[obank]
        for h in range(2):
            mm = []
            for (c, co) in locals_:
                mm.append((PT[h][c % 4][:, co:co + 128], Vt[:, c, :], [("PT", h, c % 4), ("Vt", c // 4), "Vt1"]))
            for cc in range(2):
                mm.append((PTc[h][b][:, cc, qoff:qoff + 128], Vt[:, 64 + cc, :], [("PTc", h, b, cc), ("Vt", 16), "Vt1"]))
            mm.append((PTs[h][b][0:1, qoff:qoff + 128], vsink[0:1, h, :], [("PTs", h, b), "vsink"]))
            for i, (l_, r_, rk) in enumerate(mm):
                P.pe(lambda e, l_=l_, r_=r_, i=i, h=h, last=(i == len(mm) - 1): e.matmul(ops_[:, h * 65:(h + 1) * 65], lhsT=l_, rhs=r_,
                                                                                         start=(i == 0), stop=last),
                     reads=rk, writes=[("bank", obank)])
        o3 = ops_[:, 0:130].rearrange("p (a b) -> p a b", a=2, b=65)
        rd = rden[blk % 2]
        P.dve(lambda e: e.reciprocal(out=rd, in_=o3[:, :, 64]), reads=[("bank", obank)], writes=[("rden", blk % 2)])
        for h in range(2):
            P.dve(lambda e, h=h: e.tensor_scalar(out=on[gb][:, blk % 4, h * 64:(h + 1) * 64], in0=o3[:, h, 0:64], scalar1=rd[:, h:h + 1],
                                                 scalar2=None, op0=ALU.mult),
                  reads=[("bank", obank), ("rden", blk % 2)], writes=[("on", gb, blk % 4)])

    def flush_group(g, nblk):
        gb = g % 2
        tps = C.banks[7]
        for i in range(nblk):
            P.pe(lambda e, i=i: e.transpose(tps[:, i * 128:(i + 1) * 128], on[gb][:, i, :], ident), reads=[("on", gb, i), "ident"], writes=[("bank", 7)])
        t0 = 512 * g
        n = 128 * nblk
        o_ = ob[gb]
        P.dve(lambda e: e.tensor_tensor(out=o_[:, 0:n], in0=tps[:, 0:n], in1=zs[:, t0:t0 + n], op=ALU.mult),
              reads=[("bank", 7), ("zs", g)], writes=[("ob", gb)])
        P.dma(lambda e: e.dma_start(out=mixT[0:128, t0:t0 + n], in_=o_[:, 0:n]), reads=[("ob", gb)], writes=[("mixT", "att", g)])

    def do_pv(n):
        locs = [(c, 128 * (n - c + 1)) for c in (n - 1, n, n + 1) if 0 <= c < 64]
        pv_block(n, n // 4, 128 * (n % 4), locs)
        if n % 4 == 3:
            flush_group(n // 4, 4)

    for c in range(64):
        if c % 4 == 0:
            ctx_scores(c // 4, 512 * (c // 4), 512)
        u0 = max(0, 128 * (c - 1))
        u1 = min(T, 128 * (c + 2))
        n = u1 - u0
        mo = u0 - 128 * (c - 1)
        for h in range(2):
            bank = stc["n"] % 3
            stc["n"] += 1
            ps = C.banks[bank]
            pt = PT[h][c % 4]
            P.pe(lambda e, ps=ps, h=h, c=c, u0=u0, u1=u1, n=n: e.matmul(ps[:, 0:n], lhsT=kA[0:65, 128 * c:128 * c + 128], rhs=qA[h][0:65, u0:u1],
                                                                        start=True, stop=True),
                 reads=[("kA", c // 4), "kA1"] + qkeys(h, u0, u1), writes=[("bank", bank)])
            P.act(lambda e, ps=ps, pt=pt, n=n, mo=mo: e.activation(out=pt[:, mo:mo + n], in_=ps[:, 0:n], func=AF.Exp, scale=SC),
                  reads=[("bank", bank)], writes=[("PT", h, c % 4)])
            P.pool(lambda e, pt=pt, n=n, mo=mo: e.tensor_tensor(out=pt[:, mo:mo + n], in0=pt[:, mo:mo + n], in1=mask[:, mo:mo + n], op=ALU.mult),
                   reads=[("PT", h, c % 4), "mask"], writes=[("PT", h, c % 4)])
        if c >= 2:
            do_pv(c - 2)
    do_pv(62)
    do_pv(63)
    if want_ctx:
        ctx_scores(16, T, 256)
        pv_block(64, 16, 0, [])
        pv_block(65, 16, 128, [])
        flush_group(16, 2)
    A.reset(m0)


def att_consts():
    f = np.float32
    t = np.arange(T)
    row = (t // 64).astype(np.float64)
    col = (t % 64).astype(np.float64)
    inv = (10000.0 ** (-np.arange(16, dtype=np.float32) / 16)).astype(np.float32).astype(np.float64)
    Cc = np.zeros((64, T), f); Ss = np.zeros((64, T), f)
    for axis, pos in enumerate((row, col)):
        ang = (pos[None, :].astype(np.float32) * inv[:, None].astype(np.float32)).astype(np.float32)
        for half in range(2):
            sl = slice(axis * 32 + half * 16, axis * 32 + half * 16 + 16)
            Cc[sl] = np.cos(ang); Ss[sl] = np.sin(ang)
    rotT = np.zeros((64, 64), f)
    for axis in range(2):
        for fq in range(16):
            a = axis * 32 + fq; b_ = axis * 32 + 16 + fq
            rotT[b_, a] = -1.0
            rotT[a, b_] = 1.0
    e65 = np.zeros((64, 65), f); e65[:, 64] = 1.0
    r = np.arange(128)[:, None]; u = np.arange(384)[None, :]
    mask = (np.abs(u - 128 - r) <= 128).astype(ml_dtypes.bfloat16)
    ident = np.eye(128, dtype=f)
    return dict(ropeC=Cc, ropeS=Ss, rotT=rotT, e65=e65, mask=mask, ident=ident)


def prep_att(inp, l, j):
    d = att_consts()
    d["sink"] = np.ascontiguousarray(inp["att_sink"][l][2 * j:2 * j + 2].reshape(1, 2).astype(np.float32))
    return d


def merge_gens(gens):
    gens = [g for g in gens if g is not None]
    while gens:
        nxt = []
        for g in gens:
            try:
                next(g)
                nxt.append(g)
            except StopIteration:
                pass
        gens = nxt


def gdn_consts():
    f = np.float32
    p = np.arange(GCH)[:, None]; q = np.arange(GCH)[None, :]
    bd = (p // 64) == (q // 64)
    m = np.stack([(p < q), (p <= q), (p > q), (p >= q), bd, ~bd], axis=1).astype(f)
    return dict(gmask=np.ascontiguousarray(m), ident=np.eye(128, dtype=f))


def prep_gdn(inp, l, j):
    f = np.float32
    d = gdn_consts()
    cw = np.zeros((128, 3, 4), f)
    for sec in range(3):
        cw[:, sec, :] = inp["gdn_conv"][l][:, sec * 512 + 128 * j: sec * 512 + 128 * j + 128].T
    d["gdn_cw"] = cw
    d["gdn_par"] = np.array([[inp["gdn_a_log"][l][0][j], inp["gdn_a_log"][l][1][j],
                              inp["gdn_dt_bias"][l][0][j], inp["gdn_dt_bias"][l][1][j]]], f)
    d["gdn_nw"] = np.ascontiguousarray(inp["gdn_norm"][l].reshape(128, 1).astype(f))
    return d


def phase_gdn(C, H, win, gdn_cw, gdn_par, gdn_nw, gmask_d, ident_d, mixT, scr, stop=99, grow0=128, after_tile=None):
    P, A, nc = C.P, C.A, C.nc
    m0 = A.mark()
    raw, prc, zsD, baD, rowsD, glD, OD = scr["raw"], scr["prc"], scr["zsD"], scr["ba"], scr["rows"], scr["gl"], scr["OD"]
    prcb = scr["prcb"]
    ident = A.alloc([128]); gmask = A.alloc([6, GCH]); cw = A.alloc([3, 4]); par = A.alloc([4]); nw = A.alloc([1])
    onesf = A.alloc([128]); negA = A.alloc([2]); epsb = A.alloc([1])
    P.dma(lambda e: e.dma_start(out=ident, in_=ident_d), writes=["ident"])
    P.dma(lambda e: e.dma_start(out=gmask[0:GCH], in_=gmask_d), writes=["gmask"])
    P.dma(lambda e: e.dma_start(out=cw, in_=gdn_cw), writes=["gcw"])
    P.dma(lambda e: e.dma_start(out=par, in_=gdn_par[0, :].partition_broadcast(128)), writes=["par"])
    P.dma(lambda e: e.dma_start(out=nw, in_=gdn_nw), writes=["nw"])
    P.pool(lambda e: e.memset(onesf, 1.0), writes=["onesf"])
    P.pool(lambda e: e.memset(epsb, EPS), writes=["epsb"])
    P.act(lambda e: e.activation(out=negA, in_=par[:, 0:2], func=AF.Exp), reads=["par"], writes=["negA"])
    P.dve(lambda e: e.tensor_scalar(out=negA, in0=negA, scalar1=-1.0, scalar2=None, op0=ALU.mult), reads=["negA"], writes=["negA"])
    mk2 = A.mark()
    wb, _ = load_weights(C, win, NATT, NGDN, "gdn")
    stz = [A.alloc([512], BF16) for _ in range(2)]
    stb = [A.alloc([512]) for _ in range(2)]
    RW = TT + 8
    rawS = A.alloc([3, RW])
    def col(t):
        return t + 2 if t < T else t + 5
    for (c0_, c1_) in ((0, 2), (T + 2, T + 5), (TT + 5, TT + 8)):
        P.pool(lambda e, c0_=c0_, c1_=c1_: e.memset(rawS[:, :, c0_:c1_], 0.0), writes=[("rawpad", c0_)])
    cv_ = [A.alloc([3, 512]) for _ in range(2)]
    sq_ = [A.alloc([2, 512]) for _ in range(2)]
    rs_ = [A.alloc([2, 512]) for _ in range(2)]
    cvb_ = [A.alloc([2, 512], BF16) for _ in range(2)]
    ev = dict(n=0, z=0, b=0)

    def evac(ti, t0, n, ci, ps, pskey):
        if ci < 3:
            cc_ = col(t0)
            if ci != 1:
                P.act(lambda e: e.copy(out=rawS[:, ci, cc_:cc_ + n], in_=ps[:, 0:n]), reads=[pskey], writes=[("raws", ti, ci)])
            else:
                P.dve(lambda e: e.tensor_copy(out=rawS[:, ci, cc_:cc_ + n], in_=ps[:, 0:n]), reads=[pskey], writes=[("raws", ti, ci)])
        elif ci == 3:
            b = ev["z"] % 2
            ev["z"] += 1
            s_ = stz[b]
            P.act(lambda e: e.activation(out=s_[:, 0:n], in_=ps[:, 0:n], func=AF.Silu), reads=[pskey], writes=[("stz", b)])
            P.dma(lambda e: e.dma_start(out=zsD[:, t0:t0 + n], in_=s_[:, 0:n]), reads=[("stz", b)], writes=[("zsD", ti)])
        else:
            b = ev["b"] % 2
            ev["b"] += 1
            s_ = stb[b]
            P.dve(lambda e: e.tensor_copy(out=s_[0:4, 0:n], in_=ps[0:4, 0:n]), reads=[pskey], writes=[("stb", b)])
            P.dma(lambda e: e.dma_start(out=baD[:, t0:t0 + n], in_=s_[0:4, 0:n]), reads=[("stb", b)], writes=[("baD", ti)])

    def stage2(ti):
        t0, n = TILES[ti]
        b = ti % 2
        c_, s_, r_ = cv_[b], sq_[b], rs_[b]
        cb = col(t0) - 2
        nb_ = [tj for tj in (ti - 1, ti, ti + 1) if 0 <= tj <= 16 and (tj < 16) == (ti < 16)]
        for sec in range(3):
            rk = [("raws", tj, sec) for tj in nb_] + [("rawpad", 0), ("rawpad", T + 2), ("rawpad", TT + 5), "gcw"]
            P.dve(lambda e, c_=c_, sec=sec: e.tensor_scalar(out=c_[:, sec, 0:n], in0=rawS[:, sec, cb + 2:cb + 2 + n], scalar1=cw[:, sec, 2:3], scalar2=None,
                                                           op0=ALU.mult), reads=rk, writes=[("gcv", b, sec)])
            for k in (0, 1, 3):
                P.dve(lambda e, c_=c_, sec=sec, k=k: e.scalar_tensor_tensor(out=c_[:, sec, 0:n], in0=rawS[:, sec, cb + k:cb + k + n], scalar=cw[:, sec, k:k + 1],
                                                                         in1=c_[:, sec, 0:n], op0=ALU.mult, op1=ALU.add),
                      reads=rk + [("gcv", b, sec)], writes=[("gcv", b, sec)])
        P.act(lambda e: e.activation(out=c_[:, :, 0:n], in_=c_[:, :, 0:n], func=AF.Silu),
              reads=[("gcv", b, s) for s in range(3)], writes=[("gcv", b, s) for s in range(3)])
        P.act(lambda e: e.activation(out=s_[:, :, 0:n], in_=c_[:, 0:2, 0:n], func=AF.Square),
              reads=[("gcv", b, 0), ("gcv", b, 1)], writes=[("gsq", b)])
        for sec in range(2):
            bank = 6 + sec
            ps = C.banks[bank]
            P.pe(lambda e, ps=ps, sec=sec: e.matmul(ps[:, 0:n], lhsT=onesf, rhs=s_[:, sec, 0:n], start=True, stop=True),
                 reads=[("gsq", b), "onesf"], writes=[("bank", bank)])
            P.act(lambda e, ps=ps, sec=sec: e.activation(out=r_[:, sec, 0:n], in_=ps[:, 0:n], func=AF.Ln, bias=epsb[:, 0:1]),
                  reads=[("bank", bank), "epsb"], writes=[("grs", b, sec)])
            P.act(lambda e, sec=sec: e.activation(out=r_[:, sec, 0:n], in_=r_[:, sec, 0:n], func=AF.Exp, scale=-0.5),
                  reads=[("grs", b, sec)], writes=[("grs", b, sec)])
            sc_ = (128.0 ** -0.5) if sec == 0 else 1.0
            P.dve(lambda e, sec=sec, sc_=sc_: e.scalar_tensor_tensor(out=c_[:, sec, 0:n], in0=c_[:, sec, 0:n], scalar=sc_, in1=r_[:, sec, 0:n],
                                                                  op0=ALU.mult, op1=ALU.mult),
                  reads=[("gcv", b, sec), ("grs", b, sec)], writes=[("gcv", b, sec)])
        P.dma(lambda e: e.dma_start(out=prc[:, :, t0:t0 + n].rearrange("s p t -> p s t"), in_=c_[:, :, 0:n]),
              reads=[("gcv", b, s) for s in range(3)], writes=[("prc", ti)])
        cb2 = cvb_[b]
        P.pool(lambda e: e.tensor_copy(out=cb2[:, :, 0:n], in_=c_[:, 0:2, 0:n]), reads=[("gcv", b, 0), ("gcv", b, 1)], writes=[("gcvb", b)])
        P.dma(lambda e: e.dma_start(out=prcb[:, :, t0:t0 + n].rearrange("s p t -> p s t"), in_=cb2[:, :, 0:n]),
              reads=[("gcvb", b)], writes=[("prcb", ti)])

    def hook(ti):
        if 1 <= ti <= 15:
            stage2(ti - 1)
        if ti == 15:
            stage2(15)
        if ti == 16:
            stage2(16)

    project(C, H, wb, "gdn", [(0, 128), (128, 128), (256, 128), (384, 128), (512, 4)], evac, banks=[0, 1, 2, 3, 4, 5], after_tile=hook)
    A.reset(mk2)
    P.barrier()
    if stop <= 2:
        return
    CH = GCH
    NG = 512 // CH
    NLAT = T // CH
    NCTX = L // CH
    NCH = NLAT + NCTX
    NLEV = 5
    colT = [[A.alloc([NCH]) for _ in range(4)] for d in range(2)]
    glb = [A.alloc([NCH]) for d in range(2)]
    mk3 = A.mark()
    segs = [(0, NLAT, 0), (NLAT, NCTX, 1)]
    for (ch0, npp, si) in segs:
        X = A.alloc([4, CH])
        P.dma(lambda e, X=X, ch0=ch0, npp=npp: e.dma_start(out=X[0:npp], in_=baD[:, CH * ch0:CH * (ch0 + npp)].rearrange("r (c i) -> c r i", i=CH)),
              reads=[("baD", x) for x in range(17)], writes=[("X", si)])
        for d in range(2):
            bet = A.alloc([CH]); xx = A.alloc([CH]); tt = A.alloc([CH]); G = A.alloc([CH]); q2 = A.alloc([CH]); q3 = A.alloc([CH]); gl = A.alloc([1])
            kx = ("sc", si, d)
            P.act(lambda e, bet=bet, X=X, d=d, npp=npp: e.activation(out=bet[0:npp], in_=X[0:npp, d, :], func=AF.Sigmoid), reads=[("X", si)], writes=[kx + ("bet",)])
            P.act(lambda e, tt=tt, X=X, d=d, npp=npp: e.activation(out=tt[0:npp], in_=X[0:npp, 2 + d, :], func=AF.Abs, bias=par[0:npp, 2 + d:3 + d]),
                  reads=[("X", si), "par"], writes=[kx + ("tt",)])
            P.act(lambda e, tt=tt, npp=npp: e.activation(out=tt[0:npp], in_=tt[0:npp], func=AF.Exp, scale=-1.0), reads=[kx + ("tt",)], writes=[kx + ("tt",)])
            P.act(lambda e, tt=tt, npp=npp: e.activation(out=tt[0:npp], in_=tt[0:npp], func=AF.Ln, bias=1.0), reads=[kx + ("tt",)], writes=[kx + ("tt",)])
            P.dve(lambda e, xx=xx, X=X, d=d, npp=npp: e.tensor_scalar(out=xx[0:npp], in0=X[0:npp, 2 + d, :], scalar1=par[0:npp, 2 + d:3 + d], scalar2=0.0,
                                                                     op0=ALU.add, op1=ALU.max), reads=[("X", si), "par"], writes=[kx + ("xx",)])
            P.dve(lambda e, xx=xx, tt=tt, npp=npp: e.tensor_tensor(out=xx[0:npp], in0=xx[0:npp], in1=tt[0:npp], op=ALU.add),
                  reads=[kx + ("xx",), kx + ("tt",)], writes=[kx + ("xx",)])
            P.dve(lambda e, xx=xx, d=d, npp=npp: e.tensor_scalar(out=xx[0:npp], in0=xx[0:npp], scalar1=negA[0:npp, d:d + 1], scalar2=None, op0=ALU.mult),
                  reads=[kx + ("xx",), "negA"], writes=[kx + ("xx",)])
            if d == 0:
                P.dve(lambda e, G=G, xx=xx, npp=npp: e.tensor_tensor_scan(out=G[0:npp], data0=onesf[0:npp, 0:CH], data1=xx[0:npp], initial=0.0, op0=ALU.mult, op1=ALU.add),
                      reads=[kx + ("xx",), "onesf"], writes=[kx + ("G",)])
                gle = G[0:npp, CH - 1:CH]
            else:
                P.dve(lambda e, G=G, xx=xx, npp=npp: e.tensor_tensor_scan(out=revap(G[0:npp]), data0=onesf[0:npp, 0:CH], data1=revap(xx[0:npp]), initial=0.0,
                                                                          op0=ALU.mult, op1=ALU.add),
                      reads=[kx + ("xx",), "onesf"], writes=[kx + ("G",)])
                gle = G[0:npp, 0:1]
            P.act(lambda e, q2=q2, G=G, npp=npp: e.activation(out=q2[0:npp], in_=G[0:npp], func=AF.Exp), reads=[kx + ("G",)], writes=[kx + ("q2",)])
            P.dve(lambda e, q2=q2, bet=bet, npp=npp: e.tensor_tensor(out=q2[0:npp], in0=q2[0:npp], in1=bet[0:npp], op=ALU.mult),
                  reads=[kx + ("q2",), kx + ("bet",)], writes=[kx + ("q2",)])
            P.act(lambda e, q3=q3, G=G, gle=gle, npp=npp: e.activation(out=q3[0:npp], in_=G[0:npp], func=AF.Exp, scale=-1.0, bias=gle),
                  reads=[kx + ("G",)], writes=[kx + ("q3",)])
            P.act(lambda e, gl=gl, gle=gle, npp=npp: e.activation(out=gl[0:npp], in_=gle, func=AF.Exp), reads=[kx + ("G",)], writes=[kx + ("gl",)])
            P.dma(lambda e, G=G, d=d, ch0=ch0, npp=npp: e.dma_start(out=rowsD[d, 0, CH * ch0:CH * (ch0 + npp)].rearrange("(c i) -> c i", i=CH), in_=G[0:npp]),
                  reads=[kx + ("G",)], writes=[("rowsD", d, si, 0)])
            P.dma(lambda e, bet=bet, d=d, ch0=ch0, npp=npp: e.dma_start(out=rowsD[d, 1, CH * ch0:CH * (ch0 + npp)].rearrange("(c i) -> c i", i=CH), in_=bet[0:npp]),
                  reads=[kx + ("bet",)], writes=[("rowsD", d, si, 1)])
            P.dma(lambda e, gl=gl, d=d, ch0=ch0, npp=npp: e.dma_start(out=glD[d, ch0:ch0 + npp].rearrange("(c i) -> c i", i=1), in_=gl[0:npp]),
                  reads=[kx + ("gl",)], writes=[("glD", d, si)])
            for qi, src in enumerate((G, bet, q2, q3)):
                bank = qi
                ps = C.banks[bank]
                nm = ("G", "bet", "q2", "q3")[qi]
                P.pe(lambda e, ps=ps, src=src, npp=npp: e.transpose(ps[0:CH, 0:npp], src[0:npp, :], ident[0:npp, 0:npp]), reads=[kx + (nm,), "ident"], writes=[("bank", bank)])
                P.act(lambda e, ps=ps, d=d, qi=qi, ch0=ch0, npp=npp: e.copy(out=colT[d][qi][0:CH, ch0:ch0 + npp], in_=ps[0:CH, 0:npp]),
                      reads=[("bank", bank)], writes=[("colT", d, qi, si)])
    for d in range(2):
        P.dma(lambda e, d=d: e.dma_start(out=glb[d], in_=glD[d, 0:NCH].partition_broadcast(128)), reads=[("glD", d, 0), ("glD", d, 1)], writes=[("glb", d)])
    P.barrier()
    A.reset(mk3)
    if stop <= 3:
        return
    NBUF = 2
    def gb(shape):
        return [[A.alloc(shape) for _ in range(NBUF)] for d in range(2)]
    WTs = gb([NG, CH]); Us = gb([NG, 128]); KDs = gb([NG, 128]); QGs = gb([NG, CH]); AQs = gb([NG, CH]); OGs = None
    S = [[A.alloc([128]) for _ in range(2)] for d in range(2)]
    Up = [[A.alloc([128]) for _ in range(2)] for d in range(2)]
    qkv_t = [A.alloc([3, 512]) for _ in range(2)]
    Grow = [A.alloc([512]) for _ in range(2)]
    Brow = [A.alloc([512]) for _ in range(2)]
    E1_ = [A.alloc([NG, CH]) for _ in range(2)]; DA_ = [A.alloc([NG, CH]) for _ in range(2)]
    DN_ = [A.alloc([NG, CH]) for _ in range(2)]; DL_ = [A.alloc([NG, CH]) for _ in range(2)]
    Nm_ = [[A.alloc([NG, CH], BF16) for _ in range(2)] for _ in range(2)]; Lm_ = [[A.alloc([NG, CH], BF16) for _ in range(2)] for _ in range(2)]
    Xm_ = [[A.alloc([NG, CH], BF16) for _ in range(2)] for _ in range(2)]
    Kbg_ = [A.alloc([NG, 128], BF16) for _ in range(2)]; Vb_ = [A.alloc([NG, 128], BF16) for _ in range(2)]
    qkb_t = [A.alloc([2, 512], BF16) for _ in range(2)]
    OTs = A.alloc([TT])
    otw = set()
    identb = A.alloc([CH], BF16)
    P.pool(lambda e: e.tensor_copy(out=identb[0:CH, :], in_=ident[0:CH, 0:CH]), reads=["ident"], writes=["identb"])
    nmask = A.alloc([4, CH])
    P.dve(lambda e: e.tensor_scalar(out=nmask[0:CH], in0=gmask[0:CH, 0:4, :], scalar1=-1.0, scalar2=30000.0, op0=ALU.add, op1=ALU.mult), reads=["gmask"], writes=["nmask"])
    bdm = A.alloc([2, CH], BF16)
    P.pool(lambda e: e.tensor_copy(out=bdm[0:CH], in_=gmask[0:CH, 4:6, :]), reads=["gmask"], writes=["bdm"])
    Noff_ = [A.alloc([NG, CH], BF16) for _ in range(2)]; Wd_ = [A.alloc([NG, 128], BF16) for _ in range(2)]; Ud_ = [A.alloc([NG, 128], BF16) for _ in range(2)]
    pc = dict(n=0)

    def chunk_list(d, grp):
        if grp == 16:
            cs = list(range(NLAT, NLAT + NCTX))
        else:
            cs = list(range(NG * grp, NG * grp + NG))
        return cs if d == 0 else cs[::-1]

    def pre(d, grp, buf):
        cs = sorted(chunk_list(d, grp))
        c0 = cs[0]; ncn = len(cs); n = CH * ncn; tok0 = CH * c0
        pb = d
        qt = qkv_t[pb]; gr = Grow[pb]; br = Brow[pb]; qb = qkb_t[pb]
        E1 = E1_[d]; DA = DA_[d]; DN = DN_[d]; DL = DL_[d]; Nm = Nm_[d]; Lm = Lm_[d]; Xm = Xm_[d]; Kbg = Kbg_[d]; Vb = Vb_[d]
        Noff = Noff_[d]; Wd = Wd_[d]; Ud = Ud_[d]
        BA, BB = 2 * d, 2 * d + 1
        bankA, bankB = C.banks[BA], C.banks[BB]
        K = lambda *a: ("pre", pb) + a
        P.dma(lambda e: e.dma_start(out=qt[:, :, 0:n], in_=prc[:, :, tok0:tok0 + n].rearrange("s p t -> p s t")),
              reads=[("prc", x) for x in range(17)], writes=[K("qkv")])
        P.dma(lambda e: e.dma_start(out=qb[:, :, 0:n], in_=prcb[:, :, tok0:tok0 + n].rearrange("s p t -> p s t")),
              reads=[("prcb", x) for x in range(17)], writes=[K("qkb")])
        P.dma(lambda e: e.dma_start(out=gr[:, 0:n], in_=rowsD[d, 0, tok0:tok0 + n].partition_broadcast(128)),
              reads=[("rowsD", d, s_, 0) for s_ in range(2)], writes=[K("gr")])
        P.dma(lambda e: e.dma_start(out=br[0:CH, 0:n], in_=rowsD[d, 1, tok0:tok0 + n].partition_broadcast(CH)),
              reads=[("rowsD", d, s_, 1) for s_ in range(2)], writes=[K("br")])
        yield
        v3 = lambda t: t[0:CH, 0:ncn, :]
        gr3 = gr[0:CH, 0:n].rearrange("p (c i) -> p c i", i=CH)
        br3 = br[0:CH, 0:n].rearrange("p (c i) -> p c i", i=CH)
        def colb(qi, w):
            return colT[d][qi][0:CH, c0:c0 + ncn].unsqueeze(2).to_broadcast([CH, ncn, w])
        ck = [("colT", d, qi, s_) for qi in range(4) for s_ in range(2)]
        mstr = 0 if d == 0 else 2
        mstrT = 2 if d == 0 else 0
        def nmb(i):
            return nmask[0:CH, i, :].unsqueeze(1).to_broadcast([CH, ncn, CH])
        P.pool(lambda e: e.tensor_tensor(out=v3(E1), in0=gr3, in1=colb(0, CH), op=ALU.subtract), reads=[K("gr")] + ck, writes=[("E1", d)])
        P.dve(lambda e: e.scalar_tensor_tensor(out=v3(DA), in0=v3(E1), scalar=0.0, in1=nmb(mstr + 1), op0=ALU.min, op1=ALU.add), reads=[("E1", d), "nmask"], writes=[("DA", d)])
        P.dve(lambda e: e.scalar_tensor_tensor(out=v3(DN), in0=v3(E1), scalar=0.0, in1=nmb(mstr), op0=ALU.min, op1=ALU.add), reads=[("E1", d), "nmask"], writes=[("DN", d)])
        P.dve(lambda e: e.scalar_tensor_tensor(out=v3(DL), in0=v3(E1), scalar=0.0, in1=nmb(mstrT), op0=ALU.max, op1=ALU.subtract), reads=[("E1", d), "nmask"], writes=[("DL", d)])
        P.act(lambda e: e.activation(out=v3(DA), in_=v3(DA), func=AF.Exp), reads=[("DA", d)], writes=[("DA", d)])
        P.act(lambda e: e.activation(out=v3(DN), in_=v3(DN), func=AF.Exp), reads=[("DN", d)], writes=[("DN", d)])
        P.act(lambda e: e.activation(out=v3(DL), in_=v3(DL), func=AF.Exp, scale=-1.0), reads=[("DL", d)], writes=[("DL", d)])
        P.act(lambda e: e.activation(out=gr[:, 0:n], in_=gr[:, 0:n], func=AF.Exp), reads=[K("gr"), ("E1", d)], writes=[K("gr")])
        P.pool(lambda e: e.tensor_tensor(out=QGs[d][buf][:, 0:ncn, :], in0=qt[:, 0, 0:n].rearrange("p (c i) -> p c i", i=CH),
                                         in1=gr[:, 0:n].rearrange("p (c i) -> p c i", i=CH), op=ALU.mult),
               reads=[K("qkv"), K("gr")], writes=[("QG", d, buf)])
        P.pool(lambda e: e.tensor_tensor(out=v3(DN), in0=v3(DN), in1=br3, op=ALU.mult), reads=[("DN", d), K("br")], writes=[("DN", d)])
        P.pool(lambda e: e.tensor_tensor(out=v3(DL), in0=v3(DL), in1=colb(1, CH), op=ALU.mult), reads=[("DL", d)] + ck, writes=[("DL", d)])
        yield
        for i in range(ncn):
            ks = qb[:, 1, CH * i:CH * i + CH]
            P.pe(lambda e, i=i, ks=ks: e.matmul(bankA[0:CH, CH * i:CH * i + CH], lhsT=ks, rhs=ks, start=True, stop=True), reads=[K("qkb")], writes=[("bank", BA)])
        for i in range(ncn):
            ks = qb[:, 1, CH * i:CH * i + CH]; qs = qb[:, 0, CH * i:CH * i + CH]
            P.pe(lambda e, i=i, ks=ks, qs=qs: e.matmul(bankB[0:CH, CH * i:CH * i + CH], lhsT=ks, rhs=qs, start=True, stop=True), reads=[K("qkb")], writes=[("bank", BB)])
        b03 = bankA[0:CH, 0:n].rearrange("p (c i) -> p c i", i=CH)
        b13 = bankB[0:CH, 0:n].rearrange("p (c i) -> p c i", i=CH)
        P.dve(lambda e: e.tensor_tensor(out=v3(Nm[0]), in0=b03, in1=v3(DN), op=ALU.mult), reads=[("bank", BA), ("DN", d)], writes=[("Nm", d, 0)])
        P.dve(lambda e: e.tensor_tensor(out=v3(Lm[0]), in0=b03, in1=v3(DL), op=ALU.mult), reads=[("bank", BA), ("DL", d)], writes=[("Lm", d, 0)])
        P.dve(lambda e: e.tensor_tensor(out=AQs[d][buf][0:CH, 0:ncn, :], in0=b13, in1=v3(DA), op=ALU.mult), reads=[("bank", BB), ("DA", d)], writes=[("AQ", d, buf)])
        def bdb(i):
            return bdm[0:CH, i, :].unsqueeze(1).to_broadcast([CH, ncn, CH])
        P.pool(lambda e: e.tensor_tensor(out=v3(Noff), in0=v3(Nm[0]), in1=bdb(1), op=ALU.mult), reads=[("Nm", d, 0), "bdm"], writes=[("Noff", d)])
        P.pool(lambda e: e.tensor_tensor(out=v3(Nm[0]), in0=v3(Nm[0]), in1=bdb(0), op=ALU.mult), reads=[("Nm", d, 0), "bdm", ("Noff", d)], writes=[("Nm", d, 0)])
        P.pool(lambda e: e.tensor_tensor(out=v3(Lm[0]), in0=v3(Lm[0]), in1=bdb(0), op=ALU.mult), reads=[("Lm", d, 0), "bdm"], writes=[("Lm", d, 0)])
        P.pool(lambda e: e.tensor_tensor(out=v3(Xm[0]), in0=identb[0:CH, 0:CH].unsqueeze(1).to_broadcast([CH, ncn, CH]), in1=v3(Nm[0]), op=ALU.subtract),
               reads=[("Nm", d, 0), "identb"], writes=[("Xm", d, 0)])
        yield
        for (src_sec, bank, dst, qi, nm) in ((1, 2, Kbg, 2, ("Kbg", d)), (2, 3, Vb, 1, ("Vb", d))):
            ps = C.banks[bank]
            for i in range(ncn):
                P.pe(lambda e, ps=ps, i=i, src_sec=src_sec: e.transpose(ps[0:CH, i * 128:(i + 1) * 128], qt[:, src_sec, CH * i:CH * i + CH], ident),
                     reads=[K("qkv"), "ident"], writes=[("bank", bank)])
            ps3 = ps[0:CH, 0:128 * ncn].rearrange("p (c i) -> p c i", i=128)
            cb_ = colT[d][qi][0:CH, c0:c0 + ncn].unsqueeze(2).to_broadcast([CH, ncn, 128])
            P.dve(lambda e, ps3=ps3, dst=dst, cb_=cb_: e.tensor_tensor(out=dst[0:CH, 0:ncn, :], in0=ps3, in1=cb_, op=ALU.mult),
                  reads=[("bank", bank)] + ck, writes=[(nm, d)])
            if src_sec == 1:
                cb2 = colT[d][3][0:CH, c0:c0 + ncn].unsqueeze(2).to_broadcast([CH, ncn, 128])
                P.dve(lambda e, ps3=ps3, cb2=cb2: e.tensor_tensor(out=KDs[d][buf][0:CH, 0:ncn, :], in0=ps3, in1=cb2, op=ALU.mult),
                      reads=[("bank", bank)] + ck, writes=[("KD", d, buf)])
        yield
        cur = 0
        for lev in range(1, NLEV + 1):
            nxt = 1 - cur
            for i in range(ncn):
                P.pe(lambda e, i=i, cur=cur: e.matmul(bankA[0:CH, CH * i:CH * i + CH], lhsT=Nm[cur][0:CH, i, :], rhs=Lm[cur][0:CH, i, :], start=True, stop=True),
                     reads=[("Nm", d, cur), ("Lm", d, cur)], writes=[("bank", BA)])
            if lev < NLEV:
                for i in range(ncn):
                    P.pe(lambda e, i=i, cur=cur: e.matmul(bankB[0:CH, CH * i:CH * i + CH], lhsT=Lm[cur][0:CH, i, :], rhs=Nm[cur][0:CH, i, :], start=True, stop=True),
                         reads=[("Nm", d, cur), ("Lm", d, cur)], writes=[("bank", BB)])
            b43 = bankA[0:CH, 0:n].rearrange("p (c i) -> p c i", i=CH)
            b53 = bankB[0:CH, 0:n].rearrange("p (c i) -> p c i", i=CH)
            P.act(lambda e, nxt=nxt, b43=b43: e.copy(out=v3(Lm[nxt]), in_=b43), reads=[("bank", BA)], writes=[("Lm", d, nxt)])
            if lev < NLEV:
                P.act(lambda e, nxt=nxt, b53=b53: e.copy(out=v3(Nm[nxt]), in_=b53), reads=[("bank", BB)], writes=[("Nm", d, nxt)])
            yield
            for i in range(ncn):
                P.pe(lambda e, i=i, nxt=nxt, cur=cur: e.matmul(bankA[0:CH, CH * i:CH * i + CH], lhsT=Lm[nxt][0:CH, i, :], rhs=Xm[cur][0:CH, i, :], start=True, stop=True),
                     reads=[("Lm", d, nxt), ("Xm", d, cur)], writes=[("bank", BA)])
            P.dve(lambda e, nxt=nxt, cur=cur: e.tensor_tensor(out=v3(Xm[nxt]), in0=b03, in1=v3(Xm[cur]), op=ALU.add), reads=[("bank", BA), ("Xm", d, cur)], writes=[("Xm", d, nxt)])
            cur = nxt
            yield
        Xf = Xm[cur]
        for (src, mid, nm, bnk) in ((Kbg, Wd, ("Kbg", d), 0), (Vb, Ud, ("Vb", d), 3)):
            psx = C.banks[bnk]
            for i in range(ncn):
                P.pe(lambda e, psx=psx, i=i, src=src: e.matmul(psx[0:CH, i * 128:(i + 1) * 128], lhsT=Xf[0:CH, i, :], rhs=src[0:CH, i, :], start=True, stop=True),
                     reads=[(nm, d), ("Xm", d, cur)], writes=[("bank", bnk)])
            psx3 = psx[0:CH, 0:128 * ncn].rearrange("p (c i) -> p c i", i=128)
            P.act(lambda e, psx3=psx3, mid=mid: e.copy(out=mid[0:CH, 0:ncn, :], in_=psx3), reads=[("bank", bnk)], writes=[(nm, d, "mid")])
            for i in range(ncn):
                P.pe(lambda e, psx=psx, i=i, mid=mid: e.matmul(psx[0:CH, i * 128:(i + 1) * 128], lhsT=Noff[0:CH, i, :], rhs=mid[0:CH, i, :], start=True, stop=True),
                     reads=[(nm, d, "mid"), ("Noff", d)], writes=[("bank", bnk)])
            P.dve(lambda e, psx3=psx3, src=src: e.tensor_tensor(out=src[0:CH, 0:ncn, :], in0=src[0:CH, 0:ncn, :], in1=psx3, op=ALU.subtract),
                  reads=[("bank", bnk), (nm, d)], writes=[(nm, d)])
        yield
        for i in range(ncn):
            P.pe(lambda e, i=i: e.matmul(bankB[:, CH * i:CH * i + CH], lhsT=Kbg[0:CH, i, :], rhs=Xf[0:CH, i, :], start=True, stop=True),
                 reads=[("Kbg", d), ("Xm", d, cur)], writes=[("bank", BB)])
        P.act(lambda e: e.copy(out=WTs[d][buf][:, 0:ncn, :], in_=bankB[:, 0:n].rearrange("p (c i) -> p c i", i=CH)), reads=[("bank", BB)], writes=[("WT", d, buf)])
        ps = bankA
        for i in range(ncn):
            P.pe(lambda e, ps=ps, i=i: e.matmul(ps[0:CH, i * 128:(i + 1) * 128], lhsT=Xf[0:CH, i, :], rhs=Vb[0:CH, i, :], start=True, stop=True),
                 reads=[("Vb", d), ("Xm", d, cur)], writes=[("bank", BA)])
        P.act(lambda e, ps=ps: e.copy(out=Us[d][buf][0:CH, 0:ncn, :], in_=ps[0:CH, 0:128 * ncn].rearrange("p (c i) -> p c i", i=128)),
              reads=[("bank", BA)], writes=[("U", d, buf)])
        yield

    sstate = [dict(cur=0, n=0) for d in range(2)]

    def scan(d, grp, buf):
        cs = chunk_list(d, grp)
        c0 = min(cs); ncn = len(cs)
        st = sstate[d]
        psA = C.banks[4 + 2 * d]
        psB = C.banks[5 + 2 * d]
        kA_, kB_ = ("bank", 4 + 2 * d), ("bank", 5 + 2 * d)
        for c in cs:
            i = c - c0
            cur = st["cur"]; nxt = 1 - cur
            ob_ = st["n"] % 2
            st["n"] += 1
            ub = ob_
            ku, ks, ko = kA_, kB_, kA_
            P.pe(lambda e, i=i, cur=cur: e.matmul(psA[0:CH, 0:128], lhsT=WTs[d][buf][:, i, :], rhs=S[d][cur], start=True, stop=True),
                 reads=[("WT", d, buf), ("S", d, cur)], writes=[ku])
            P.dve(lambda e, i=i, ub=ub: e.tensor_tensor(out=Up[d][ub][0:CH, :], in0=Us[d][buf][0:CH, i, :], in1=psA[0:CH, 0:128], op=ALU.subtract),
                  reads=[ku, ("U", d, buf)], writes=[("Up", d, ub)])
            P.pe(lambda e, i=i, ub=ub: e.matmul(psB[:, 0:128], lhsT=KDs[d][buf][0:CH, i, :], rhs=Up[d][ub][0:CH, :], start=True, stop=True),
                 reads=[("KD", d, buf), ("Up", d, ub)], writes=[ks])
            oc = 128 + CH * ob_
            P.pe(lambda e, i=i, cur=cur, oc=oc: e.matmul(psA[:, oc:oc + CH], lhsT=S[d][cur], rhs=QGs[d][buf][:, i, :], start=True, stop=False),
                 reads=[("S", d, cur), ("QG", d, buf)], writes=[ko])
            P.pe(lambda e, i=i, ub=ub, oc=oc: e.matmul(psA[:, oc:oc + CH], lhsT=Up[d][ub][0:CH, :], rhs=AQs[d][buf][0:CH, i, :], start=False, stop=True),
                 reads=[("Up", d, ub), ("AQ", d, buf)], writes=[ko])
            P.act(lambda e, c=c, cur=cur, nxt=nxt: e.activation(out=S[d][nxt], in_=S[d][cur], func=AF.Copy, scale=glb[d][:, c:c + 1]),
                  reads=[("S", d, cur), ("glb", d)], writes=[("S", d, nxt)])
            P.dve(lambda e, nxt=nxt: e.tensor_tensor(out=S[d][nxt], in0=S[d][nxt], in1=psB[:, 0:128], op=ALU.add),
                  reads=[ks, ("S", d, nxt)], writes=[("S", d, nxt)])
            if c not in otw:
                otw.add(c)
                P.act(lambda e, c=c, oc=oc: e.copy(out=OTs[:, CH * c:CH * c + CH], in_=psA[:, oc:oc + CH]), reads=[ko], writes=[("OT", c)])
            else:
                P.dve(lambda e, c=c, oc=oc: e.tensor_tensor(out=OTs[:, CH * c:CH * c + CH], in0=OTs[:, CH * c:CH * c + CH], in1=psA[:, oc:oc + CH], op=ALU.add),
                      reads=[ko, ("OT", c)], writes=[("OT", c)])
            st["cur"] = nxt
            yield

    for d in range(2):
        P.pool(lambda e, d=d: e.memset(S[d][0], 0.0), writes=[("S", d, 0)])
    order = [[16] + list(range(16)), [16] + list(range(15, -1, -1))]
    nsteps = 17

    merge_gens([pre(0, order[0][0], 0), pre(1, order[1][0], 0)])
    for s_ in range(nsteps):
        gens = [scan(0, order[0][s_], s_ % 2), scan(1, order[1][s_], s_ % 2)]
        if s_ + 1 < nsteps:
            gens.append(pre(0, order[0][s_ + 1], (s_ + 1) % 2))
            gens.append(pre(1, order[1][s_ + 1], (s_ + 1) % 2))
        merge_gens(gens)
    P.barrier()
    mk5 = A.mark()
    if stop <= 4:
        return
    o0 = [A.alloc([512]) for _ in range(2)]; o1 = [A.alloc([512]) for _ in range(2)]; sq2 = [A.alloc([512]) for _ in range(2)]
    rr = [A.alloc([512]) for _ in range(2)]; zt = [A.alloc([512], BF16) for _ in range(2)]; ob = [A.alloc([512], BF16) for _ in range(2)]
    eps128 = A.alloc([1])
    P.pool(lambda e: e.memset(eps128, 128.0 * EPS), writes=["eps128"])
    nws = A.alloc([1])
    P.dve(lambda e: e.tensor_scalar(out=nws, in0=nw, scalar1=math.sqrt(128.0), scalar2=None, op0=ALU.mult), reads=["nw"], writes=["nws"])
    for ti, (t0, n) in enumerate(TILES):
        b = ti % 2
        P.dma(lambda e, t0=t0, n=n, b=b: e.dma_start(out=zt[b][:, 0:n], in_=zsD[:, t0:t0 + n]), reads=[("zsD", ti)], writes=[("zt", b)])
        otk = [("OT", c_) for c_ in range(t0 // GCH, (t0 + n) // GCH)]
        P.act(lambda e, t0=t0, n=n, b=b: e.activation(out=sq2[b][:, 0:n], in_=OTs[:, t0:t0 + n], func=AF.Square), reads=otk, writes=[("sq2", b)])
        bank = b
        ps = C.banks[bank]
        P.pe(lambda e, ps=ps, n=n, b=b: e.matmul(ps[:, 0:n], lhsT=onesf, rhs=sq2[b][:, 0:n], start=True, stop=True), reads=[("sq2", b), "onesf"], writes=[("bank", bank)])
        P.act(lambda e, ps=ps, n=n, b=b: e.activation(out=rr[b][:, 0:n], in_=ps[:, 0:n], func=AF.Ln, bias=eps128[:, 0:1]), reads=[("bank", bank), "eps128"], writes=[("rr", b)])
        P.act(lambda e, n=n, b=b: e.activation(out=rr[b][:, 0:n], in_=rr[b][:, 0:n], func=AF.Exp, scale=-0.5), reads=[("rr", b)], writes=[("rr", b)])
        P.dve(lambda e, t0=t0, n=n, b=b: e.scalar_tensor_tensor(out=o0[b][:, 0:n], in0=OTs[:, t0:t0 + n], scalar=nws[:, 0:1], in1=rr[b][:, 0:n], op0=ALU.mult, op1=ALU.mult),
              reads=otk + [("rr", b), "nws"], writes=[("o0", b)])
        P.dve(lambda e, n=n, b=b: e.tensor_tensor(out=ob[b][:, 0:n], in0=o0[b][:, 0:n], in1=zt[b][:, 0:n], op=ALU.mult), reads=[("o0", b), ("zt", b)], writes=[("gob", b)])
        P.dma(lambda e, t0=t0, n=n, b=b: e.dma_start(out=mixT[grow0:grow0 + 128, t0:t0 + n], in_=ob[b][:, 0:n]), reads=[("gob", b)], writes=[("mixT", "gdn", ti)])
        if after_tile is not None:
            after_tile(ti)
    A.reset(m0)


A_INPUTS = ["xT", "cv", "wmod", "bmod", "ng", "win", "lru_cw", "lru_cb", "lru_w", "lru_b", "lru_lam",
            "ropeC", "ropeS", "rotT", "e65", "mask", "ident", "sink", "gdn_cw", "gdn_par", "gdn_nw", "gmask"]


def build_A():
    nc = bass.Bass("TRN2", target_bir_lowering=False)
    xT = dram_in(nc, "xT", [D, TT]); cv = dram_in(nc, "cv", [128, 8, 2]); wmod = dram_in(nc, "wmod", [128, 8, 2048])
    bmod = dram_in(nc, "bmod", [128, 16]); ng = dram_in(nc, "ng", [128, 8]); win = dram_in(nc, "win", [128, 8, NATT + NGDN + NLRU])
    lru_cw = dram_in(nc, "lru_cw", [128, 4]); lru_cb = dram_in(nc, "lru_cb", [128, 1]); lru_w = dram_in(nc, "lru_w", [128, 2, 2, 128])
    lru_b = dram_in(nc, "lru_b", [128, 2, 2]); lru_lam = dram_in(nc, "lru_lam", [128, 2])
    ropeC = dram_in(nc, "ropeC", [64, T]); ropeS = dram_in(nc, "ropeS", [64, T]); rotT = dram_in(nc, "rotT", [64, 64]); e65 = dram_in(nc, "e65", [64, 65])
    mask = dram_in(nc, "mask", [128, 384], BF16); ident = dram_in(nc, "ident", [128, 128]); sink = dram_in(nc, "sink", [1, 2])
    gdn_cw = dram_in(nc, "gdn_cw", [128, 3, 4]); gdn_par = dram_in(nc, "gdn_par", [1, 4]); gdn_nw = dram_in(nc, "gdn_nw", [128, 1])
    gmask = dram_in(nc, "gmask", [GCH, 6, GCH])
    H = dram_tmp(nc, "H", [128, 8, TT], BF16)
    scr = dict(raw=dram_tmp(nc, "raw", [3, 128, TT]), prc=dram_tmp(nc, "prc", [3, 128, TT]), zsD=dram_tmp(nc, "zsD", [128, TT], BF16),
               ba=dram_tmp(nc, "ba", [4, TT]), rows=dram_tmp(nc, "rows", [2, 2, TT]), gl=dram_tmp(nc, "gl", [2, 132]), OD=dram_tmp(nc, "OD", [2, 128, TT]),
               prcb=dram_tmp(nc, "prcb", [2, 128, TT], BF16))
    mixT = dram_out(nc, "mixT", [384, TT], BF16)
    with ExitStack() as st:
        C = setup(nc, st)
        C.P._bar_t = C.A.alloc([1])
        phase_h(C, xT, cv, wmod, bmod, ng, H)
        C.P.barrier()
        phase_att(C, H, win, ropeC, ropeS, rotT, e65, mask, ident, sink, mixT, True)
        C.P.barrier()
        phase_lru(C, H, win, lru_cw, lru_cb, lru_w, lru_b, lru_lam, mixT, 256)
        C.P.barrier()
        phase_gdn(C, H, win, gdn_cw, gdn_par, gdn_nw, gmask, ident, mixT, scr)
        C.P.build()
    return nc


NTB = 2112
BTILES = [(512 * i, 512) for i in range(4)] + [(2048, 64)]
B_INPUTS = ["xs", "ms", "wout", "cv", "wmodg", "bmodg", "fg"]


def build_B():
    nc = bass.Bass("TRN2", target_bir_lowering=False)
    xs = dram_in(nc, "xs", [D, NTB]); ms = dram_in(nc, "ms", [1536, NTB], BF16); wout = dram_in(nc, "wout", [128, 12, D])
    cv = dram_in(nc, "cv", [128, 8, 2]); wmodg = dram_in(nc, "wmodg", [128, 8, 1024]); bmodg = dram_in(nc, "bmodg", [128, 8]); fg = dram_in(nc, "fg", [128, 8])
    xo = dram_out(nc, "xo", [D, NTB]); xn = dram_out(nc, "xn", [D, NTB])
    with ExitStack() as st:
        C = setup(nc, st)
        P, A = C.P, C.A
        P._bar_t = A.alloc([1])
        ones_bf = A.alloc([128], BF16)
        cvt = A.alloc([8, 2]); scv = A.alloc([8, 2]); bm = A.alloc([8]); gate = A.alloc([8, 2]); fgt = A.alloc([8]); epsb = A.alloc([1])
        P.pool(lambda e: e.memset(ones_bf, 1.0), writes=["ones_bf"])
        P.pool(lambda e: e.memset(epsb, 1024.0 * EPS), writes=["epsb"])
        P.dma(lambda e: e.dma_start(out=cvt, in_=cv), writes=["cvt"])
        P.dma(lambda e: e.dma_start(out=bm, in_=bmodg), writes=["bm"])
        P.dma(lambda e: e.dma_start(out=fgt, in_=fg), writes=["fgt"])
        P.act(lambda e: e.activation(out=scv, in_=cvt, func=AF.Silu), reads=["cvt"], writes=["scv"])
        wm = A.alloc([8, 1024])
        P.dma(lambda e: e.dma_start(out=wm, in_=wmodg), writes=["wm"])
        psm = C.banks[0][:, 0:16].rearrange("p (a b) -> p a b", a=8, b=2)
        for cc in range(8):
            for k in range(8):
                P.pe(lambda e, cc=cc, k=k: e.matmul(psm[:, cc, :], lhsT=wm[:, k, cc * 128:(cc + 1) * 128], rhs=scv[:, k, :], start=(k == 0), stop=(k == 7)),
                     reads=["wm", "scv"], writes=[("bank", 0)])
        P.dve(lambda e: e.tensor_tensor(out=gate, in0=psm, in1=bm.unsqueeze(2).to_broadcast([128, 8, 2]), op=ALU.add), reads=[("bank", 0), "bm"], writes=["gate"])
        P.dve(lambda e: e.tensor_scalar(out=fgt, in0=fgt, scalar1=32.0, scalar2=None, op0=ALU.mult), reads=["fgt"], writes=["fgt"])
        wob = A.alloc([12, D], BF16)
        wof = A.alloc([12, D])
        P.dma(lambda e: e.dma_start(out=wof, in_=wout), writes=["wof"])
        for k in range(12):
            eng = P.dve if k % 2 == 0 else P.pool
            eng(lambda e, k=k: e.tensor_copy(out=wob[:, k, :], in_=wof[:, k, :]), reads=["wof"], writes=[("wob", k)])
        mt = [A.alloc([12, 512], BF16) for _ in range(2)]
        xt = [A.alloc([8, 512]) for _ in range(2)]
        tt_ = [A.alloc([512]) for _ in range(2)]
        sq = [A.alloc([8, 512], BF16) for _ in range(2)]
        rs = [A.alloc([512]) for _ in range(2)]
        msv = ms.rearrange("(k p) t -> p k t", p=128)
        xsv = xs.rearrange("(k p) t -> p k t", p=128)
        xov = xo.rearrange("(k p) t -> p k t", p=128)
        xnv = xn.rearrange("(k p) t -> p k t", p=128)
        cnt = 0
        for ti, (t0, n) in enumerate(BTILES):
            b = ti % 2
            s = 1 if ti == 4 else 0
            P.dma(lambda e, b=b, t0=t0, n=n: e.dma_start(out=mt[b][:, :, 0:n], in_=msv[:, :, t0:t0 + n]), writes=[("mt", b)])
            P.dma(lambda e, b=b, t0=t0, n=n: e.dma_start(out=xt[b][:, :, 0:n], in_=xsv[:, :, t0:t0 + n]), writes=[("xt", b)])
            for dc in range(8):
                bank = 1 + (cnt % 4)
                cnt += 1
                ps = C.banks[bank]
                for k in range(12):
                    P.pe(lambda e, ps=ps, b=b, k=k, dc=dc, n=n: e.matmul(ps[:, 0:n], lhsT=wob[:, k, dc * 128:(dc + 1) * 128], rhs=mt[b][:, k, 0:n],
                                                                       start=(k == 0), stop=(k == 11)),
                         reads=[("mt", b), ("wob", k)], writes=[("bank", bank)])
                tb = cnt % 2
                P.act(lambda e, ps=ps, tb=tb, dc=dc, n=n, s=s: e.activation(out=tt_[tb][:, 0:n], in_=ps[:, 0:n], func=AF.Identity, scale=gate[:, dc, s:s + 1]),
                      reads=[("bank", bank), "gate"], writes=[("tt", tb)])
                P.dve(lambda e, b=b, tb=tb, dc=dc, n=n: e.tensor_tensor(out=xt[b][:, dc, 0:n], in0=xt[b][:, dc, 0:n], in1=tt_[tb][:, 0:n], op=ALU.add),
                      reads=[("tt", tb), ("xt", b)], writes=[("xt", b)])
            P.dma(lambda e, b=b, t0=t0, n=n: e.dma_start(out=xov[:, :, t0:t0 + n], in_=xt[b][:, :, 0:n]), reads=[("xt", b)], writes=[("xo", ti)])
            P.act(lambda e, b=b, n=n: e.activation(out=sq[b][:, :, 0:n], in_=xt[b][:, :, 0:n], func=AF.Square), reads=[("xt", b)], writes=[("sq", b)])
            bank = 5 + b
            ps = C.banks[bank]
            for k in range(8):
                P.pe(lambda e, ps=ps, b=b, k=k, n=n: e.matmul(ps[:, 0:n], lhsT=ones_bf, rhs=sq[b][:, k, 0:n], start=(k == 0), stop=(k == 7)),
                     reads=[("sq", b), "ones_bf"], writes=[("bank", bank)])
            P.act(lambda e, ps=ps, b=b, n=n: e.activation(out=rs[b][:, 0:n], in_=ps[:, 0:n], func=AF.Ln, bias=epsb[:, 0:1]), reads=[("bank", bank), "epsb"], writes=[("rs", b)])
            P.act(lambda e, b=b, n=n: e.activation(out=rs[b][:, 0:n], in_=rs[b][:, 0:n], func=AF.Exp, scale=-0.5), reads=[("rs", b)], writes=[("rs", b)])
            P.dve(lambda e, b=b, n=n: e.tensor_tensor(out=xt[b][:, :, 0:n], in0=xt[b][:, :, 0:n], in1=rs[b][:, 0:n].unsqueeze(1).to_broadcast([128, 8, n]), op=ALU.mult),
                  reads=[("xt", b), ("rs", b), ("xo", ti)], writes=[("xt", b)])
            P.dve(lambda e, b=b, n=n: e.tensor_tensor(out=xt[b][:, :, 0:n], in0=xt[b][:, :, 0:n], in1=fgt.unsqueeze(2).to_broadcast([128, 8, n]), op=ALU.mult),
                  reads=[("xt", b), "fgt"], writes=[("xt", b)])
            P.dma(lambda e, b=b, t0=t0, n=n: e.dma_start(out=xnv[:, :, t0:t0 + n], in_=xt[b][:, :, 0:n]), reads=[("xt", b)], writes=[("xn", ti)])
        P.build()
    return nc


def prep_B(inp, l, b, j, xT_full, mix_full):
    f = np.float32
    lat = slice(2048 * j, 2048 * (j + 1)); cx = slice(T + 64 * j, T + 64 * (j + 1))
    d = {}
    d["xs"] = np.ascontiguousarray(np.concatenate([xT_full[:, lat], xT_full[:, cx]], axis=1))
    d["ms"] = np.ascontiguousarray(np.concatenate([mix_full[:, lat], mix_full[:, cx]], axis=1))
    d["wout"] = np.ascontiguousarray(inp["w_out"][l].reshape(12, 128, D).transpose(1, 0, 2))
    d["cv"] = pk(np.stack([inp["c"][b], inp["c_ctx"]], axis=1).astype(f))
    d["wmodg"] = pk(inp["w_mod"][l][:, 2048:3072])
    d["bmodg"] = np.ascontiguousarray(inp["b_mod"][l][2048:3072].reshape(8, 128).T)
    d["fg"] = np.ascontiguousarray(inp["final_g"].reshape(8, 128).T)
    return d


def prep_A_all(inp, l, b, j, xT_full):
    d = prep_A(inp, l, b, j, xT_full)
    d.update(prep_att(inp, l, j))
    d.update(prep_gdn(inp, l, j))
    return d


def run_forward(inp):
    inp = {k: np.asarray(v) for k, v in inp.items()}
    ncA = build_A()
    ncB = build_B()
    xT = [np.ascontiguousarray(np.concatenate([inp["x"][b], inp["ctx"][b]], axis=0).T.astype(np.float32)) for b in range(2)]
    out = None
    for l in range(2):
        maps = []
        for core in range(8):
            b, j = core // 4, core % 4
            d = prep_A_all(inp, l, b, j, xT[b])
            maps.append({k: d[k] for k in A_INPUTS})
        res = run_bass_kernel_spmd(ncA, maps, core_ids=list(range(8)))
        mix = []
        for b in range(2):
            m = np.zeros((1536, TT), ml_dtypes.bfloat16)
            for j in range(4):
                r = np.asarray(res.results[4 * b + j]["mixT"])
                m[128 * j:128 * j + 128] = r[0:128]
                m[512 + 128 * j:512 + 128 * j + 128] = r[128:256]
                m[1024 + 128 * j:1024 + 128 * j + 128] = r[256:384]
            mix.append(m)
        maps = []
        for core in range(8):
            b, j = core // 4, core % 4
            d = prep_B(inp, l, b, j, xT[b], mix[b])
            maps.append({k: d[k] for k in B_INPUTS})
        res = run_bass_kernel_spmd(ncB, maps, core_ids=list(range(8)))
        if l == 0:
            for b in range(2):
                nx = np.empty((D, TT), np.float32)
                for j in range(4):
                    r = np.asarray(res.results[4 * b + j]["xo"])
                    nx[:, 2048 * j:2048 * (j + 1)] = r[:, 0:2048]
                    nx[:, T + 64 * j:T + 64 * (j + 1)] = r[:, 2048:2112]
                xT[b] = nx
        else:
            out = np.empty((2, T, D), np.float32)
            for b in range(2):
                for j in range(4):
                    r = np.asarray(res.results[4 * b + j]["xn"])
                    out[b, 2048 * j:2048 * (j + 1), :] = r[:, 0:2048].T
    return out


A_CONST = ["ropeC", "ropeS", "rotT", "e65", "mask", "ident", "gmask"]
A_LAYER = ["cv", "wmod", "bmod", "ng", "win", "lru_cw", "lru_cb", "lru_w", "lru_b", "lru_lam", "sink", "gdn_cw", "gdn_par", "gdn_nw"]
A_SHAPES = dict(cv=[128, 8, 2], wmod=[128, 8, 2048], bmod=[128, 16], ng=[128, 8], win=[128, 8, NATT + NGDN + NLRU], lru_cw=[128, 4], lru_cb=[128, 1],
                lru_w=[128, 2, 2, 128], lru_b=[128, 2, 2], lru_lam=[128, 2], sink=[1, 2], gdn_cw=[128, 3, 4], gdn_par=[1, 4], gdn_nw=[128, 1],
                ropeC=[64, T], ropeS=[64, T], rotT=[64, 64], e65=[64, 65], mask=[128, 384], ident=[128, 128], gmask=[GCH, 6, GCH])
B_LAYER = ["woutj", "wmodgj", "bmodgj"]
B_SHAPES = dict(woutj=[128, 12, 256], wmodgj=[128, 8, 256], bmodgj=[128, 2])
GROUPS = [[0, 1, 2, 3], [4, 5, 6, 7]]


def phase_bf_gen(C, l, mixA_all, mixG_all, xsrc, xdst, xdst_b, x_all, cv, wmodgj, bmodgj, woutj):
    P, A = C.P, C.A
    m0 = A.mark()
    cvt = A.alloc([8, 2]); scv = A.alloc([8, 2]); bm = A.alloc([2]); gate = A.alloc([2, 2])
    P.dma(lambda e: e.dma_start(out=cvt, in_=cv), writes=["cvt_B"])
    P.dma(lambda e: e.dma_start(out=bm, in_=bmodgj), writes=["bm_B"])
    P.act(lambda e: e.activation(out=scv, in_=cvt, func=AF.Silu), reads=["cvt_B"], writes=["scv_B"])
    wm = A.alloc([8, 256])
    P.dma(lambda e: e.dma_start(out=wm, in_=wmodgj), writes=["wm_B"])
    psm = C.banks[0][:, 0:4].rearrange("p (a b) -> p a b", a=2, b=2)
    for cc in range(2):
        for k in range(8):
            P.pe(lambda e, cc=cc, k=k: e.matmul(psm[:, cc, :], lhsT=wm[:, k, cc * 128:(cc + 1) * 128], rhs=scv[:, k, :], start=(k == 0), stop=(k == 7)),
                 reads=["wm_B", "scv_B"], writes=[("bank", 0)])
    P.dve(lambda e: e.tensor_tensor(out=gate, in0=psm, in1=bm.unsqueeze(2).to_broadcast([128, 2, 2]), op=ALU.add), reads=[("bank", 0), "bm_B"], writes=["gate_B"])
    wob = A.alloc([12, 256], BF16)
    wof = A.alloc([12, 256])
    P.dma(lambda e: e.dma_start(out=wof, in_=woutj), writes=["wof_B"])
    for k in range(12):
        if k % 2 == 0:
            P.dve(lambda e, k=k: e.tensor_copy(out=wob[:, k, :], in_=wof[:, k, :]), reads=["wof_B"], writes=[("wob", k)])
        else:
            P.act(lambda e, k=k: e.copy(out=wob[:, k, :], in_=wof[:, k, :]), reads=["wof_B"], writes=[("wob", k)])
    mt = [A.alloc([12, 512], BF16) for _ in range(2)]
    xt = [A.alloc([2, 512]) for _ in range(2)]
    xb = [A.alloc([2, 512], BF16) for _ in range(2)]
    tt_ = [A.alloc([512]) for _ in range(2)]
    cnt = 0
    yield
    for ti, (t0, n) in enumerate(TILES):
        b = ti % 2
        s = 1 if ti == 16 else 0
        def _ldb(tj):
            tt0, nn = TILES[tj]
            bb = tj % 2
            P.dma(lambda e: e.dma_start(out=mt[bb][:, 0:8, 0:nn], in_=mixA_all.pk(tt0, nn)), reads=[("mixA_all", mixA_all.cidx(tt0))], writes=[("mt", bb)])
            P.dma(lambda e: e.dma_start(out=mt[bb][:, 8:12, 0:nn], in_=mixG_all.pk(tt0, nn)), reads=[("mixG_all", mixG_all.cidx(tt0))], writes=[("mtg", bb)])
            P.dma(lambda e: e.dma_start(out=xt[bb][:, :, 0:nn], in_=xsrc.pk(tt0, nn) if isinstance(xsrc, ChunkedDram) else xsrc.rearrange("(k p) t -> p k t", p=128)[:, :, tt0:tt0 + nn]),
                  reads=[("xrows", l - 1, tj)], writes=[("xtB", bb)])
        if ti == 0:
            _ldb(0)
        if ti + 1 < len(TILES):
            _ldb(ti + 1)
        for dc in range(2):
            bank = 1 + (cnt % 4)
            cnt += 1
            ps = C.banks[bank]
            for k in range(12):
                P.pe(lambda e, ps=ps, b=b, k=k, dc=dc, n=n: e.matmul(ps[:, 0:n], lhsT=wob[:, k, dc * 128:(dc + 1) * 128], rhs=mt[b][:, k, 0:n],
                                                                   start=(k == 0), stop=(k == 11)),
                     reads=[("mt", b), ("mtg", b), ("wob", k)], writes=[("bank", bank)])
            tb = cnt % 2
            P.act(lambda e, ps=ps, tb=tb, dc=dc, n=n, s=s: e.activation(out=tt_[tb][:, 0:n], in_=ps[:, 0:n], func=AF.Identity, scale=gate[:, dc, s:s + 1]),
                  reads=[("bank", bank), "gate_B"], writes=[("ttB", tb)])
            P.dve(lambda e, b=b, tb=tb, dc=dc, n=n: e.tensor_tensor(out=xt[b][:, dc, 0:n], in0=xt[b][:, dc, 0:n], in1=tt_[tb][:, 0:n], op=ALU.add),
                  reads=[("ttB", tb), ("xtB", b)], writes=[("xtB", b)])
        P.act(lambda e, b=b, n=n: e.copy(out=xb[b][:, :, 0:n], in_=xt[b][:, :, 0:n]), reads=[("xtB", b)], writes=[("xbB", b)])
        P.dma(lambda e, b=b, t0=t0, n=n: e.dma_start(out=xdst.pk(t0, n), in_=xt[b][:, :, 0:n]), reads=[("xtB", b)], writes=[("xrows", l, ti)])
        P.dma(lambda e, b=b, t0=t0, n=n: e.dma_start(out=xdst_b.pk(t0, n), in_=xb[b][:, :, 0:n]), reads=[("xbB", b)], writes=[("xrowsb", l, ti)])
        if xdst_b.last_tile_of_chunk(ti):
            c = xdst_b.cidx(t0)
            tis = xdst_b.tiles_of_chunk(c)
            P.cc(lambda e, c=c: e.collective_compute("AllGather", ALU.bypass, replica_groups=GROUPS, ins=[xdst_b.t[c].opt()], outs=[x_all.t[c].opt()]),
                 reads=[("xrowsb", l, t_) for t_ in tis], writes=[("x_all", c)])
        yield
    A.reset(m0)


def phase_final_gen(C, x_all, xo, fgj, xn):
    P, A = C.P, C.A
    m0 = A.mark()
    ones_bf = A.alloc([128], BF16); fgt = A.alloc([2]); epsb = A.alloc([1])
    P.pool(lambda e: e.memset(ones_bf, 1.0), writes=["ones_bf"])
    P.pool(lambda e: e.memset(epsb, 1024.0 * EPS), writes=["epsb"])
    P.dma(lambda e: e.dma_start(out=fgt, in_=fgj), writes=["fgt"])
    P.dve(lambda e: e.tensor_scalar(out=fgt, in0=fgt, scalar1=32.0, scalar2=None, op0=ALU.mult), reads=["fgt"], writes=["fgt"])
    xa = [A.alloc([8, 512], BF16) for _ in range(2)]; sq = [A.alloc([8, 512], BF16) for _ in range(2)]
    rs = [A.alloc([512]) for _ in range(2)]; xt = [A.alloc([2, 512]) for _ in range(2)]
    xnv = xn.rearrange("(k p) t -> p k t", p=128)
    yield
    for ti, (t0, n) in enumerate(TILES[:16]):
        b = ti % 2
        P.dma(lambda e, b=b, t0=t0, n=n: e.dma_start(out=xa[b][:, :, 0:n], in_=x_all.pk(t0, n)), reads=[("x_all", x_all.cidx(t0))], writes=[("xa", b)])
        P.dma(lambda e, b=b, t0=t0, n=n: e.dma_start(out=xt[b][:, :, 0:n], in_=xo.pk(t0, n)), reads=[("xrows", 1, ti)], writes=[("xt", b)])
        P.act(lambda e, b=b, n=n: e.activation(out=sq[b][:, :, 0:n], in_=xa[b][:, :, 0:n], func=AF.Square), reads=[("xa", b)], writes=[("sq", b)])
        bank = 5 + b
        ps = C.banks[bank]
        for k in range(8):
            P.pe(lambda e, ps=ps, b=b, k=k, n=n: e.matmul(ps[:, 0:n], lhsT=ones_bf, rhs=sq[b][:, k, 0:n], start=(k == 0), stop=(k == 7)),
                 reads=[("sq", b), "ones_bf"], writes=[("bank", bank)])
        P.act(lambda e, ps=ps, b=b, n=n: e.activation(out=rs[b][:, 0:n], in_=ps[:, 0:n], func=AF.Ln, bias=epsb[:, 0:1]), reads=[("bank", bank), "epsb"], writes=[("rs", b)])
        P.act(lambda e, b=b, n=n: e.activation(out=rs[b][:, 0:n], in_=rs[b][:, 0:n], func=AF.Exp, scale=-0.5), reads=[("rs", b)], writes=[("rs", b)])
        P.dve(lambda e, b=b, n=n: e.tensor_tensor(out=xt[b][:, :, 0:n], in0=xt[b][:, :, 0:n], in1=rs[b][:, 0:n].unsqueeze(1).to_broadcast([128, 2, n]), op=ALU.mult),
              reads=[("xt", b), ("rs", b)], writes=[("xt", b)])
        P.dve(lambda e, b=b, n=n: e.tensor_tensor(out=xt[b][:, :, 0:n], in0=xt[b][:, :, 0:n], in1=fgt.unsqueeze(2).to_broadcast([128, 2, n]), op=ALU.mult),
              reads=[("xt", b), "fgt"], writes=[("xt", b)])
        P.dma(lambda e, b=b, t0=t0, n=n: e.dma_start(out=xnv[:, :, t0:t0 + n], in_=xt[b][:, :, 0:n]), reads=[("xt", b)], writes=[("xn", ti)])
        yield
    A.reset(m0)


def build_fused():
    nc = bass.Bass("TRN2", target_bir_lowering=False)
    xT0 = dram_in(nc, "xT0", [D, TT]); xr0 = dram_in(nc, "xr0", [256, TT]); fgj = dram_in(nc, "fgj", [128, 2])
    cst = {k: dram_in(nc, k, A_SHAPES[k], BF16 if k == "mask" else F32) for k in A_CONST}
    lay = [{k: dram_in(nc, f"{k}_{l}", A_SHAPES[k]) for k in A_LAYER} for l in range(2)]
    layb = [{k: dram_in(nc, f"{k}_{l}", B_SHAPES[k]) for k in B_LAYER} for l in range(2)]
    H = dram_tmp(nc, "H", [128, 8, TT], BF16)
    scr = dict(raw=dram_tmp(nc, "raw", [3, 128, TT]), prc=dram_tmp(nc, "prc", [3, 128, TT]), zsD=dram_tmp(nc, "zsD", [128, TT], BF16),
               ba=dram_tmp(nc, "ba", [4, TT]), rows=dram_tmp(nc, "rows", [2, 2, TT]), gl=dram_tmp(nc, "gl", [2, 132]), OD=dram_tmp(nc, "OD", [2, 128, TT]),
               prcb=dram_tmp(nc, "prcb", [2, 128, TT], BF16))
    mixA_mine = ChunkedDram(nc, "mixA_mine", 256, BF16, 512); mixA_all = ChunkedDram(nc, "mixA_all", 1024, BF16, 512)
    mixG_mine = ChunkedDram(nc, "mixG_mine", 128, BF16, 512); mixG_all = ChunkedDram(nc, "mixG_all", 512, BF16, 512)
    xo = [ChunkedDram(nc, f"xo{l}", 256, F32, 512) for l in range(2)]
    xob = [ChunkedDram(nc, f"xob{l}", 256, BF16, 512) for l in range(2)]
    x_all = ChunkedDram(nc, "x_all", D, BF16, 512)
    xn = dram_out(nc, "xn", [256, T])
    with ExitStack() as st:
        C = setup(nc, st)
        P = C.P
        P._bar_t = C.A.alloc([1])
        base_mark = C.A.mark()

        def lagged(gb_, gn_, lag=4):
            next(gb_)
            C.A.reset(base_mark)
            next(gn_)
            nb_done = 0
            alive_b = alive_n = True
            while alive_b or alive_n:
                if alive_b:
                    try:
                        next(gb_); nb_done += 1
                    except StopIteration:
                        alive_b = False
                if alive_n and (nb_done >= lag or not alive_b):
                    try:
                        next(gn_)
                    except StopIteration:
                        alive_n = False

        hgen = None
        for l in range(2):
            p = lay[l]
            if l == 0:
                C.A.reset(base_mark)
                phase_h(C, xT0, p["cv"], p["wmod"], p["bmod"], p["ng"], H)
            P.barrier()
            C.A.reset(base_mark)
            phase_att(C, H, p["win"], cst["ropeC"], cst["ropeS"], cst["rotT"], cst["e65"], cst["mask"], cst["ident"], p["sink"], mixA_mine, True)
            P.barrier()
            phase_lru(C, H, p["win"], p["lru_cw"], p["lru_cb"], p["lru_w"], p["lru_b"], p["lru_lam"], mixA_mine, 128)
            P.barrier()
            akeys = [("mixT", nm, ti) for nm in ("att", "lru") for ti in range(17)]
            for c in range(mixA_mine.nchunks):
                P.cc(lambda e, c=c: e.collective_compute("AllGather", ALU.bypass, replica_groups=GROUPS, ins=[mixA_mine.t[c].opt()], outs=[mixA_all.t[c].opt()]),
                     reads=akeys, writes=[("mixA_all", c)])
            def _ag(ti):
                if mixG_mine.last_tile_of_chunk(ti):
                    c = mixG_mine.cidx(TILES[ti][0])
                    tis = mixG_mine.tiles_of_chunk(c)
                    P.cc(lambda e, c=c: e.collective_compute("AllGather", ALU.bypass, replica_groups=GROUPS, ins=[mixG_mine.t[c].opt()], outs=[mixG_all.t[c].opt()]),
                         reads=[("mixT", "gdn", t_) for t_ in tis], writes=[("mixG_all", c)])
            phase_gdn(C, H, p["win"], p["gdn_cw"], p["gdn_par"], p["gdn_nw"], cst["gmask"], cst["ident"], mixG_mine, scr, grow0=0, after_tile=_ag)
            P.barrier()
            pb = layb[l]
            C.A.reset(31200)
            gb_ = phase_bf_gen(C, l, mixA_all, mixG_all, xr0 if l == 0 else xo[0], xo[l], xob[l], x_all, p["cv"], pb["wmodgj"], pb["bmodgj"], pb["woutj"])
            if l == 0:
                p1 = lay[1]
                gn_ = phase_h_gen(C, x_all, p1["cv"], p1["wmod"], p1["bmod"], p1["ng"], H, xkey="x_all", xdt=BF16)
            else:
                gn_ = phase_final_gen(C, x_all, xo[1], fgj, xn)
            lagged(gb_, gn_)
        C.A.reset(base_mark)
        P.build()
        C.stats = P.stats
    return nc


def wout_reordered(w_out_l):
    rows = []
    for r in range(4):
        rows += list(range(128 * r, 128 * r + 128)) + list(range(1024 + 128 * r, 1024 + 128 * r + 128))
    for r in range(4):
        rows += list(range(512 + 128 * r, 512 + 128 * r + 128))
    return w_out_l[np.array(rows)]


def prep_fused(inp, b, j, xT_full):
    f = np.float32
    d = {}
    d["xT0"] = xT_full
    d["xr0"] = np.ascontiguousarray(xT_full[256 * j:256 * j + 256])
    d["fgj"] = np.ascontiguousarray(inp["final_g"][256 * j:256 * j + 256].reshape(2, 128).T.astype(f))
    c = att_consts(); c.update(gdn_consts())
    for k in A_CONST:
        d[k] = c[k]
    for l in range(2):
        a = prep_A_all(inp, l, b, j, None)
        for k in A_LAYER:
            d[f"{k}_{l}"] = a[k]
        wo = wout_reordered(inp["w_out"][l])[:, 256 * j:256 * j + 256]
        d[f"woutj_{l}"] = np.ascontiguousarray(wo.reshape(12, 128, 256).transpose(1, 0, 2))
        d[f"wmodgj_{l}"] = pk(inp["w_mod"][l][:, 2048 + 256 * j:2048 + 256 * j + 256])
        d[f"bmodgj_{l}"] = np.ascontiguousarray(inp["b_mod"][l][2048 + 256 * j:2048 + 256 * j + 256].reshape(2, 128).T)
    return d


_NC_CACHE = {}


def run_fused(inp):
    inp = {k: np.asarray(v) for k, v in inp.items()}
    if "nc" not in _NC_CACHE:
        _NC_CACHE["nc"] = build_fused()
    nc = _NC_CACHE["nc"]
    xT = [np.ascontiguousarray(np.concatenate([inp["x"][b], inp["ctx"][b]], axis=0).T.astype(np.float32)) for b in range(2)]
    maps = []
    for core in range(8):
        b, j = core // 4, core % 4
        maps.append(prep_fused(inp, b, j, xT[b]))
    res = run_bass_kernel_spmd(nc, maps, core_ids=list(range(8)))
    out = np.empty((2, T, D), np.float32)
    for core in range(8):
        b, j = core // 4, core % 4
        out[b, :, 256 * j:256 * j + 256] = np.asarray(res.results[core]["xn"]).T
    return out


def kernel(**inputs):
    return run_fused(inputs)
```
